# Optimizing a Trainium2 kernel written in Bass

```python
import math
import jax, jax.numpy as jnp
from jax import lax
import numpy as np

D_MODEL = 2048
BATCH = 4
SEQ = 4096
DEPTH = 4

MEM_LEN = 256
FOX_HEAD_DIM = 128
FOX_HEADS = (3 * D_MODEL // 8) // FOX_HEAD_DIM
FOX_W = FOX_HEADS * FOX_HEAD_DIM
DIFF_QK_DIM = 64
DIFF_V_DIM = 2 * DIFF_QK_DIM
DIFF_HEADS = (3 * D_MODEL // 8) // DIFF_V_DIM
DIFF_QK_W = DIFF_HEADS * 2 * DIFF_QK_DIM
DIFF_V_W = DIFF_HEADS * DIFF_V_DIM
CONV_CH = D_MODEL - FOX_W - DIFF_V_W
CONV_WIDTH = 3
MIX_WIDTH = FOX_W + DIFF_V_W + CONV_CH
IN_SIZES = (FOX_W, FOX_W, FOX_W, FOX_HEADS,
            DIFF_QK_W, DIFF_QK_W, DIFF_V_W,
            CONV_CH, CONV_CH, CONV_CH)
IN_WIDTH = sum(IN_SIZES)
IN_SPLITS = [int(v) for v in np.cumsum(IN_SIZES)[:-1]]
ROPE_THETA = 500000.0
ROT_DIM = DIFF_QK_DIM // 4
CROSS_HEADS = 4
CROSS_HEAD_DIM = 128
CROSS_W = CROSS_HEADS * CROSS_HEAD_DIM
D_FF = 5632
Q_BLOCK = 128
EPS = 1e-6
NEG_INF = -1e30

kernel_name = "hymba_fox_diff_conv_macaron"


def rmsnorm(x, g):
    x32 = x.astype(jnp.float32)
    y = x32 * lax.rsqrt(jnp.mean(x32 * x32, axis=-1, keepdims=True) + EPS)
    return (y * g.astype(jnp.float32)).astype(x.dtype)


def swiglu(x, w_gate, w_up, w_down):
    return (jax.nn.silu(x @ w_gate) * (x @ w_up)) @ w_down


def partial_rope(x, cos, sin):
    half = ROT_DIM // 2
    xr = x[..., :ROT_DIM].astype(jnp.float32)
    x1, x2 = xr[..., :half], xr[..., half:]
    rot = jnp.concatenate([x1 * cos - x2 * sin, x2 * cos + x1 * sin], axis=-1).astype(x.dtype)
    return jnp.concatenate([rot, x[..., ROT_DIM:]], axis=-1)


def fox_attention(q, k, v, log_f):
    B, S, H, Dh = q.shape
    nb = S // Q_BLOCK
    c = jnp.cumsum(log_f, axis=1)
    c_keys = c.transpose(0, 2, 1)
    q_blocks = q.reshape(B, nb, Q_BLOCK, H, Dh).swapaxes(0, 1)
    c_blocks = c.reshape(B, nb, Q_BLOCK, H).swapaxes(0, 1)
    key_pos = jnp.arange(S)
    scale = Dh ** -0.5

    def block(args):
        qi, ci, i = args
        s = jnp.einsum('bqhd,bkhd->bhqk', qi, k).astype(jnp.float32) * scale
        s = s + ci.transpose(0, 2, 1)[..., None] - c_keys[:, :, None, :]
        mask = (i * Q_BLOCK + jnp.arange(Q_BLOCK))[:, None] >= key_pos[None, :]
        p = jax.nn.softmax(jnp.where(mask, s, NEG_INF), axis=-1)
        return jnp.einsum('bhqk,bkhd->bqhd', p.astype(v.dtype), v)

    out = lax.map(block, (q_blocks, c_blocks, jnp.arange(nb)))
    return out.swapaxes(0, 1).reshape(B, S, H, Dh)


def diff_attention(q, k, v, lam):
    B, S, H, _, dk = q.shape
    nb = S // Q_BLOCK
    q_blocks = q.reshape(B, nb, Q_BLOCK, H, 2, dk).swapaxes(0, 1)
    key_pos = jnp.arange(S)
    scale = dk ** -0.5

    def block(args):
        qi, i = args
        s = jnp.einsum('bqhcd,bkhcd->bhcqk', qi, k).astype(jnp.float32) * scale
        mask = (i * Q_BLOCK + jnp.arange(Q_BLOCK))[:, None] >= key_pos[None, :]
        p = jax.nn.softmax(jnp.where(mask, s, NEG_INF), axis=-1)
        a = p[:, :, 0] - lam * p[:, :, 1]
        return jnp.einsum('bhqk,bkhd->bqhd', a.astype(v.dtype), v)

    out = lax.map(block, (q_blocks, jnp.arange(nb)))
    return out.swapaxes(0, 1).reshape(B, S, H, v.shape[-1])


def causal_depthwise_conv(z, w, b):
    S = z.shape[1]
    zp = jnp.pad(z, ((0, 0), (CONV_WIDTH - 1, 0), (0, 0)))
    out = zp[:, 0:S] * w[0]
    for j in range(1, CONV_WIDTH):
        out = out + zp[:, j:j + S] * w[j]
    return out + b


def cross_attention(xn, memn, w_q, w_kv, w_o):
    B, S, _ = xn.shape
    M = memn.shape[1]
    q = (xn @ w_q).reshape(B, S, CROSS_HEADS, CROSS_HEAD_DIM)
    kv = (memn @ w_kv).reshape(B, M, 2, CROSS_HEADS, CROSS_HEAD_DIM)
    k, v = kv[:, :, 0], kv[:, :, 1]
    s = jnp.einsum('bqhd,bmhd->bhqm', q, k).astype(jnp.float32) * (CROSS_HEAD_DIM ** -0.5)
    p = jax.nn.softmax(s, axis=-1)
    o = jnp.einsum('bhqm,bmhd->bqhd', p.astype(v.dtype), v).reshape(B, S, CROSS_W)
    return o @ w_o


def setup_inputs(seed: int = 0) -> dict:
    key = jax.random.key(seed)
    ks = jax.random.split(key, 32)

    def w(k, shape, fan_in):
        return jax.random.normal(k, shape, jnp.float32) * (fan_in ** -0.5)

    def gain(k, shape):
        return 1.0 + 0.02 * jax.random.normal(k, shape, jnp.float32)

    L, D = DEPTH, D_MODEL
    x = jax.random.normal(ks[0], (BATCH, SEQ, D), jnp.float32)
    mem = jax.random.normal(ks[1], (BATCH, MEM_LEN, D), jnp.float32)
    offsets = jax.random.randint(ks[2], (BATCH, 1), 0, 1024, dtype=jnp.int32)
    positions = (offsets + jnp.arange(SEQ, dtype=jnp.int32)[None, :]).astype(jnp.int32)
    return {
        "x": x,
        "mem": mem,
        "positions": positions,
        "ffn1_norm": gain(ks[3], (L, D)),
        "ffn1_w_gate": w(ks[4], (L, D, D_FF), D),
        "ffn1_w_up": w(ks[5], (L, D, D_FF), D),
        "ffn1_w_down": w(ks[6], (L, D_FF, D), D_FF),
        "mix_norm": gain(ks[7], (L, D)),
        "mix_w_in": w(ks[8], (L, D, IN_WIDTH), D),
        "forget_bias": jax.random.uniform(ks[9], (L, FOX_HEADS), jnp.float32, 1.0, 4.0),
        "conv_w": w(ks[10], (L, CONV_WIDTH, CONV_CH), CONV_WIDTH),
        "conv_b": 0.01 * jax.random.normal(ks[11], (L, CONV_CH), jnp.float32),
        "lambda_q1": 0.1 * jax.random.normal(ks[12], (L, DIFF_QK_DIM), jnp.float32),
        "lambda_k1": 0.1 * jax.random.normal(ks[13], (L, DIFF_QK_DIM), jnp.float32),
        "lambda_q2": 0.1 * jax.random.normal(ks[14], (L, DIFF_QK_DIM), jnp.float32),
        "lambda_k2": 0.1 * jax.random.normal(ks[15], (L, DIFF_QK_DIM), jnp.float32),
        "diff_subln": gain(ks[16], (L, DIFF_V_DIM)),
        "mix_w_out": w(ks[17], (L, MIX_WIDTH, D), MIX_WIDTH),
        "cross_norm": gain(ks[18], (L, D)),
        "mem_norm": gain(ks[19], (L, D)),
        "cross_w_q": w(ks[20], (L, D, CROSS_W), D),
        "cross_w_kv": w(ks[21], (L, D, 2 * CROSS_W), D),
        "cross_w_o": w(ks[22], (L, CROSS_W, D), CROSS_W),
        "ffn2_norm": gain(ks[23], (L, D)),
        "ffn2_w_gate": w(ks[24], (L, D, D_FF), D),
        "ffn2_w_up": w(ks[25], (L, D, D_FF), D),
        "ffn2_w_down": w(ks[26], (L, D_FF, D), D_FF),
        "final_norm": gain(ks[27], (D,)),
    }


def reference(x, mem, positions, ffn1_norm, ffn1_w_gate, ffn1_w_up, ffn1_w_down,
              mix_norm, mix_w_in, forget_bias, conv_w, conv_b,
              lambda_q1, lambda_k1, lambda_q2, lambda_k2, diff_subln, mix_w_out,
              cross_norm, mem_norm, cross_w_q, cross_w_kv, cross_w_o,
              ffn2_norm, ffn2_w_gate, ffn2_w_up, ffn2_w_down, final_norm):
    B, S, _ = x.shape
    inv_freq = ROPE_THETA ** (-jnp.arange(0, ROT_DIM, 2, dtype=jnp.float32) / ROT_DIM)
    ang = positions.astype(jnp.float32)[..., None] * inv_freq
    cos = jnp.cos(ang)[:, :, None, None, :]
    sin = jnp.sin(ang)[:, :, None, None, :]

    h = x
    for l in range(DEPTH):
        h = h + 0.5 * swiglu(rmsnorm(h, ffn1_norm[l]), ffn1_w_gate[l], ffn1_w_up[l], ffn1_w_down[l])

        n = rmsnorm(h, mix_norm[l])
        proj = n @ mix_w_in[l]
        fq, fk, fv, ff, dq, dk, dv, gb, gc, hc = jnp.split(proj, IN_SPLITS, axis=-1)

        log_f = jax.nn.log_sigmoid(ff.astype(jnp.float32) + forget_bias[l].astype(jnp.float32))
        fox = fox_attention(fq.reshape(B, S, FOX_HEADS, FOX_HEAD_DIM),
                            fk.reshape(B, S, FOX_HEADS, FOX_HEAD_DIM),
                            fv.reshape(B, S, FOX_HEADS, FOX_HEAD_DIM), log_f)
        fox = fox.reshape(B, S, FOX_W)

        lam_init = 0.8 - 0.6 * math.exp(-0.3 * l)
        lam = (jnp.exp(jnp.sum(lambda_q1[l].astype(jnp.float32) * lambda_k1[l].astype(jnp.float32)))
               - jnp.exp(jnp.sum(lambda_q2[l].astype(jnp.float32) * lambda_k2[l].astype(jnp.float32)))
               + lam_init)
        dq = partial_rope(dq.reshape(B, S, DIFF_HEADS, 2, DIFF_QK_DIM), cos, sin)
        dk = partial_rope(dk.reshape(B, S, DIFF_HEADS, 2, DIFF_QK_DIM), cos, sin)
        diff = diff_attention(dq, dk, dv.reshape(B, S, DIFF_HEADS, DIFF_V_DIM), lam)
        diff = (rmsnorm(diff, diff_subln[l]) * (1.0 - lam_init)).reshape(B, S, DIFF_V_W)

        conv = gb * causal_depthwise_conv(gc * hc, conv_w[l], conv_b[l])

        mixed = jnp.concatenate([fox, diff, conv], axis=-1) @ mix_w_out[l]
        h = h + mixed

        h = h + cross_attention(rmsnorm(h, cross_norm[l]), rmsnorm(mem, mem_norm[l]),
                                cross_w_q[l], cross_w_kv[l], cross_w_o[l])

        h = h + 0.5 * swiglu(rmsnorm(h, ffn2_norm[l]), ffn2_w_gate[l], ffn2_w_up[l], ffn2_w_down[l])

    return rmsnorm(h, final_norm)
```

```python
import numpy as np
import ml_dtypes
from contextlib import ExitStack

import concourse.bass as bass
import concourse.mybir as mybir
from concourse.bass_utils import run_bass_kernel_spmd

F32 = mybir.dt.float32
BF16 = mybir.dt.bfloat16
I32 = mybir.dt.int32
AF = mybir.ActivationFunctionType
ALU = mybir.AluOpType

D = 2048
KC = 16
DFF = 5632
FC = 44
S_FULL = 4096
SBT = 1024
NBLK = SBT // 512
NT = SBT // 128
MEM = 256
NFH = 6
NDH = 6
IN_W = 6150
OFF_FQ, OFF_FK, OFF_FV, OFF_FF = 0, 768, 1536, 2304
OFF_DQ, OFF_DK, OFF_DV = 2310, 3078, 3846
OFF_GB, OFF_GC, OFF_HC = 4614, 5126, 5638
EPS = 1e-6
ROPE_THETA = 500000.0
N_CORES = 4


class Ev:
    __slots__ = ("eng", "sem", "val", "idx")

    def __init__(self, eng, sem=None, val=0, idx=0):
        self.eng, self.sem, self.val, self.idx = eng, sem, val, idx


class Buf:
    __slots__ = ("name", "w", "r", "const")

    def __init__(self, name):
        self.name, self.w, self.r, self.const = name, None, {}, False


class DmaSem:
    __slots__ = ("h", "count", "key")

    def __init__(self, h, key):
        self.h, self.count, self.key = h, 0, key


class Sched:
    ENGS = ("pe", "act", "dve", "pool", "sp")

    def __init__(self, esems):
        self.esem = esems
        self.q = {e: [] for e in self.ENGS}
        self.cnt = {e: 0 for e in self.ENGS}
        self.seen = {e: {} for e in self.ENGS}
        self.pe_idx = 0
        self.pe_sig = []
        self.ndma = 0
        self.dma_active = {}

    def _resolve(self, ev):
        if ev.eng == "pe" and ev.sem is None:
            lo, hi = 0, len(self.pe_sig)
            while lo < hi:
                mid = (lo + hi) // 2
                if self.pe_sig[mid][0] >= ev.idx:
                    hi = mid
                else:
                    lo = mid + 1
            assert lo < len(self.pe_sig), "PE event not yet signalled"
            return "pe", self.esem["pe"], self.pe_sig[lo][1]
        return ev.sem[0], ev.sem[1], ev.val

    def _wait(self, eng, ev):
        key, sem, val = self._resolve(ev)
        if self.seen[eng].get(key, 0) >= val:
            return
        self.seen[eng][key] = val
        self.q[eng].append(("w", sem, val))

    def op(self, eng, fn, reads=(), writes=(), signal=True, dma=None):
        deps = []
        for b in reads:
            if b.w is not None:
                deps.append(b.w)
        for b in writes:
            if b.w is not None:
                deps.append(b.w)
            deps.extend(b.r.values())
        is_dma = dma is not None
        for ev in deps:
            if (not is_dma) and eng == "pe" and ev.eng == "pe":
                continue
            self._wait(eng, ev)
        if is_dma:
            dma.count += 16
            ev = Ev("dma", (dma.key, dma.h), dma.count)
            self.q[eng].append(("i", fn, dma.h, 16))
            self.ndma += 1
            self.dma_active[dma.key] = ev
            rkey = ("dma", dma.key)
        elif eng == "pe":
            self.pe_idx += 1
            if signal:
                self.cnt["pe"] += 1
                self.pe_sig.append((self.pe_idx, self.cnt["pe"]))
                self.q[eng].append(("i", fn, self.esem["pe"], 1))
            else:
                self.q[eng].append(("i", fn, None, 0))
            ev = Ev("pe", None, 0, self.pe_idx)
            rkey = "pe"
        else:
            self.cnt[eng] += 1
            ev = Ev(eng, (eng, self.esem[eng]), self.cnt[eng])
            self.q[eng].append(("i", fn, self.esem[eng], 1))
            rkey = eng
        for b in reads:
            if not b.const:
                b.r[rkey] = ev
        for b in writes:
            b.w = ev
            b.r = {}
        return ev

    def fence(self, engs=("pe", "act", "dve", "sp")):
        last = {}
        for e in engs:
            if e == "pe":
                if self.pe_sig:
                    last[e] = Ev("pe", ("pe", self.esem["pe"]), self.pe_sig[-1][1])
            elif self.cnt[e] > 0:
                last[e] = Ev(e, (e, self.esem[e]), self.cnt[e])
        for e in engs:
            for o, ev in last.items():
                if o != e:
                    self._wait(e, ev)
            for ev in self.dma_active.values():
                self._wait(e, ev)
        self.dma_active = {}


def build_program(depth, nsb, s_len):
    nc = bass.Bass("TRN2", target_bir_lowering=False)
    L = depth

    def din(name, shape, dt=F32):
        return nc.dram_tensor(name, list(shape), dt, kind="ExternalInput").ap()

    def dscr(name, shape, dt):
        return nc.dram_tensor(name, list(shape), dt, kind="Internal").ap()

    x_d = din("x", [s_len, D])
    mem_d = din("mem", [MEM, D])
    pos_d = din("positions", [s_len], I32)
    w_g1 = din("ffn1_w_gate", [L, D, DFF]); w_u1 = din("ffn1_w_up", [L, D, DFF]); w_d1 = din("ffn1_w_down", [L, DFF, D])
    w_g2 = din("ffn2_w_gate", [L, D, DFF]); w_u2 = din("ffn2_w_up", [L, D, DFF]); w_d2 = din("ffn2_w_down", [L, DFF, D])
    w_in = din("mix_w_in", [L, D, IN_W]); w_out = din("mix_w_out", [L, D, D])
    w_cq = din("cross_w_q", [L, D, 512]); w_ckv = din("cross_w_kv", [L, D, 1024]); w_co = din("cross_w_o", [L, 512, D])
    gains_d = din("gains", [128, 4 * L + 1, KC])
    memg_d = din("mem_gain_b", [128, L, D])
    fbias_d = din("forget_bias_t", [NFH, L])
    convw_d = din("conv_wb", [128, L, 4, 4])
    lam_d = din("lam_b", [128, L, 4, 64])
    subln_d = din("subln_b", [128, L, 128])
    cf32_d = din("const_f32", [128, 3 * 128 + 1 + 2])
    cbf_d = din("const_bf16", [128, 128], BF16)
    sel_d = din("const_sel", [NFH, NFH * 128])
    mask_d = din("const_mask", [128, 4, 512])
    laminit_d = din("const_laminit", [128, L])
    out_d = nc.dram_tensor("out", [s_len, D], F32, kind="ExternalOutput").ap()

    hT_d = dscr("hT", [KC, 128, s_len], F32)
    fk_d = dscr("fk_cache", [L, NFH, 128, s_len], BF16)
    fv_d = dscr("fv_cache", [L, NFH, 128, s_len // 128, 129], BF16)
    fc_d = dscr("fc_cache", [L, 128, s_len // 128, NFH], F32)
    dk_d = dscr("dk_cache", [L, NDH, 128, s_len], BF16)
    dv_d = dscr("dv_cache", [L, NDH, 128, s_len // 128, 129], BF16)
    mk_d = dscr("memk", [L, 128, 4, MEM], BF16)
    mv_d = dscr("memv", [L, 128, 2, 4, 129], BF16)

    es = ExitStack()
    with es:
        def sb_t(name, shape, dt):
            return es.enter_context(nc.sbuf_tensor("s_" + name, list(shape), dt))

        esems = {e: es.enter_context(nc.semaphore("sem_" + e)) for e in Sched.ENGS}
        sc = Sched(esems)
        dma_sem_n = [0]

        def new_dsem():
            dma_sem_n[0] += 1
            h = es.enter_context(nc.semaphore("dsem%d" % dma_sem_n[0]))
            return DmaSem(h, "d%d" % dma_sem_n[0])

        xnT = sb_t("xnT", [128, KC, SBT], BF16)
        R = sb_t("R", [128, 45056], BF16)
        wsl = [sb_t("wsl%d" % i, [128, 11264], BF16) for i in range(2)]
        NWT = 8
        wts = [sb_t("wt%d" % i, [128, 512], F32) for i in range(NWT)]
        gains = sb_t("gains", [128, 4 * L + 1, KC], F32)
        cf32 = sb_t("cf32", [128, 3 * 128 + 3], F32)
        cbf = sb_t("cbf", [128, 128], BF16)
        selt = sb_t("selt", [NFH, NFH * 128], F32)
        maskt = sb_t("maskt", [128, 4, 512], F32)
        fbias = sb_t("fbias", [NFH, L], F32)
        nfbias = sb_t("nfbias", [NFH, L], F32)
        convw = sb_t("convw", [128, L, 4, 4], F32)
        subln = sb_t("subln", [128, L, 128], F32)
        lamt = sb_t("lamt", [128, L], F32)
        nlamt = sb_t("nlamt", [128, L], F32)
        laminit = sb_t("laminit", [128, L], F32)
        ccarry = sb_t("ccarry", [NFH, L], F32)
        halo = sb_t("halo", [128, L, 4, 2], F32)
        small = sb_t("small", [128, 64], F32)

        ident_f = cf32[:, 0:128]
        ones_f = cf32[:, 128:256]
        perm_f = cf32[:, 256:384]
        invf = cf32[:, 384:385]
        negpi = cf32[:, 385:386]
        ident_b = cbf[:, :]

        banks = [es.enter_context(nc.psum_tensor("bank%d" % i, [128, 512], F32)) for i in range(8)]
        bank_buf = [Buf("bank%d" % i) for i in range(8)]

        class Ring:
            def __init__(self, items):
                self.items, self.i = items, 0

            def next(self):
                it = self.items[self.i % len(self.items)]
                self.i += 1
                return it

        mm_ring = Ring([0, 1, 2, 3])
        s_ring = Ring([4, 5])
        rstd_t = [sb_t("rstd%d" % i, [128, 512], F32) for i in range(2)]
        rstd_b = [Buf("rstd%d" % i) for i in range(2)]
        rstd_i = [0]
        x_dsem = [new_dsem() for _ in range(NT)]
        wt_buf = [Buf("wt%d" % i) for i in range(NWT)]
        wt_ring = Ring(list(range(NWT)))
        wt_dsem = [new_dsem() for _ in range(NWT)]
        ws_buf = [[Buf("ws%d_%d" % (i, p)) for p in range(3)] for i in range(2)]
        ws_dsem = [[new_dsem() for p in range(3)] for i in range(2)]
        ws_ring = Ring([0, 1])

        B_xnT = [Buf("xnT%d" % b) for b in range(NBLK)]
        B_const = Buf("const"); B_const.const = True
        B_R = Buf("Rmisc")
        misc_dsem = new_dsem()
        hT_buf = [[Buf("hT%d_%d" % (c, b)) for b in range(s_len // 512)] for c in range(KC)]
        st_dsem = [new_dsem() for _ in range(4)]
        st_ring = Ring([0, 1, 2, 3])

        def PE(fn, reads, writes, signal):
            return sc.op("pe", fn, reads, writes, signal=signal)

        def ACT(fn, reads, writes):
            return sc.op("act", fn, reads, writes)

        def DVE(fn, reads, writes):
            return sc.op("dve", fn, reads, writes)

        def POOLC(fn, reads, writes):
            return sc.op("pool", fn, reads, writes)

        def DMA_SP(out, in_, reads, writes, dsem):
            return sc.op("sp", lambda e, o=out, i=in_: e.dma_start(out=o, in_=i), reads, writes, dma=dsem)

        def DMA_POOL(out, in_, reads, writes, dsem):
            return sc.op("pool", lambda e, o=out, i=in_: e.dma_start(out=o, in_=i), reads, writes, dma=dsem)

        def mm_group(bank_i, out_ap, pairs, reads, sig_all=False):
            n = len(pairs)
            for k, pr in enumerate(pairs):
                lt, rh = pr[0], pr[1]
                rds = list(reads) + (list(pr[2]) if len(pr) > 2 else [])
                PE(lambda e, o=out_ap, a=lt, b=rh, s=(k == 0), t=(k == n - 1): e.matmul(o, a, b, start=s, stop=t),
                   rds, [bank_buf[bank_i]], signal=(sig_all or k == n - 1))

        def load_w(parts):
            si = ws_ring.next()
            views = []
            for p, (src, shape) in enumerate(parts):
                n = 1
                for d_ in shape[1:]:
                    n *= d_
                v = wsl[si][:, p * 4096:p * 4096 + n] if len(parts) > 1 else wsl[si][:, 0:n]
                if len(shape) == 3:
                    v = v.rearrange("p (a b) -> p a b", a=shape[1])
                wr = [ws_buf[si][p]] if len(parts) > 1 else ws_buf[si]
                DMA_POOL(v, src, [], wr, ws_dsem[si][p])
                views.append(v)
            rd = ws_buf[si][:len(parts)] if len(parts) > 1 else ws_buf[si]
            return rd, views

        def wview(w_ap, l, c0, ncols):
            return w_ap[l].rearrange("(kc p) n -> p kc n", p=128)[:, :, c0:c0 + ncols]

        def get_wt():
            i = wt_ring.next()
            return i, wts[i], wt_buf[i]

        def load_const(dst, src):
            DMA_SP(dst, src, [], [B_const], misc_dsem)

        B_const.const = False
        load_const(gains[:], gains_d)
        load_const(cf32[:], cf32_d)
        load_const(cbf[:], cbf_d)
        load_const(selt[:], sel_d)
        load_const(maskt[:], mask_d)
        load_const(fbias[:], fbias_d)
        load_const(convw[:], convw_d)
        load_const(subln[:], subln_d)
        load_const(laminit[:], laminit_d)
        lamv = R[:, 0:L * 4 * 64 * 2].bitcast(F32).rearrange("p (l f d) -> p l f d", l=L, f=4)
        DMA_SP(lamv, lam_d, [], [B_R], misc_dsem)
        DVE(lambda e: e.memset(ccarry[:], 0.0), [], [B_const])
        DVE(lambda e: e.memset(halo[:], 0.0), [], [B_const])
        ACT(lambda e: e.mul(nfbias[:], fbias[:], -1.0), [B_const], [B_const])
        for l in range(L):
            s1 = small[:, 0:1]; s2 = small[:, 1:2]; junk = small[:, 8:8 + 0]
            pr = wts[0][:, 0:64]
            DVE(lambda e, l=l: e.tensor_tensor(out=wts[0][:, 0:64], in0=lamv[:, l, 0, :], in1=lamv[:, l, 1, :], op=ALU.mult), [B_R], [wt_buf[0]])
            DVE(lambda e: e.reduce_sum(out=small[:, 0:1], in_=wts[0][:, 0:64], axis=mybir.AxisListType.X), [wt_buf[0]], [B_const])
            DVE(lambda e, l=l: e.tensor_tensor(out=wts[0][:, 64:128], in0=lamv[:, l, 2, :], in1=lamv[:, l, 3, :], op=ALU.mult), [B_R], [wt_buf[0]])
            DVE(lambda e: e.reduce_sum(out=small[:, 1:2], in_=wts[0][:, 64:128], axis=mybir.AxisListType.X), [wt_buf[0]], [B_const])
            ACT(lambda e: e.activation(out=small[:, 2:4], in_=small[:, 0:2], func=AF.Exp), [B_const], [B_const])
            DVE(lambda e: e.tensor_tensor(out=small[:, 4:5], in0=small[:, 2:3], in1=small[:, 3:4], op=ALU.subtract), [B_const], [B_const])
            DVE(lambda e, l=l: e.tensor_tensor(out=lamt[:, l:l + 1], in0=small[:, 4:5], in1=laminit[:, l:l + 1], op=ALU.add), [B_const], [B_const])
            DVE(lambda e, l=l: e.tensor_scalar(out=nlamt[:, l:l + 1], in0=lamt[:, l:l + 1], scalar1=-1.0, scalar2=None, op0=ALU.mult), [B_const], [B_const])
        DVE(lambda e: e.tensor_scalar(out=laminit[:], in0=laminit[:], scalar1=-1.0, scalar2=1.0, op0=ALU.mult, op1=ALU.add), [B_const], [B_const])
        sc.fence(("pe", "act", "dve", "sp", "pool"))
        B_const.const = True

        def rmw_h(bank_i, ps_ap, chunk, gblk, scale):
            i, wt, wb = get_wt()
            hb = hT_buf[chunk][gblk]
            DMA_SP(wt[:], hT_d[chunk, :, gblk * 512:(gblk + 1) * 512], [hb], [wb], wt_dsem[i])
            DVE(lambda e, o=wt, p=ps_ap, s=scale: e.scalar_tensor_tensor(out=o[:], in0=p, scalar=s, in1=o[:], op0=ALU.mult, op1=ALU.add),
                [bank_buf[bank_i], wb], [wb])
            DMA_SP(hT_d[chunk, :, gblk * 512:(gblk + 1) * 512], wt[:], [wb], [hb], wt_dsem[i])

        def stage_init(j):
            sc.fence()
            xt = R[:, 0:NT * D * 2].bitcast(F32).rearrange("p (t d) -> p t d", t=NT)
            xb = [Buf("xin%d" % t) for t in range(NT)]
            for t in range(NT):
                tok0 = j * SBT + t * 128
                DMA_SP(xt[:, t, :], x_d[tok0:tok0 + 128, :], [], [xb[t]], x_dsem[t])
            for c in range(KC):
                for b in range(NBLK):
                    bi = mm_ring.next()
                    for q in range(4):
                        t = b * 4 + q
                        PE(lambda e, o=banks[bi][:, q * 128:(q + 1) * 128], a=xt[:, t, c * 128:(c + 1) * 128]: e.transpose(o, a, ident_f),
                           [xb[t], B_const], [bank_buf[bi]], signal=(q == 3))
                    i, wt, wb = get_wt()
                    ACT(lambda e, o=wt, p=banks[bi]: e.copy(o[:], p[:]), [bank_buf[bi]], [wb])
                    gb = j * NBLK + b
                    DMA_SP(hT_d[c, :, gb * 512:(gb + 1) * 512], wt[:], [wb], [hT_buf[c][gb]], wt_dsem[i])
            sc.fence()

        def stage_norm(j, gidx, out_bf=True, fin=None):
            for b in range(NBLK):
                gb = j * NBLK + b
                bi = mm_ring.next()
                for c in range(KC):
                    i, wt, wb = get_wt()
                    DMA_SP(wt[:], hT_d[c, :, gb * 512:(gb + 1) * 512], [hT_buf[c][gb]], [wb], wt_dsem[i])
                    ACT(lambda e, o=wt: e.activation(out=o[:], in_=o[:], func=AF.Square), [wb], [wb])
                    PE(lambda e, o=banks[bi], r=wt, s=(c == 0), t=(c == KC - 1): e.matmul(o[:], ones_f, r[:], start=s, stop=t),
                       [wb, B_const], [bank_buf[bi]], signal=True)
                rt, rb = rstd_t[rstd_i[0] % 2], rstd_b[rstd_i[0] % 2]
                rstd_i[0] += 1
                ACT(lambda e, o=rt, p=banks[bi]: e.activation(out=o[:], in_=p[:], func=AF.Sqrt, bias=eps_t[:, 0:1], scale=1.0 / D),
                    [bank_buf[bi], B_const], [rb])
                DVE(lambda e, o=rt: e.reciprocal(out=o[:], in_=o[:]), [rb], [rb])
                for c in range(KC):
                    i, wt, wb = get_wt()
                    DMA_SP(wt[:], hT_d[c, :, gb * 512:(gb + 1) * 512], [hT_buf[c][gb]], [wb], wt_dsem[i])
                    if fin is None:
                        DVE(lambda e, o=xnT[:, c, b * 512:(b + 1) * 512], w=wt, r=rt, g=gains[:, gidx, c:c + 1]:
                            e.scalar_tensor_tensor(out=o, in0=w[:], scalar=g, in1=r[:], op0=ALU.mult, op1=ALU.mult),
                            [wb, rb, B_const], [B_xnT[b]])
                    else:
                        DVE(lambda e, w=wt, r=rt, g=gains[:, gidx, c:c + 1]:
                            e.scalar_tensor_tensor(out=w[:], in0=w[:], scalar=g, in1=r[:], op0=ALU.mult, op1=ALU.mult),
                            [wb, rb, B_const], [wb])
                        fin(c, b, wt, wb)

        def stage_ffn(j, l, wg, wu, wd, gidx):
            sc.fence()
            stage_norm(j, gidx)
            gT = R[:, 0:FC * SBT].rearrange("p (c t) -> p c t", c=FC)
            B_gT = [[Buf("gT%d_%d" % (c, b)) for b in range(NBLK)] for c in range(FC)]
            for cp in range(FC // 2):
                rd, (vg, vu) = load_w([(wview(wg, l, cp * 256, 256), [128, KC, 256]),
                                       (wview(wu, l, cp * 256, 256), [128, KC, 256])])
                for b in range(NBLK):
                    for cc in range(2):
                        c = cp * 2 + cc
                        bg = mm_ring.next()
                        mm_group(bg, banks[bg][:], [(vg[:, k, cc * 128:(cc + 1) * 128], xnT[:, k, b * 512:(b + 1) * 512]) for k in range(KC)],
                                 [rd[0], B_xnT[b]])
                        bu = mm_ring.next()
                        mm_group(bu, banks[bu][:], [(vu[:, k, cc * 128:(cc + 1) * 128], xnT[:, k, b * 512:(b + 1) * 512]) for k in range(KC)],
                                 [rd[1], B_xnT[b]])
                        i, wt, wb = get_wt()
                        ACT(lambda e, o=wt, p=banks[bg]: e.activation(out=o[:], in_=p[:], func=AF.Silu), [bank_buf[bg]], [wb])
                        DVE(lambda e, o=gT[:, c, b * 512:(b + 1) * 512], w=wt, p=banks[bu]: e.tensor_tensor(out=o, in0=w[:], in1=p[:], op=ALU.mult),
                            [wb, bank_buf[bu]], [B_gT[c][b]])
            for mg in range(D // 256):
                rd, (vd,) = load_w([(wd[l].rearrange("(c p) n -> p c n", p=128)[:, :, mg * 256:(mg + 1) * 256], [128, FC, 256])])
                for mm in range(2):
                    m = mg * 2 + mm
                    for b in range(NBLK):
                        bi = mm_ring.next()
                        mm_group(bi, banks[bi][:], [(vd[:, c, mm * 128:(mm + 1) * 128], gT[:, c, b * 512:(b + 1) * 512], [B_gT[c][b]]) for c in range(FC)],
                                 list(rd))
                        rmw_h(bi, banks[bi][:], m, j * NBLK + b, 0.5)

        def stage_final(j):
            sc.fence()
            ot = R[:, 0:NT * D * 2].bitcast(F32).rearrange("p (t d) -> p t d", t=NT)
            ob = [Buf("oout%d" % t) for t in range(NT)]

            def fin(c, b, wt, wb):
                bi = mm_ring.next()
                for q in range(4):
                    PE(lambda e, o=banks[bi][:, q * 128:(q + 1) * 128], a=wt[:, q * 128:(q + 1) * 128]: e.transpose(o, a, ident_f),
                       [wb, B_const], [bank_buf[bi]], signal=(q == 3))
                for q in range(4):
                    t = b * 4 + q
                    ACT(lambda e, o=ot[:, t, c * 128:(c + 1) * 128], p=banks[bi][:, q * 128:(q + 1) * 128]: e.copy(o, p),
                        [bank_buf[bi]], [ob[t]])

            stage_norm(j, 4 * L, fin=fin)
            for t in range(NT):
                tok0 = j * SBT + t * 128
                ev = DMA_SP(out_d[tok0:tok0 + 128, :], ot[:, t, :], [ob[t]], [Buf("outd")], x_dsem[t])
                final_evs.append(ev)
            sc.fence()


        mma_ring = Ring([0, 1])
        acc_sets = Ring([(6, 7), (2, 3)])
        smallcols = Ring(list(range(40, 64)))
        B_small = Buf("smallcols")

        def Rf32(off, n):
            return R[:, off:off + 2 * n].bitcast(F32)

        def attn(j, QT, KT, vtile, nkt_list, scale, Pt, B_P, reads_q, reads_k, reads_v, out_fn,
                 Ctb=None, B_Ctb=None, negc=None, B_negc=None, causal=True):
            for qb in range(NBLK):
                q0 = j * SBT + qb * 512
                kts = nkt_list(qb)
                accb = acc_sets.next()
                for ki, kt in enumerate(kts):
                    k0 = kt * 128
                    diag = causal and (k0 >= q0)
                    bS = s_ring.next()
                    PE(lambda e, o=banks[bS], a=KT[:, k0:k0 + 128], b=QT[:, qb * 512:(qb + 1) * 512]: e.matmul(o[:], a, b, start=True, stop=True),
                       list(reads_q) + list(reads_k(kt)), [bank_buf[bS]], signal=True)
                    pi = Pt.next()
                    Pv, Pb = pi
                    if Ctb is not None or diag:
                        i, wt, wb = get_wt()
                        if Ctb is not None:
                            DVE(lambda e, o=wt, p=banks[bS], c=Ctb[:, qb * 512:(qb + 1) * 512], s=scale:
                                e.scalar_tensor_tensor(out=o[:], in0=p[:], scalar=s, in1=c, op0=ALU.mult, op1=ALU.add),
                                [bank_buf[bS], B_Ctb], [wb])
                            if diag:
                                v = (k0 - q0) // 128
                                DVE(lambda e, o=wt, m=maskt[:, v, :]: e.tensor_tensor(out=o[:], in0=o[:], in1=m, op=ALU.add), [wb, B_const], [wb])
                            esc = 1.0
                        else:
                            v = (k0 - q0) // 128
                            DVE(lambda e, o=wt, p=banks[bS], m=maskt[:, v, :], s=scale:
                                e.scalar_tensor_tensor(out=o[:], in0=p[:], scalar=s, in1=m, op0=ALU.mult, op1=ALU.add),
                                [bank_buf[bS], B_const], [wb])
                            esc = 1.0
                        if negc is not None:
                            ACT(lambda e, o=Pv, w=wt, bcol=negc(kt): e.activation(out=o, in_=w[:], func=AF.Exp, bias=bcol, scale=1.0),
                                [wb, B_negc(kt)], [Pb])
                        else:
                            ACT(lambda e, o=Pv, w=wt: e.activation(out=o, in_=w[:], func=AF.Exp), [wb], [Pb])
                    else:
                        ACT(lambda e, o=Pv, p=banks[bS], s=scale: e.activation(out=o, in_=p[:], func=AF.Exp, scale=s), [bank_buf[bS]], [Pb])
                    for qs in range(4):
                        bk = accb[qs // 2]
                        off = (qs % 2) * 129
                        first = (ki == 0)
                        PE(lambda e, o=banks[bk][:, off:off + 129], a=Pv[:, qs * 128:(qs + 1) * 128], b=vtile(kt), s=(first and qs % 2 == 0), t=(ki == len(kts) - 1):
                           e.matmul(o, a, b, start=s, stop=t, skip_group_check=True),
                           [Pb] + list(reads_v(kt)), [bank_buf[bk]], signal=(ki == len(kts) - 1))
                for qs in range(4):
                    bk = accb[qs // 2]
                    off = (qs % 2) * 129
                    col = smallcols.next()
                    rec = small[:, col:col + 1]
                    DVE(lambda e, o=rec, p=banks[bk][:, off + 128:off + 129]: e.reciprocal(out=o, in_=p), [bank_buf[bk]], [B_small])
                    out_fn(qb, qs, banks[bk][:, off:off + 128], rec, bank_buf[bk])

        def transpose_to(On, B_On, dst_fn, B_dst, n=4):
            bi = mma_ring.next()
            pb = banks[bi].bitcast(BF16)
            for q in range(n):
                PE(lambda e, o=pb[:, q * 128:(q + 1) * 128], a=On[:, q, :]: e.transpose(o, a, ident_b), [B_On, B_const], [bank_buf[bi]], signal=(q == n - 1))
            ACT(lambda e, o=dst_fn(), p=pb[:, 0:n * 128]: e.copy(o, p), [bank_buf[bi]], [B_dst])

        def proj_fm(vw, rd, b, kcn=KC, rhs=None, B_rhs=None, ncol=512, pslice=None):
            bi = mma_ring.next()
            rh = rhs if rhs is not None else (lambda k: xnT[:, k, b * 512:(b + 1) * 512])
            Br = B_rhs if B_rhs is not None else B_xnT[b]
            out = banks[bi][:, 0:ncol] if pslice is None else banks[bi][pslice[0]:pslice[1], 0:ncol]
            mm_group(bi, out, [(vw(k), rh(k)) for k in range(kcn)], list(rd) + [Br])
            return bi

        def stage_mem(l):
            sc.fence()
            mt = Rf32(0, 2 * D).rearrange("p (t d) -> p t d", t=2)
            memn = R[:, 8192:12288].rearrange("p (t d) -> p t d", t=2)
            memnT = R[:, 12288:16384].rearrange("p (c t) -> p c t", c=KC)
            memg = Rf32(16384, D)
            mk_s = R[:, 20480:21504].rearrange("p (h t) -> p h t", h=4)
            mv_s = R[:, 21504:22536].rearrange("p (t h d) -> p t h d", t=2, h=4)
            junk = Rf32(22544, D)
            B_mt = [Buf("mt0"), Buf("mt1")]; B_memn = [Buf("memn0"), Buf("memn1")]; B_memnT = Buf("memnT"); B_memg = Buf("memg")
            B_mk = Buf("mk_s"); B_mv = Buf("mv_s"); B_junk = Buf("junk")
            DMA_SP(memg, memg_d[:, l, :], [], [B_memg], misc_dsem)
            DVE(lambda e: e.memset(mv_s[:, :, :, 128:129], 1.0), [], [B_mv])
            for t in range(2):
                DMA_SP(mt[:, t, :], mem_d[t * 128:(t + 1) * 128, :], [], [B_mt[t]], x_dsem[t])
                ACT(lambda e, t=t: e.activation(out=junk, in_=mt[:, t, :], func=AF.Square, accum_out=small[:, 34 + t:35 + t]), [B_mt[t]], [B_junk, B_small])
                ACT(lambda e, t=t: e.activation(out=small[:, 36 + t:37 + t], in_=small[:, 34 + t:35 + t], func=AF.Sqrt, bias=eps_t[:, 0:1], scale=1.0 / D), [B_small, B_const], [B_small])
                DVE(lambda e, t=t: e.reciprocal(out=small[:, 36 + t:37 + t], in_=small[:, 36 + t:37 + t]), [B_small], [B_small])
                DVE(lambda e, t=t: e.scalar_tensor_tensor(out=memn[:, t, :], in0=mt[:, t, :], scalar=small[:, 36 + t:37 + t], in1=memg, op0=ALU.mult, op1=ALU.mult),
                    [B_mt[t], B_small, B_memg], [B_memn[t]])
            for c in range(KC):
                bi = mma_ring.next()
                pb = banks[bi].bitcast(BF16)
                for t in range(2):
                    PE(lambda e, o=pb[:, t * 128:(t + 1) * 128], a=memn[:, t, c * 128:(c + 1) * 128]: e.transpose(o, a, ident_b), [B_memn[t], B_const], [bank_buf[bi]], signal=(t == 1))
                ACT(lambda e, o=memnT[:, c, :], p=pb[:, 0:256]: e.copy(o, p), [bank_buf[bi]], [B_memnT])
            for hd in range(4):
                rd, (vk,) = load_w([(wview(w_ckv, l, hd * 128, 128), [128, KC, 128])])
                bi = mma_ring.next()
                mm_group(bi, banks[bi][:, 0:MEM], [(vk[:, k, :], memnT[:, k, :]) for k in range(KC)], list(rd) + [B_memnT])
                ACT(lambda e, o=mk_s[:, hd, :], p=banks[bi][:, 0:MEM]: e.copy(o, p), [bank_buf[bi]], [B_mk])
            rd, (vv,) = load_w([(wview(w_ckv, l, 512, 512), [128, KC, 512])])
            for t in range(2):
                bi = mma_ring.next()
                mm_group(bi, banks[bi][:], [(memnT[:, k, t * 128:(t + 1) * 128], vv[:, k, :]) for k in range(KC)], list(rd) + [B_memnT])
                ACT(lambda e, o=mv_s[:, t, :, 0:128], p=banks[bi][:].rearrange("p (h d) -> p h d", h=4): e.copy(o, p), [bank_buf[bi]], [B_mv])
            DMA_SP(mk_d[l], mk_s, [B_mk], [B_mkd[l]], misc_dsem)
            DMA_SP(mv_d[l], mv_s, [B_mv], [B_mvd[l]], misc_dsem)
            sc.fence()

        B_mkd = [Buf("mkd%d" % l) for l in range(L)]
        B_mvd = [Buf("mvd%d" % l) for l in range(L)]

        def stage_cross(j, l):
            sc.fence()
            stage_norm(j, 4 * l + 2)
            crossT = R[:, 0:4096].rearrange("p (c t) -> p c t", c=4)
            memK = R[:, 4096:5120].rearrange("p (h t) -> p h t", h=4)
            memV = R[:, 5120:6152].rearrange("p (t h d) -> p t h d", t=2, h=4)
            QT = R[:, 6160:7184]
            Pt = Ring([(R[:, 7184 + i * 512:7184 + (i + 1) * 512], Buf("P%d" % i)) for i in range(3)])
            On = R[:, 8720:9232].rearrange("p (q d) -> p q d", q=4)
            B_cT = Buf("crossT"); B_mK = Buf("memK"); B_mV = Buf("memV"); B_QT = Buf("QT"); B_On = Buf("On")
            DMA_SP(memK, mk_d[l], [B_mkd[l]], [B_mK], misc_dsem)
            DMA_SP(memV, mv_d[l], [B_mvd[l]], [B_mV], misc_dsem)
            for hd in range(4):
                rd, (vq,) = load_w([(wview(w_cq, l, hd * 128, 128), [128, KC, 128])])
                for b in range(NBLK):
                    bi = proj_fm(lambda k: vq[:, k, :], rd, b)
                    ACT(lambda e, o=QT[:, b * 512:(b + 1) * 512], p=banks[bi]: e.copy(o, p[:]), [bank_buf[bi]], [B_QT])

                def out_fn(qb, qs, acc, rec, bb, hd=hd):
                    DVE(lambda e, o=On[:, qs, :], a=acc, r=rec: e.tensor_scalar(out=o, in0=a, scalar1=r, scalar2=None, op0=ALU.mult), [bb, B_small], [B_On])
                    if qs == 3:
                        transpose_to(On, B_On, lambda: crossT[:, hd, qb * 512:(qb + 1) * 512], B_cT)

                attn(j, QT, memK[:, hd, :], lambda kt: memV[:, kt, hd, :], lambda qb: [0, 1], 128.0 ** -0.5, Pt, None,
                     [B_QT], lambda kt: [B_mK], lambda kt: [B_mV], out_fn, causal=False)
            for mg in range(D // 256):
                rd, (vo,) = load_w([(w_co[l].rearrange("(c p) n -> p c n", p=128)[:, :, mg * 256:(mg + 1) * 256], [128, 4, 256])])
                for mm in range(2):
                    for b in range(NBLK):
                        bi = mma_ring.next()
                        mm_group(bi, banks[bi][:], [(vo[:, c, mm * 128:(mm + 1) * 128], crossT[:, c, b * 512:(b + 1) * 512]) for c in range(4)], list(rd) + [B_cT])
                        rmw_h(bi, banks[bi][:], mg * 2 + mm, j * NBLK + b, 1.0)

        B_fk = [[Buf("fkc") for _ in range(NFH)] for _ in range(L)]
        B_fv = [[Buf("fvc") for _ in range(NFH)] for _ in range(L)]
        B_dk = [[Buf("dkc") for _ in range(NDH)] for _ in range(L)]
        B_dv = [[Buf("dvc") for _ in range(NDH)] for _ in range(L)]
        B_fc = [Buf("fcc") for _ in range(L)]
        kv_dsem = [new_dsem() for _ in range(4)]

        def stage_mix(j, l):
            sc.fence()
            stage_norm(j, 4 * l + 1)
            mixT = R[:, 0:16384].rearrange("p (c t) -> p c t", c=KC)
            cosT = Rf32(16384, SBT); sinT = Rf32(18432, SBT); Ctb = Rf32(20480, SBT)
            QT = R[:, 22528:23552]
            KT = R[:, 23552:27648]
            V1 = R[:, 27648:31776].rearrange("p (t d) -> p t d", d=129)
            negc = Rf32(31776, 32 * NFH).rearrange("p (t h) -> p t h", h=NFH)
            cT = Rf32(32160, SBT); lf = Rf32(34208, SBT); onesr = Rf32(36256, SBT)
            lf_full = lf
            Pt = Ring([(R[:, 38304 + i * 512:38304 + (i + 1) * 512], Buf("P%d" % i)) for i in range(3)])
            On = R[:, 39840:40352].rearrange("p (q d) -> p q d", q=4)
            O1n = Rf32(40352, 1024).rearrange("p (q d) -> p q d", q=8)
            tmpA = Rf32(42400, SBT)
            B_mixT = Buf("mixT"); B_cos = Buf("cos"); B_sin = Buf("sin"); B_Ctb = Buf("Ctb"); B_QT = Buf("QT")
            B_KTo = Buf("KTown"); B_KTp = Buf("KTprev"); B_Vo = Buf("Vown"); B_Vp = Buf("Vprev")
            B_nco = Buf("negc_own"); B_ncp = Buf("negc_prev"); B_cT = Buf("cT"); B_lf = Buf("lf"); B_ones = Buf("onesr")
            B_On = Buf("On"); B_O1 = Buf("O1n"); B_tmpA = Buf("tmpA")
            tok0 = j * SBT
            npt = j * NT
            nkt = lambda qb: list(range((tok0 + (qb + 1) * 512) // 128))
            rk = lambda kt: [B_KTp] if kt < npt else [B_KTo]
            rv = lambda kt: [B_Vp] if kt < npt else [B_Vo]

            posi = tmpA.bitcast(I32)
            DMA_SP(posi, pos_d[tok0:tok0 + SBT].partition_broadcast(128), [], [B_tmpA], misc_dsem)
            TWO_PI = float(2.0 * np.pi)
            PI = float(np.pi)
            posf = Ctb
            DVE(lambda e: e.tensor_copy(out=posf, in_=posi), [B_tmpA], [B_Ctb])

            def trig_table(dst, B_dst, shift):
                kf = tmpA
                ki = tmpA.bitcast(I32)
                DVE(lambda e: e.tensor_scalar(out=dst, in0=posf, scalar1=invf, scalar2=float(shift), op0=ALU.mult, op1=ALU.add), [B_Ctb, B_const], [B_dst])
                DVE(lambda e: e.tensor_scalar(out=ki, in0=dst, scalar1=float(1.0 / TWO_PI), scalar2=None, op0=ALU.mult), [B_dst], [B_tmpA])
                DVE(lambda e: e.tensor_copy(out=lf_full, in_=ki), [B_tmpA], [B_lf])
                DVE(lambda e: e.scalar_tensor_tensor(out=dst, in0=lf_full, scalar=-TWO_PI, in1=dst, op0=ALU.mult, op1=ALU.add), [B_lf, B_dst], [B_dst])
                DVE(lambda e: e.tensor_scalar(out=kf, in0=dst, scalar1=PI, scalar2=-TWO_PI, op0=ALU.is_gt, op1=ALU.mult), [B_dst], [B_tmpA])
                DVE(lambda e: e.tensor_tensor(out=dst, in0=dst, in1=kf, op=ALU.add), [B_dst, B_tmpA], [B_dst])
                DVE(lambda e: e.tensor_scalar(out=kf, in0=dst, scalar1=-PI, scalar2=TWO_PI, op0=ALU.is_lt, op1=ALU.mult), [B_dst], [B_tmpA])
                DVE(lambda e: e.tensor_tensor(out=dst, in0=dst, in1=kf, op=ALU.add), [B_dst, B_tmpA], [B_dst])
                DVE(lambda e: e.tensor_scalar(out=dst, in0=dst, scalar1=PI, scalar2=-PI, op0=ALU.min, op1=ALU.max), [B_dst], [B_dst])
                ACT(lambda e: e.activation(out=dst, in_=dst, func=AF.Sin), [B_dst], [B_dst])

            trig_table(sinT, B_sin, 0.0)
            trig_table(cosT, B_cos, PI / 2)
            DVE(lambda e: e.memset(V1[:, :, 128:129], 1.0), [], [B_Vo, B_Vp])
            DVE(lambda e: e.memset(onesr[0:NFH, :], 1.0), [], [B_ones])

            rd, (vf,) = load_w([(wview(w_in, l, OFF_FF, NFH), [128, KC, NFH])])
            for b in range(NBLK):
                bi = proj_fm(lambda k: vf[:, k, :], rd, b, pslice=(0, NFH))
                ACT(lambda e, o=lf[0:NFH, b * 512:(b + 1) * 512], p=banks[bi][0:NFH, :]: e.activation(out=o, in_=p, func=AF.Exp, bias=nfbias[:, l:l + 1], scale=-1.0),
                    [bank_buf[bi], B_const], [B_lf])
            ACT(lambda e: e.activation(out=lf[0:NFH, :], in_=lf[0:NFH, :], func=AF.Ln, bias=one_t[0:NFH, 0:1], scale=1.0), [B_lf, B_const], [B_lf])
            DVE(lambda e: e.tensor_tensor_scan(out=cT[0:NFH, :], data0=onesr[0:NFH, :], data1=lf[0:NFH, :], initial=ccarry[:, l:l + 1], op0=ALU.mult, op1=ALU.subtract),
                [B_ones, B_lf, B_carry], [B_cT])
            DVE(lambda e: e.tensor_copy(out=ccarry[:, l:l + 1], in_=cT[0:NFH, SBT - 1:SBT]), [B_cT], [B_carry])
            bi = mma_ring.next()
            for t in range(NT):
                PE(lambda e, o=banks[bi][:, t * NFH:(t + 1) * NFH], a=cT[0:NFH, t * 128:(t + 1) * 128]: e.transpose(o, a, ident_f[0:NFH, 0:NFH]),
                   [B_cT, B_const], [bank_buf[bi]], signal=(t == NT - 1))
            ACT(lambda e, o=negc[:, npt:npt + NT, :], p=banks[bi][:, 0:NT * NFH].rearrange("p (t h) -> p t h", h=NFH): e.mul(o, p, -1.0), [bank_buf[bi]], [B_nco])
            if j < nsb - 1:
                DMA_SP(fc_d[l, :, npt:npt + NT, :], negc[:, npt:npt + NT, :], [B_nco], [B_fc[l]], kv_dsem[3])
            if j > 0:
                DMA_SP(negc[:, 0:npt, :], fc_d[l, :, 0:npt, :], [B_fc[l]], [B_ncp], kv_dsem[3])

            def kv_proj(vk, vv, rd, kc_d, vc_d, B_kc, B_vc, rope):
                for b in range(NBLK):
                    bi = proj_fm(lambda k: vk[:, k, :], [rd[1]], b)
                    dst = KT[:, tok0 + b * 512:tok0 + (b + 1) * 512]
                    if rope:
                        rope_apply(bi, dst, B_KTo, b)
                    else:
                        ACT(lambda e, o=dst, p=banks[bi]: e.copy(o, p[:]), [bank_buf[bi]], [B_KTo])
                for tq in range(NT // 4):
                    bi = mma_ring.next()
                    for t4 in range(4):
                        t = tq * 4 + t4
                        mm_group(bi, banks[bi][:, t4 * 128:(t4 + 1) * 128], [(xnT[:, k, t * 128:(t + 1) * 128], vv[:, k, :]) for k in range(KC)],
                                 [rd[2], B_xnT[t // 4]])
                    ACT(lambda e, o=V1[:, npt + tq * 4:npt + tq * 4 + 4, 0:128], p=banks[bi][:].rearrange("p (t d) -> p t d", t=4): e.copy(o, p), [bank_buf[bi]], [B_Vo])
                if j < nsb - 1:
                    DMA_SP(kc_d[:, tok0:tok0 + SBT], KT[:, tok0:tok0 + SBT], [B_KTo], [B_kc], kv_dsem[0])
                    DMA_SP(vc_d[:, npt:npt + NT, :], V1[:, npt:npt + NT, :], [B_Vo], [B_vc], kv_dsem[1])
                if j > 0:
                    DMA_SP(KT[:, 0:tok0], kc_d[:, 0:tok0], [B_kc], [B_KTp], kv_dsem[0])
                    DMA_SP(V1[:, 0:npt, :], vc_d[:, 0:npt, :], [B_vc], [B_Vp], kv_dsem[1])

            def rope_apply(bi, dst, B_dst, b):
                i, wt, wb = get_wt()
                ACT(lambda e, o=wt, p=banks[bi]: e.copy(o[:], p[:]), [bank_buf[bi]], [wb])
                b2 = mma_ring.next()
                PE(lambda e, o=banks[b2], w=wt: e.matmul(o[:], perm_f, w[:], start=True, stop=True), [wb, B_const], [bank_buf[b2]], signal=True)
                i2, wt2, wb2 = get_wt()
                DVE(lambda e, o=wt2, p=banks[b2], s=sinT[:, b * 512:(b + 1) * 512]: e.tensor_tensor(out=o[:], in0=p[:], in1=s, op=ALU.mult), [bank_buf[b2], B_sin], [wb2])
                DVE(lambda e, o=wt, c=cosT[:, b * 512:(b + 1) * 512]: e.tensor_tensor(out=o[:], in0=o[:], in1=c, op=ALU.mult), [wb, B_cos], [wb])
                DVE(lambda e, o=dst, a=wt, c=wt2: e.tensor_tensor(out=o, in0=a[:], in1=c[:], op=ALU.add), [wb, wb2], [B_dst])

            for hd in range(NFH):
                rd, (vq, vk, vv) = load_w([(wview(w_in, l, OFF_FQ + hd * 128, 128), [128, KC, 128]),
                                           (wview(w_in, l, OFF_FK + hd * 128, 128), [128, KC, 128]),
                                           (wview(w_in, l, OFF_FV + hd * 128, 128), [128, KC, 128])])
                for b in range(NBLK):
                    bi = proj_fm(lambda k: vq[:, k, :], [rd[0]], b)
                    ACT(lambda e, o=QT[:, b * 512:(b + 1) * 512], p=banks[bi]: e.copy(o, p[:]), [bank_buf[bi]], [B_QT])
                kv_proj(vk, vv, rd, fk_d[l, hd], fv_d[l, hd], B_fk[l][hd], B_fv[l][hd], rope=False)
                for b in range(NBLK):
                    bi = mma_ring.next()
                    PE(lambda e, o=banks[bi], a=selt[0:NFH, hd * 128:(hd + 1) * 128], r=cT[0:NFH, b * 512:(b + 1) * 512]: e.matmul(o[:], a, r, start=True, stop=True),
                       [B_cT, B_const], [bank_buf[bi]], signal=True)
                    ACT(lambda e, o=Ctb[:, b * 512:(b + 1) * 512], p=banks[bi]: e.copy(o, p[:]), [bank_buf[bi]], [B_Ctb])

                def out_fn(qb, qs, acc, rec, bb, hd=hd):
                    DVE(lambda e, o=On[:, qs, :], a=acc, r=rec: e.tensor_scalar(out=o, in0=a, scalar1=r, scalar2=None, op0=ALU.mult), [bb, B_small], [B_On])
                    if qs == 3:
                        transpose_to(On, B_On, lambda: mixT[:, hd, qb * 512:(qb + 1) * 512], B_mixT)

                attn(j, QT, KT, lambda kt: V1[:, kt, :], nkt, 128.0 ** -0.5, Pt, None, [B_QT], rk, rv, out_fn,
                     Ctb=Ctb, B_Ctb=B_Ctb, negc=lambda kt, hd=hd: negc[:, kt, hd:hd + 1], B_negc=lambda kt: (B_ncp if kt < npt else B_nco))

            for hd in range(NDH):
                rd, (vq, vk, vv) = load_w([(wview(w_in, l, OFF_DQ + hd * 128, 128), [128, KC, 128]),
                                           (wview(w_in, l, OFF_DK + hd * 128, 128), [128, KC, 128]),
                                           (wview(w_in, l, OFF_DV + hd * 128, 128), [128, KC, 128])])
                for b in range(NBLK):
                    bi = proj_fm(lambda k: vq[:, k, :], [rd[0]], b)
                    rope_apply(bi, QT[:, b * 512:(b + 1) * 512], B_QT, b)
                kv_proj(vk, vv, rd, dk_d[l, hd], dv_d[l, hd], B_dk[l][hd], B_dv[l][hd], rope=True)
                for comp in range(2):
                    p0, p1 = comp * 64, comp * 64 + 64

                    def out_fn(qb, qs, acc, rec, bb, hd=hd, comp=comp):
                        if comp == 0:
                            DVE(lambda e, o=O1n[:, qb * 4 + qs, :], a=acc, r=rec: e.tensor_scalar(out=o, in0=a, scalar1=r, scalar2=None, op0=ALU.mult), [bb, B_small], [B_O1])
                            return
                        i, wt, wb = get_wt()
                        a_ap = wt[:, 0:128]
                        DVE(lambda e, o=a_ap, a=acc, r=rec: e.tensor_scalar(out=o, in0=a, scalar1=r, scalar2=None, op0=ALU.mult), [bb, B_small], [wb])
                        DVE(lambda e, o=a_ap, o1=O1n[:, qb * 4 + qs, :]: e.scalar_tensor_tensor(out=o, in0=o, scalar=nlamt[:, l:l + 1], in1=o1, op0=ALU.mult, op1=ALU.add),
                            [wb, B_O1, B_const], [wb])
                        c1 = smallcols.next(); c2 = smallcols.next()
                        ACT(lambda e, a=a_ap, j_=wt[:, 128:256], s=small[:, c1:c1 + 1]: e.activation(out=j_, in_=a, func=AF.Square, accum_out=s), [wb], [wb, B_small])
                        ACT(lambda e, s=small[:, c1:c1 + 1], r=small[:, c2:c2 + 1]: e.activation(out=r, in_=s, func=AF.Sqrt, bias=eps_t[:, 0:1], scale=1.0 / 128), [B_small, B_const], [B_small])
                        DVE(lambda e, r=small[:, c2:c2 + 1]: e.reciprocal(out=r, in_=r), [B_small], [B_small])
                        DVE(lambda e, o=On[:, qs, :], a=a_ap, r=small[:, c2:c2 + 1]: e.scalar_tensor_tensor(out=o, in0=a, scalar=r, in1=subln[:, l, :], op0=ALU.mult, op1=ALU.mult),
                            [wb, B_small, B_const], [B_On])
                        if qs == 3:
                            transpose_to(On, B_On, lambda: mixT[:, NFH + hd, qb * 512:(qb + 1) * 512], B_mixT)

                    attn(j, QT[p0:p1, :], KT[p0:p1, :], lambda kt: V1[:, kt, :], nkt, 64.0 ** -0.5, Pt, None, [B_QT], rk, rv, out_fn)

            sc.fence()
            u = Rf32(22528, SBT + 2); gbS = Rf32(24580, SBT); tconv = Rf32(26628, SBT)
            B_u = Buf("u"); B_gbS = Buf("gbS"); B_tc = Buf("tconv")
            for ch in range(4):
                rd, (vgb, vgc, vhc) = load_w([(wview(w_in, l, OFF_GB + ch * 128, 128), [128, KC, 128]),
                                              (wview(w_in, l, OFF_GC + ch * 128, 128), [128, KC, 128]),
                                              (wview(w_in, l, OFF_HC + ch * 128, 128), [128, KC, 128])])
                ACT(lambda e, ch=ch: e.copy(u[:, 0:2], halo[:, l, ch, :]), [B_halo], [B_u])
                for b in range(NBLK):
                    bi = proj_fm(lambda k: vgb[:, k, :], [rd[0]], b)
                    ACT(lambda e, o=gbS[:, b * 512:(b + 1) * 512], p=banks[bi]: e.copy(o, p[:]), [bank_buf[bi]], [B_gbS])
                    bi = proj_fm(lambda k: vgc[:, k, :], [rd[1]], b)
                    i, wt, wb = get_wt()
                    ACT(lambda e, o=wt, p=banks[bi]: e.copy(o[:], p[:]), [bank_buf[bi]], [wb])
                    bi = proj_fm(lambda k: vhc[:, k, :], [rd[2]], b)
                    DVE(lambda e, o=u[:, 2 + b * 512:2 + (b + 1) * 512], w=wt, p=banks[bi]: e.tensor_tensor(out=o, in0=w[:], in1=p[:], op=ALU.mult), [wb, bank_buf[bi]], [B_u])
                cw = convw[:, l, ch, :]
                DVE(lambda e, cw=cw: e.tensor_scalar(out=tconv, in0=u[:, 2:2 + SBT], scalar1=cw[:, 2:3], scalar2=cw[:, 3:4], op0=ALU.mult, op1=ALU.add), [B_u, B_const], [B_tc])
                DVE(lambda e, cw=cw: e.scalar_tensor_tensor(out=tconv, in0=u[:, 1:1 + SBT], scalar=cw[:, 1:2], in1=tconv, op0=ALU.mult, op1=ALU.add), [B_u, B_tc, B_const], [B_tc])
                DVE(lambda e, cw=cw: e.scalar_tensor_tensor(out=tconv, in0=u[:, 0:SBT], scalar=cw[:, 0:1], in1=tconv, op0=ALU.mult, op1=ALU.add), [B_u, B_tc, B_const], [B_tc])
                DVE(lambda e, ch=ch: e.tensor_tensor(out=mixT[:, 2 * NFH + ch, :], in0=tconv, in1=gbS, op=ALU.mult), [B_tc, B_gbS], [B_mixT])
                ACT(lambda e, ch=ch: e.copy(halo[:, l, ch, :], u[:, SBT:SBT + 2]), [B_u], [B_halo])

            for mg in range(D // 256):
                rd, (vo,) = load_w([(wview(w_out, l, mg * 256, 256), [128, KC, 256])])
                for mm in range(2):
                    for b in range(NBLK):
                        bi = mma_ring.next()
                        mm_group(bi, banks[bi][:], [(vo[:, c, mm * 128:(mm + 1) * 128], mixT[:, c, b * 512:(b + 1) * 512]) for c in range(KC)], list(rd) + [B_mixT])
                        rmw_h(bi, banks[bi][:], mg * 2 + mm, j * NBLK + b, 1.0)

        B_carry = Buf("ccarry")
        B_halo = Buf("halo")
        one_t = small[:, 33:34]
        final_evs = []
        eps_t = small[:, 32:33]
        B_const.const = False
        DVE(lambda e: e.memset(small[:, 32:33], EPS), [], [B_const])
        DVE(lambda e: e.memset(small[:, 33:34], 1.0), [], [B_const])
        for l in range(L):
            DVE(lambda e, l=l: e.tensor_scalar(out=subln[:, l, :], in0=subln[:, l, :], scalar1=laminit[:, l:l + 1], scalar2=None, op0=ALU.mult), [B_const], [B_const])
        sc.fence(("pe", "act", "dve", "sp", "pool"))
        B_const.const = True

        if STAGES["cross"]:
            for l in range(L):
                stage_mem(l)
        for j in range(nsb):
            stage_init(j)
            for l in range(L):
                if STAGES["ffn1"]:
                    stage_ffn(j, l, w_g1, w_u1, w_d1, 4 * l + 0)
                if STAGES["mix"]:
                    stage_mix(j, l)
                if STAGES["cross"]:
                    stage_cross(j, l)
                if STAGES["ffn2"]:
                    stage_ffn(j, l, w_g2, w_u2, w_d2, 4 * l + 3)
            stage_final(j)

        for ev in final_evs:
            sc._wait("sp", ev)
        sc.fence(("pe", "act", "dve", "sp", "pool"))

        engmap = {"pe": "tensor", "act": "scalar", "dve": "vector", "pool": "gpsimd", "sp": "sync"}
        with nc.Block() as block:
            for en in Sched.ENGS:
                prog = sc.q[en]

                def body(e, prog=prog):
                    for it in prog:
                        if it[0] == "w":
                            e.wait_ge(it[1], it[2])
                        else:
                            ins = it[1](e)
                            if it[2] is not None:
                                ins.then_inc(it[2], it[3])

                getattr(block, engmap[en])(body)
        stats = {e: len(sc.q[e]) for e in Sched.ENGS}
    return nc, stats


STAGES = {"ffn1": True, "mix": True, "cross": True, "ffn2": True}


def host_consts(depth):
    L = depth
    cf = np.zeros((128, 3 * 128 + 3), np.float32)
    cf[:, 0:128] = np.eye(128, dtype=np.float32)
    cf[:, 128:256] = 1.0
    perm = np.zeros((128, 128), np.float32)
    invf = np.zeros((128,), np.float32)
    inv_freq = ROPE_THETA ** (-np.arange(0, 16, 2, dtype=np.float32) / 16.0)
    for comp in range(2):
        base = comp * 64
        for i in range(8):
            m1, m2 = base + i, base + 8 + i
            perm[m2, m1] = 1.0
            perm[m1, m2] = 1.0
            invf[m1] = -inv_freq[i]
            invf[m2] = inv_freq[i]
    cf[:, 256:384] = perm
    cf[:, 384] = invf
    cf[:, 385] = -np.pi
    cbf = np.eye(128, dtype=np.float32).astype(ml_dtypes.bfloat16)
    sel = np.zeros((NFH, NFH * 128), np.float32)
    for h in range(NFH):
        sel[h, h * 128:(h + 1) * 128] = 1.0
    mask = np.zeros((128, 4, 512), np.float32)
    p = np.arange(128)[:, None]
    jj = np.arange(512)[None, :]
    for v in range(4):
        mask[:, v, :] = np.where(jj - p >= 128 * v, 0.0, -1e30)
    laminit = np.zeros((128, L), np.float32)
    for l in range(L):
        laminit[:, l] = 0.8 - 0.6 * np.exp(-0.3 * l)
    return cf, cbf, sel, mask, laminit


def make_in_maps(inputs, depth, s_len, batches):
    L = depth
    f = lambda a: np.ascontiguousarray(np.asarray(a))
    cf, cbf, sel, mask, laminit = host_consts(L)
    g = np.zeros((4 * L + 1, D), np.float32)
    for l in range(L):
        g[4 * l + 0] = inputs["ffn1_norm"][l]
        g[4 * l + 1] = inputs["mix_norm"][l]
        g[4 * l + 2] = inputs["cross_norm"][l]
        g[4 * l + 3] = inputs["ffn2_norm"][l]
    g[4 * L] = inputs["final_norm"]
    gains = f(g.reshape(4 * L + 1, KC, 128).transpose(2, 0, 1))
    memg = f(np.broadcast_to(np.asarray(inputs["mem_norm"])[None, :L, :], (128, L, D)))
    fb = f(np.asarray(inputs["forget_bias"])[:L].T)
    cw = np.zeros((128, L, 4, 4), np.float32)
    for l in range(L):
        for ch in range(4):
            cw[:, l, ch, 0:3] = np.asarray(inputs["conv_w"])[l, :, ch * 128:(ch + 1) * 128].T
            cw[:, l, ch, 3] = np.asarray(inputs["conv_b"])[l, ch * 128:(ch + 1) * 128]
    lam = np.stack([np.asarray(inputs[k])[:L] for k in ("lambda_q1", "lambda_k1", "lambda_q2", "lambda_k2")], axis=1)
    lam = f(np.broadcast_to(lam[None], (128, L, 4, 64)))
    subln = f(np.broadcast_to(np.asarray(inputs["diff_subln"])[None, :L, :], (128, L, 128)))
    shared = {
        "gains": gains, "mem_gain_b": memg, "forget_bias_t": fb, "conv_wb": cw, "lam_b": lam, "subln_b": subln,
        "const_f32": cf, "const_bf16": cbf, "const_sel": sel, "const_mask": mask, "const_laminit": laminit,
    }
    for k in ("ffn1_w_gate", "ffn1_w_up", "ffn1_w_down", "ffn2_w_gate", "ffn2_w_up", "ffn2_w_down",
              "mix_w_in", "mix_w_out", "cross_w_q", "cross_w_kv", "cross_w_o"):
        shared[k] = f(np.asarray(inputs[k])[:L])
    maps = []
    for b in batches:
        m = dict(shared)
        m["x"] = f(np.asarray(inputs["x"])[b, :s_len])
        m["mem"] = f(np.asarray(inputs["mem"])[b])
        m["positions"] = f(np.asarray(inputs["positions"])[b, :s_len].astype(np.int32))
        maps.append(m)
    return maps


_CACHE = {}


def run(inputs, depth=4, nsb=4, batches=(0, 1, 2, 3), trace=False):
    s_len = nsb * SBT
    key = (depth, nsb, tuple(sorted(STAGES.items())))
    if key not in _CACHE:
        _CACHE[key] = build_program(depth, nsb, s_len)
    nc, stats = _CACHE[key]
    maps = make_in_maps(inputs, depth, s_len, batches)
    res = run_bass_kernel_spmd(nc, maps, core_ids=list(range(len(batches))), trace=trace)
    outs = np.stack([r["out"] for r in res.results], axis=0)
    return outs, res, stats


def kernel(**inputs):
    outs, _, _ = run(inputs, depth=4, nsb=4, batches=(0, 1, 2, 3))
    return outs.astype(np.float32)
```

```python
import numpy as np
import ml_dtypes
from contextlib import ExitStack

import concourse.bass as bass
import concourse.mybir as mybir
from concourse.bass_utils import run_bass_kernel_spmd

F32 = mybir.dt.float32
BF16 = mybir.dt.bfloat16
I32 = mybir.dt.int32
AF = mybir.ActivationFunctionType
ALU = mybir.AluOpType

D = 2048
KC = 16
DFF = 5632
FC = 44
S_FULL = 4096
SBT = 1024
NBLK = SBT // 512
NT = SBT // 128
MEM = 256
NFH = 6
NDH = 6
IN_W = 6150
OFF_FQ, OFF_FK, OFF_FV, OFF_FF = 0, 768, 1536, 2304
OFF_DQ, OFF_DK, OFF_DV = 2310, 3078, 3846
OFF_GB, OFF_GC, OFF_HC = 4614, 5126, 5638
EPS = 1e-6
ROPE_THETA = 500000.0
N_CORES = 4


class Ev:
    __slots__ = ("eng", "sem", "val", "idx")

    def __init__(self, eng, sem=None, val=0, idx=0):
        self.eng, self.sem, self.val, self.idx = eng, sem, val, idx


class Buf:
    __slots__ = ("name", "w", "r", "const")

    def __init__(self, name):
        self.name, self.w, self.r, self.const = name, None, {}, False


class DmaSem:
    __slots__ = ("h", "count", "key")

    def __init__(self, h, key):
        self.h, self.count, self.key = h, 0, key


class Sched:
    ENGS = ("pe", "act", "dve", "pool", "sp")

    def __init__(self, esems):
        self.esem = esems
        self.q = {e: [] for e in self.ENGS}
        self.cnt = {e: 0 for e in self.ENGS}
        self.seen = {e: {} for e in self.ENGS}
        self.pe_idx = 0
        self.pe_sig = []
        self.ndma = 0
        self.dma_active = {}

    def _resolve(self, ev):
        if ev.eng == "pe" and ev.sem is None:
            lo, hi = 0, len(self.pe_sig)
            while lo < hi:
                mid = (lo + hi) // 2
                if self.pe_sig[mid][0] >= ev.idx:
                    hi = mid
                else:
                    lo = mid + 1
            assert lo < len(self.pe_sig), "PE event not yet signalled"
            return "pe", self.esem["pe"], self.pe_sig[lo][1]
        return ev.sem[0], ev.sem[1], ev.val

    def _wait(self, eng, ev):
        key, sem, val = self._resolve(ev)
        if self.seen[eng].get(key, 0) >= val:
            return
        self.seen[eng][key] = val
        self.q[eng].append(("w", sem, val))

    def op(self, eng, fn, reads=(), writes=(), signal=True, dma=None):
        deps = []
        for b in reads:
            if b.w is not None:
                deps.append(b.w)
        for b in writes:
            if b.w is not None:
                deps.append(b.w)
            deps.extend(b.r.values())
        is_dma = dma is not None
        for ev in deps:
            if (not is_dma) and eng == "pe" and ev.eng == "pe":
                continue
            self._wait(eng, ev)
        if is_dma:
            dma.count += 16
            ev = Ev("dma", (dma.key, dma.h), dma.count)
            self.q[eng].append(("i", fn, dma.h, 16))
            self.ndma += 1
            if eng != "pool":
                self.dma_active[dma.key] = ev
            rkey = ("dma", dma.key)
        elif eng == "pe":
            self.pe_idx += 1
            if signal:
                self.cnt["pe"] += 1
                self.pe_sig.append((self.pe_idx, self.cnt["pe"]))
                self.q[eng].append(("i", fn, self.esem["pe"], 1))
            else:
                self.q[eng].append(("i", fn, None, 0))
            ev = Ev("pe", None, 0, self.pe_idx)
            rkey = "pe"
        else:
            self.cnt[eng] += 1
            ev = Ev(eng, (eng, self.esem[eng]), self.cnt[eng])
            self.q[eng].append(("i", fn, self.esem[eng], 1))
            rkey = eng
        for b in reads:
            if not b.const:
                b.r[rkey] = ev
        for b in writes:
            b.w = ev
            b.r = {}
        return ev

    def fence(self, engs=("pe", "act", "dve", "sp")):
        last = {}
        for e in engs:
            if e == "pe":
                if self.pe_sig:
                    last[e] = Ev("pe", ("pe", self.esem["pe"]), self.pe_sig[-1][1])
            elif self.cnt[e] > 0:
                last[e] = Ev(e, (e, self.esem[e]), self.cnt[e])
        for e in engs:
            for o, ev in last.items():
                if o != e:
                    self._wait(e, ev)
            for ev in self.dma_active.values():
                self._wait(e, ev)
        self.dma_active = {}


def build_program(depth, nsb, s_len):
    nc = bass.Bass("TRN2", target_bir_lowering=False)
    L = depth

    def din(name, shape, dt=F32):
        return nc.dram_tensor(name, list(shape), dt, kind="ExternalInput").ap()

    def dscr(name, shape, dt):
        return nc.dram_tensor(name, list(shape), dt, kind="Internal").ap()

    x_d = din("x", [s_len, D])
    mem_d = din("mem", [MEM, D])
    pos_d = din("positions", [s_len], I32)
    w_g1 = din("ffn1_w_gate", [L, D, DFF]); w_u1 = din("ffn1_w_up", [L, D, DFF]); w_d1 = din("ffn1_w_down", [L, DFF, D])
    w_g2 = din("ffn2_w_gate", [L, D, DFF]); w_u2 = din("ffn2_w_up", [L, D, DFF]); w_d2 = din("ffn2_w_down", [L, DFF, D])
    w_in = din("mix_w_in", [L, D, IN_W]); w_out = din("mix_w_out", [L, D, D])
    w_cq = din("cross_w_q", [L, D, 512]); w_ckv = din("cross_w_kv", [L, D, 1024]); w_co = din("cross_w_o", [L, 512, D])
    gains_d = din("gains", [128, 4 * L + 1, KC])
    memg_d = din("mem_gain_b", [128, L, D])
    fbias_d = din("forget_bias_t", [NFH, L])
    convw_d = din("conv_wb", [128, L, 4, 4])
    lam_d = din("lam_b", [128, L, 4, 64])
    subln_d = din("subln_b", [128, L, 128])
    cf32_d = din("const_f32", [128, 3 * 128 + 1 + 2])
    cbf_d = din("const_bf16", [128, 128], BF16)
    sel_d = din("const_sel", [NFH, NFH * 128])
    mask_d = din("const_mask", [128, 4, 512])
    laminit_d = din("const_laminit", [128, L])
    out_d = nc.dram_tensor("out", [s_len, D], F32, kind="ExternalOutput").ap()

    hT_d = dscr("hT", [KC, 128, s_len], F32)
    fk_d = dscr("fk_cache", [L, NFH, 128, s_len], BF16)
    fv_d = dscr("fv_cache", [L, NFH, 128, s_len // 128, 129], BF16)
    fc_d = dscr("fc_cache", [L, 128, s_len // 128, NFH], F32)
    dk_d = dscr("dk_cache", [L, NDH, 128, s_len], BF16)
    dv_d = dscr("dv_cache", [L, NDH, 128, s_len // 128, 129], BF16)
    mk_d = dscr("memk", [L, 128, 4, MEM], BF16)
    mv_d = dscr("memv", [L, 128, 2, 4, 129], BF16)

    es = ExitStack()
    with es:
        def sb_t(name, shape, dt):
            return es.enter_context(nc.sbuf_tensor("s_" + name, list(shape), dt))

        esems = {e: es.enter_context(nc.semaphore("sem_" + e)) for e in Sched.ENGS}
        sc = Sched(esems)
        dma_sem_n = [0]

        def new_dsem():
            dma_sem_n[0] += 1
            h = es.enter_context(nc.semaphore("dsem%d" % dma_sem_n[0]))
            return DmaSem(h, "d%d" % dma_sem_n[0])

        xnT = sb_t("xnT", [128, KC, SBT], BF16)
        R = sb_t("R", [128, 45056], BF16)
        wsl = [sb_t("wsl%d" % i, [128, 11264], BF16) for i in range(2)]
        NWT = 8
        wts = [sb_t("wt%d" % i, [128, 512], F32) for i in range(NWT)]
        gains = sb_t("gains", [128, 4 * L + 1, KC], F32)
        cf32 = sb_t("cf32", [128, 3 * 128 + 3], F32)
        cbf = sb_t("cbf", [128, 128], BF16)
        selt = sb_t("selt", [NFH, NFH * 128], F32)
        maskt = sb_t("maskt", [128, 4, 512], F32)
        fbias = sb_t("fbias", [NFH, L], F32)
        nfbias = sb_t("nfbias", [NFH, L], F32)
        convw = sb_t("convw", [128, L, 4, 4], F32)
        subln = sb_t("subln", [128, L, 128], F32)
        lamt = sb_t("lamt", [128, L], F32)
        nlamt = sb_t("nlamt", [128, L], F32)
        laminit = sb_t("laminit", [128, L], F32)
        ccarry = sb_t("ccarry", [NFH, L], F32)
        halo = sb_t("halo", [128, L, 4, 2], F32)
        small = sb_t("small", [128, 64], F32)

        ident_f = cf32[:, 0:128]
        ones_f = cf32[:, 128:256]
        perm_f = cf32[:, 256:384]
        invf = cf32[:, 384:385]
        negpi = cf32[:, 385:386]
        ident_b = cbf[:, :]

        banks = [es.enter_context(nc.psum_tensor("bank%d" % i, [128, 512], F32)) for i in range(8)]
        bank_buf = [Buf("bank%d" % i) for i in range(8)]

        class Ring:
            def __init__(self, items):
                self.items, self.i = items, 0

            def next(self):
                it = self.items[self.i % len(self.items)]
                self.i += 1
                return it

        mm_ring = Ring([0, 1, 2, 3])
        s_ring = Ring([4, 5])
        rstd_t = [sb_t("rstd%d" % i, [128, 512], F32) for i in range(2)]
        rstd_b = [Buf("rstd%d" % i) for i in range(2)]
        rstd_i = [0]
        x_dsem = [new_dsem() for _ in range(NT)]
        wt_buf = [Buf("wt%d" % i) for i in range(NWT)]
        wt_ring = Ring(list(range(NWT)))
        wt_dsem = [new_dsem() for _ in range(NWT)]
        ws_buf = [[Buf("ws%d_%d" % (i, p)) for p in range(3)] for i in range(2)]
        ws_dsem = [[new_dsem() for p in range(3)] for i in range(2)]
        ws_ring = Ring([0, 1])

        B_xnT = [Buf("xnT%d" % b) for b in range(NBLK)]
        B_const = Buf("const"); B_const.const = True
        B_R = Buf("Rmisc")
        misc_dsem = new_dsem()
        hT_buf = [[Buf("hT%d_%d" % (c, b)) for b in range(s_len // 512)] for c in range(KC)]
        st_dsem = [new_dsem() for _ in range(4)]
        st_ring = Ring([0, 1, 2, 3])

        def PE(fn, reads, writes, signal):
            return sc.op("pe", fn, reads, writes, signal=signal)

        def ACT(fn, reads, writes):
            return sc.op("act", fn, reads, writes)

        def DVE(fn, reads, writes):
            return sc.op("dve", fn, reads, writes)

        def POOLC(fn, reads, writes):
            return sc.op("pool", fn, reads, writes)

        def DMA_SP(out, in_, reads, writes, dsem):
            return sc.op("sp", lambda e, o=out, i=in_: e.dma_start(out=o, in_=i), reads, writes, dma=dsem)

        def DMA_POOL(out, in_, reads, writes, dsem):
            return sc.op("pool", lambda e, o=out, i=in_: e.dma_start(out=o, in_=i), reads, writes, dma=dsem)

        def mm_group(bank_i, out_ap, pairs, reads, sig_all=False):
            n = len(pairs)
            for k, pr in enumerate(pairs):
                lt, rh = pr[0], pr[1]
                rds = list(reads) + (list(pr[2]) if len(pr) > 2 else [])
                PE(lambda e, o=out_ap, a=lt, b=rh, s=(k == 0), t=(k == n - 1): e.matmul(o, a, b, start=s, stop=t),
                   rds, [bank_buf[bank_i]], signal=(sig_all or k == n - 1))

        def load_w(parts):
            si = ws_ring.next()
            views = []
            for p, (src, shape) in enumerate(parts):
                n = 1
                for d_ in shape[1:]:
                    n *= d_
                v = wsl[si][:, p * 4096:p * 4096 + n] if len(parts) > 1 else wsl[si][:, 0:n]
                if len(shape) == 3:
                    v = v.rearrange("p (a b) -> p a b", a=shape[1])
                wr = [ws_buf[si][p]] if len(parts) > 1 else ws_buf[si]
                DMA_POOL(v, src, [], wr, ws_dsem[si][p])
                views.append(v)
            rd = ws_buf[si][:len(parts)] if len(parts) > 1 else ws_buf[si]
            return rd, views

        def wview(w_ap, l, c0, ncols):
            return w_ap[l].rearrange("(kc p) n -> p kc n", p=128)[:, :, c0:c0 + ncols]

        def get_wt():
            i = wt_ring.next()
            return i, wts[i], wt_buf[i]

        def load_const(dst, src):
            DMA_SP(dst, src, [], [B_const], misc_dsem)

        B_const.const = False
        load_const(gains[:], gains_d)
        load_const(cf32[:], cf32_d)
        load_const(cbf[:], cbf_d)
        load_const(selt[:], sel_d)
        load_const(maskt[:], mask_d)
        load_const(fbias[:], fbias_d)
        load_const(convw[:], convw_d)
        load_const(subln[:], subln_d)
        load_const(laminit[:], laminit_d)
        lamv = R[:, 0:L * 4 * 64 * 2].bitcast(F32).rearrange("p (l f d) -> p l f d", l=L, f=4)
        DMA_SP(lamv, lam_d, [], [B_R], misc_dsem)
        DVE(lambda e: e.memset(ccarry[:], 0.0), [], [B_const])
        DVE(lambda e: e.memset(halo[:], 0.0), [], [B_const])
        ACT(lambda e: e.mul(nfbias[:], fbias[:], -1.0), [B_const], [B_const])
        for l in range(L):
            s1 = small[:, 0:1]; s2 = small[:, 1:2]; junk = small[:, 8:8 + 0]
            pr = wts[0][:, 0:64]
            DVE(lambda e, l=l: e.tensor_tensor(out=wts[0][:, 0:64], in0=lamv[:, l, 0, :], in1=lamv[:, l, 1, :], op=ALU.mult), [B_R], [wt_buf[0]])
            DVE(lambda e: e.reduce_sum(out=small[:, 0:1], in_=wts[0][:, 0:64], axis=mybir.AxisListType.X), [wt_buf[0]], [B_const])
            DVE(lambda e, l=l: e.tensor_tensor(out=wts[0][:, 64:128], in0=lamv[:, l, 2, :], in1=lamv[:, l, 3, :], op=ALU.mult), [B_R], [wt_buf[0]])
            DVE(lambda e: e.reduce_sum(out=small[:, 1:2], in_=wts[0][:, 64:128], axis=mybir.AxisListType.X), [wt_buf[0]], [B_const])
            ACT(lambda e: e.activation(out=small[:, 2:4], in_=small[:, 0:2], func=AF.Exp), [B_const], [B_const])
            DVE(lambda e: e.tensor_tensor(out=small[:, 4:5], in0=small[:, 2:3], in1=small[:, 3:4], op=ALU.subtract), [B_const], [B_const])
            DVE(lambda e, l=l: e.tensor_tensor(out=lamt[:, l:l + 1], in0=small[:, 4:5], in1=laminit[:, l:l + 1], op=ALU.add), [B_const], [B_const])
            DVE(lambda e, l=l: e.tensor_scalar(out=nlamt[:, l:l + 1], in0=lamt[:, l:l + 1], scalar1=-1.0, scalar2=None, op0=ALU.mult), [B_const], [B_const])
        DVE(lambda e: e.tensor_scalar(out=laminit[:], in0=laminit[:], scalar1=-1.0, scalar2=1.0, op0=ALU.mult, op1=ALU.add), [B_const], [B_const])
        sc.fence(("pe", "act", "dve", "sp", "pool"))
        B_const.const = True

        def rmw_h(bank_i, ps_ap, chunk, gblk, scale):
            i, wt, wb = get_wt()
            hb = hT_buf[chunk][gblk]
            DMA_SP(wt[:], hT_d[chunk, :, gblk * 512:(gblk + 1) * 512], [hb], [wb], wt_dsem[i])
            DVE(lambda e, o=wt, p=ps_ap, s=scale: e.scalar_tensor_tensor(out=o[:], in0=p, scalar=s, in1=o[:], op0=ALU.mult, op1=ALU.add),
                [bank_buf[bank_i], wb], [wb])
            DMA_SP(hT_d[chunk, :, gblk * 512:(gblk + 1) * 512], wt[:], [wb], [hb], wt_dsem[i])

        def stage_init(j):
            sc.fence()
            xt = R[:, 0:NT * D * 2].bitcast(F32).rearrange("p (t d) -> p t d", t=NT)
            xb = [Buf("xin%d" % t) for t in range(NT)]
            for t in range(NT):
                tok0 = j * SBT + t * 128
                DMA_SP(xt[:, t, :], x_d[tok0:tok0 + 128, :], [], [xb[t]], x_dsem[t])
            for c in range(KC):
                for b in range(NBLK):
                    bi = mm_ring.next()
                    for q in range(4):
                        t = b * 4 + q
                        PE(lambda e, o=banks[bi][:, q * 128:(q + 1) * 128], a=xt[:, t, c * 128:(c + 1) * 128]: e.transpose(o, a, ident_f),
                           [xb[t], B_const], [bank_buf[bi]], signal=(q == 3))
                    i, wt, wb = get_wt()
                    ACT(lambda e, o=wt, p=banks[bi]: e.copy(o[:], p[:]), [bank_buf[bi]], [wb])
                    gb = j * NBLK + b
                    DMA_SP(hT_d[c, :, gb * 512:(gb + 1) * 512], wt[:], [wb], [hT_buf[c][gb]], wt_dsem[i])
            sc.fence()

        def stage_norm(j, gidx, out_bf=True, fin=None):
            for b in range(NBLK):
                gb = j * NBLK + b
                bi = mm_ring.next()
                for c in range(KC):
                    i, wt, wb = get_wt()
                    DMA_SP(wt[:], hT_d[c, :, gb * 512:(gb + 1) * 512], [hT_buf[c][gb]], [wb], wt_dsem[i])
                    ACT(lambda e, o=wt: e.activation(out=o[:], in_=o[:], func=AF.Square), [wb], [wb])
                    PE(lambda e, o=banks[bi], r=wt, s=(c == 0), t=(c == KC - 1): e.matmul(o[:], ones_f, r[:], start=s, stop=t),
                       [wb, B_const], [bank_buf[bi]], signal=True)
                rt, rb = rstd_t[rstd_i[0] % 2], rstd_b[rstd_i[0] % 2]
                rstd_i[0] += 1
                ACT(lambda e, o=rt, p=banks[bi]: e.activation(out=o[:], in_=p[:], func=AF.Ln, bias=eps_t[:, 0:1], scale=1.0 / D),
                    [bank_buf[bi], B_const], [rb])
                ACT(lambda e, o=rt: e.activation(out=o[:], in_=o[:], func=AF.Exp, scale=-0.5), [rb], [rb])
                for c in range(KC):
                    i, wt, wb = get_wt()
                    DMA_SP(wt[:], hT_d[c, :, gb * 512:(gb + 1) * 512], [hT_buf[c][gb]], [wb], wt_dsem[i])
                    if fin is None:
                        DVE(lambda e, o=xnT[:, c, b * 512:(b + 1) * 512], w=wt, r=rt, g=gains[:, gidx, c:c + 1]:
                            e.scalar_tensor_tensor(out=o, in0=w[:], scalar=g, in1=r[:], op0=ALU.mult, op1=ALU.mult),
                            [wb, rb, B_const], [B_xnT[b]])
                    else:
                        DVE(lambda e, w=wt, r=rt, g=gains[:, gidx, c:c + 1]:
                            e.scalar_tensor_tensor(out=w[:], in0=w[:], scalar=g, in1=r[:], op0=ALU.mult, op1=ALU.mult),
                            [wb, rb, B_const], [wb])
                        fin(c, b, wt, wb)

        def stage_ffn(j, l, wg, wu, wd, gidx):
            sc.fence()
            stage_norm(j, gidx)
            gT = R[:, 0:FC * SBT].rearrange("p (c t) -> p c t", c=FC)
            B_gT = [[Buf("gT%d_%d" % (c, b)) for b in range(NBLK)] for c in range(FC)]
            for cp in range(FC // 2):
                rd, (vg, vu) = load_w([(wview(wg, l, cp * 256, 256), [128, KC, 256]),
                                       (wview(wu, l, cp * 256, 256), [128, KC, 256])])
                for b in range(NBLK):
                    for cc in range(2):
                        c = cp * 2 + cc
                        bg = mm_ring.next()
                        mm_group(bg, banks[bg][:], [(vg[:, k, cc * 128:(cc + 1) * 128], xnT[:, k, b * 512:(b + 1) * 512]) for k in range(KC)],
                                 [rd[0], B_xnT[b]])
                        bu = mm_ring.next()
                        mm_group(bu, banks[bu][:], [(vu[:, k, cc * 128:(cc + 1) * 128], xnT[:, k, b * 512:(b + 1) * 512]) for k in range(KC)],
                                 [rd[1], B_xnT[b]])
                        i, wt, wb = get_wt()
                        ACT(lambda e, o=wt, p=banks[bg]: e.activation(out=o[:], in_=p[:], func=AF.Silu), [bank_buf[bg]], [wb])
                        DVE(lambda e, o=gT[:, c, b * 512:(b + 1) * 512], w=wt, p=banks[bu]: e.tensor_tensor(out=o, in0=w[:], in1=p[:], op=ALU.mult),
                            [wb, bank_buf[bu]], [B_gT[c][b]])
            for mg in range(D // 256):
                rd, (vd,) = load_w([(wd[l].rearrange("(c p) n -> p c n", p=128)[:, :, mg * 256:(mg + 1) * 256], [128, FC, 256])])
                for mm in range(2):
                    m = mg * 2 + mm
                    for b in range(NBLK):
                        bi = mm_ring.next()
                        mm_group(bi, banks[bi][:], [(vd[:, c, mm * 128:(mm + 1) * 128], gT[:, c, b * 512:(b + 1) * 512], [B_gT[c][b]]) for c in range(FC)],
                                 list(rd))
                        rmw_h(bi, banks[bi][:], m, j * NBLK + b, 0.5)

        def stage_final(j):
            sc.fence()
            ot = R[:, 0:NT * D * 2].bitcast(F32).rearrange("p (t d) -> p t d", t=NT)
            ob = [Buf("oout%d" % t) for t in range(NT)]

            def fin(c, b, wt, wb):
                bi = mm_ring.next()
                for q in range(4):
                    PE(lambda e, o=banks[bi][:, q * 128:(q + 1) * 128], a=wt[:, q * 128:(q + 1) * 128]: e.transpose(o, a, ident_f),
                       [wb, B_const], [bank_buf[bi]], signal=(q == 3))
                for q in range(4):
                    t = b * 4 + q
                    ACT(lambda e, o=ot[:, t, c * 128:(c + 1) * 128], p=banks[bi][:, q * 128:(q + 1) * 128]: e.copy(o, p),
                        [bank_buf[bi]], [ob[t]])

            stage_norm(j, 4 * L, fin=fin)
            for t in range(NT):
                tok0 = j * SBT + t * 128
                ev = DMA_SP(out_d[tok0:tok0 + 128, :], ot[:, t, :], [ob[t]], [Buf("outd")], x_dsem[t])
                final_evs.append(ev)
            sc.fence()


        mma_ring = Ring([0, 1, 2, 3])
        acc_sets = Ring([(4, 5), (6, 7)])
        s_ring3 = mma_ring
        deferred = []

        def flush_deferred():
            while deferred:
                deferred.pop(0)()

        B_smallc = {c: Buf("sc%d" % c) for c in range(34, 64)}
        smallcols = Ring(list(range(40, 64)))
        B_small = Buf("smallcols")

        def Rf32(off, n):
            return R[:, off:off + 2 * n].bitcast(F32)

        def attn(j, QT, KT, vtile, nkt_list, scale, Pt, B_P, reads_q, reads_k, reads_v, out_fn,
                 Ctb=None, B_Ctb=None, negc=None, B_negc=None, causal=True):
            for qb in range(NBLK):
                q0 = j * SBT + qb * 512
                kts = nkt_list(qb)
                accb = acc_sets.next()
                n_k = len(kts)

                def issue_S(ki):
                    kt = kts[ki]
                    k0 = kt * 128
                    diag = causal and (k0 >= q0)
                    bS = s_ring3.next()
                    PE(lambda e, o=banks[bS], a=KT[:, k0:k0 + 128], b=QT[:, qb * 512:(qb + 1) * 512]: e.matmul(o[:], a, b, start=True, stop=True),
                       list(reads_q) + list(reads_k(kt)), [bank_buf[bS]], signal=True)
                    Pv, Pb = Pt.next()
                    if Ctb is not None or diag:
                        i, wt, wb = get_wt()
                        if Ctb is not None:
                            DVE(lambda e, o=wt, p=banks[bS], c=Ctb[:, qb * 512:(qb + 1) * 512], s=scale:
                                e.scalar_tensor_tensor(out=o[:], in0=p[:], scalar=s, in1=c, op0=ALU.mult, op1=ALU.add),
                                [bank_buf[bS], B_Ctb], [wb])
                            if diag:
                                v = (k0 - q0) // 128
                                DVE(lambda e, o=wt, m=maskt[:, v, :]: e.tensor_tensor(out=o[:], in0=o[:], in1=m, op=ALU.add), [wb, B_const], [wb])
                        else:
                            v = (k0 - q0) // 128
                            DVE(lambda e, o=wt, p=banks[bS], m=maskt[:, v, :], s=scale:
                                e.scalar_tensor_tensor(out=o[:], in0=p[:], scalar=s, in1=m, op0=ALU.mult, op1=ALU.add),
                                [bank_buf[bS], B_const], [wb])
                        if negc is not None:
                            ACT(lambda e, o=Pv, w=wt, bcol=negc(kt): e.activation(out=o, in_=w[:], func=AF.Exp, bias=bcol, scale=1.0),
                                [wb, B_negc(kt)], [Pb])
                        else:
                            ACT(lambda e, o=Pv, w=wt: e.activation(out=o, in_=w[:], func=AF.Exp), [wb], [Pb])
                    else:
                        ACT(lambda e, o=Pv, p=banks[bS], s=scale: e.activation(out=o, in_=p[:], func=AF.Exp, scale=s), [bank_buf[bS]], [Pb])
                    return Pv, Pb

                LA = 2
                pend = {}
                for ki in range(min(LA, n_k)):
                    pend[ki] = issue_S(ki)
                for ki in range(n_k):
                    if ki + LA < n_k:
                        pend[ki + LA] = issue_S(ki + LA)
                    Pv, Pb = pend.pop(ki)
                    kt = kts[ki]
                    if ki == min(2, n_k - 1):
                        flush_deferred()
                    for qs in range(4):
                        bk = accb[qs // 2]
                        off = (qs % 2) * 129
                        first = (ki == 0)
                        PE(lambda e, o=banks[bk][:, off:off + 129], a=Pv[:, qs * 128:(qs + 1) * 128], b=vtile(kt), s=(first and qs % 2 == 0), t=(ki == n_k - 1):
                           e.matmul(o, a, b, start=s, stop=t, skip_group_check=True),
                           [Pb] + list(reads_v(kt)), [bank_buf[bk]], signal=(ki == n_k - 1))
                for qs in range(4):
                    bk = accb[qs // 2]
                    off = (qs % 2) * 129
                    col = smallcols.next()
                    rec = small[:, col:col + 1]
                    DVE(lambda e, o=rec, p=banks[bk][:, off + 128:off + 129]: e.reciprocal(out=o, in_=p), [bank_buf[bk]], [B_smallc[col]])
                    out_fn(qb, qs, banks[bk][:, off:off + 128], rec, bank_buf[bk], B_smallc[col])

        def transpose_to(On, B_On, dst_fn, B_dst, n=4):
            def go():
                bi = mma_ring.next()
                pb = banks[bi].bitcast(BF16)
                for q in range(n):
                    PE(lambda e, o=pb[:, q * 128:(q + 1) * 128], a=On[:, q, :]: e.transpose(o, a, ident_b), [B_On[q], B_const], [bank_buf[bi]], signal=(q == n - 1))
                ACT(lambda e, o=dst_fn(), p=pb[:, 0:n * 128]: e.copy(o, p), [bank_buf[bi]], [B_dst])
            deferred.append(go)

        def proj_fm(vw, rd, b, kcn=KC, rhs=None, B_rhs=None, ncol=512, pslice=None):
            bi = mma_ring.next()
            rh = rhs if rhs is not None else (lambda k: xnT[:, k, b * 512:(b + 1) * 512])
            Br = B_rhs if B_rhs is not None else B_xnT[b]
            out = banks[bi][:, 0:ncol] if pslice is None else banks[bi][pslice[0]:pslice[1], 0:ncol]
            mm_group(bi, out, [(vw(k), rh(k)) for k in range(kcn)], list(rd) + [Br])
            return bi

        def stage_mem(l):
            sc.fence()
            mt = Rf32(0, 2 * D).rearrange("p (t d) -> p t d", t=2)
            memn = R[:, 8192:12288].rearrange("p (t d) -> p t d", t=2)
            memnT = R[:, 12288:16384].rearrange("p (c t) -> p c t", c=KC)
            memg = Rf32(16384, D)
            mk_s = R[:, 20480:21504].rearrange("p (h t) -> p h t", h=4)
            mv_s = R[:, 21504:22536].rearrange("p (t h d) -> p t h d", t=2, h=4)
            junk = Rf32(22544, D)
            B_mt = [Buf("mt0"), Buf("mt1")]; B_memn = [Buf("memn0"), Buf("memn1")]; B_memnT = Buf("memnT"); B_memg = Buf("memg")
            B_mk = Buf("mk_s"); B_mv = Buf("mv_s"); B_junk = Buf("junk")
            DMA_SP(memg, memg_d[:, l, :], [], [B_memg], misc_dsem)
            DVE(lambda e: e.memset(mv_s[:, :, :, 128:129], 1.0), [], [B_mv])
            for t in range(2):
                DMA_SP(mt[:, t, :], mem_d[t * 128:(t + 1) * 128, :], [], [B_mt[t]], x_dsem[t])
                ACT(lambda e, t=t: e.activation(out=junk, in_=mt[:, t, :], func=AF.Square, accum_out=small[:, 34 + t:35 + t]), [B_mt[t]], [B_junk, B_small])
                ACT(lambda e, t=t: e.activation(out=small[:, 36 + t:37 + t], in_=small[:, 34 + t:35 + t], func=AF.Ln, bias=eps_t[:, 0:1], scale=1.0 / D), [B_small, B_const], [B_small])
                ACT(lambda e, t=t: e.activation(out=small[:, 36 + t:37 + t], in_=small[:, 36 + t:37 + t], func=AF.Exp, scale=-0.5), [B_small], [B_small])
                DVE(lambda e, t=t: e.scalar_tensor_tensor(out=memn[:, t, :], in0=mt[:, t, :], scalar=small[:, 36 + t:37 + t], in1=memg, op0=ALU.mult, op1=ALU.mult),
                    [B_mt[t], B_small, B_memg], [B_memn[t]])
            for c in range(KC):
                bi = mma_ring.next()
                pb = banks[bi].bitcast(BF16)
                for t in range(2):
                    PE(lambda e, o=pb[:, t * 128:(t + 1) * 128], a=memn[:, t, c * 128:(c + 1) * 128]: e.transpose(o, a, ident_b), [B_memn[t], B_const], [bank_buf[bi]], signal=(t == 1))
                ACT(lambda e, o=memnT[:, c, :], p=pb[:, 0:256]: e.copy(o, p), [bank_buf[bi]], [B_memnT])
            for hd in range(4):
                rd, (vk,) = load_w([(wview(w_ckv, l, hd * 128, 128), [128, KC, 128])])
                bi = mma_ring.next()
                mm_group(bi, banks[bi][:, 0:MEM], [(vk[:, k, :], memnT[:, k, :]) for k in range(KC)], list(rd) + [B_memnT])
                ACT(lambda e, o=mk_s[:, hd, :], p=banks[bi][:, 0:MEM]: e.copy(o, p), [bank_buf[bi]], [B_mk])
            rd, (vv,) = load_w([(wview(w_ckv, l, 512, 512), [128, KC, 512])])
            for t in range(2):
                bi = mma_ring.next()
                mm_group(bi, banks[bi][:], [(memnT[:, k, t * 128:(t + 1) * 128], vv[:, k, :]) for k in range(KC)], list(rd) + [B_memnT])
                ACT(lambda e, o=mv_s[:, t, :, 0:128], p=banks[bi][:].rearrange("p (h d) -> p h d", h=4): e.copy(o, p), [bank_buf[bi]], [B_mv])
            DMA_SP(mk_d[l], mk_s, [B_mk], [B_mkd[l]], misc_dsem)
            DMA_SP(mv_d[l], mv_s, [B_mv], [B_mvd[l]], misc_dsem)
            sc.fence()

        B_mkd = [Buf("mkd%d" % l) for l in range(L)]
        B_mvd = [Buf("mvd%d" % l) for l in range(L)]

        def stage_cross(j, l):
            sc.fence()
            stage_norm(j, 4 * l + 2)
            crossT = R[:, 0:4096].rearrange("p (c t) -> p c t", c=4)
            memK = R[:, 4096:5120].rearrange("p (h t) -> p h t", h=4)
            memV = R[:, 5120:6152].rearrange("p (t h d) -> p t h d", t=2, h=4)
            QT = R[:, 6160:7184]
            Pt = Ring([(R[:, 7184 + i * 512:7184 + (i + 1) * 512], Buf("P%d" % i)) for i in range(4)])
            On = R[:, 9232:9744].rearrange("p (q d) -> p q d", q=4)
            B_cT = Buf("crossT"); B_mK = Buf("memK"); B_mV = Buf("memV"); B_QT = Buf("QT"); B_On = [Buf("On%d" % q) for q in range(4)]
            DMA_SP(memK, mk_d[l], [B_mkd[l]], [B_mK], misc_dsem)
            DMA_SP(memV, mv_d[l], [B_mvd[l]], [B_mV], misc_dsem)
            for hd in range(4):
                rd, (vq,) = load_w([(wview(w_cq, l, hd * 128, 128), [128, KC, 128])])
                for b in range(NBLK):
                    bi = proj_fm(lambda k: vq[:, k, :], rd, b)
                    ACT(lambda e, o=QT[:, b * 512:(b + 1) * 512], p=banks[bi]: e.copy(o, p[:]), [bank_buf[bi]], [B_QT])

                def out_fn(qb, qs, acc, rec, bb, bs, hd=hd):
                    DVE(lambda e, o=On[:, qs, :], a=acc, r=rec: e.tensor_scalar(out=o, in0=a, scalar1=r, scalar2=None, op0=ALU.mult), [bb, bs], [B_On[qs]])
                    if qs == 3:
                        transpose_to(On, B_On, lambda: crossT[:, hd, qb * 512:(qb + 1) * 512], B_cT)

                attn(j, QT, memK[:, hd, :], lambda kt: memV[:, kt, hd, :], lambda qb: [0, 1], 128.0 ** -0.5, Pt, None,
                     [B_QT], lambda kt: [B_mK], lambda kt: [B_mV], out_fn, causal=False)
            flush_deferred()
            for mg in range(D // 256):
                rd, (vo,) = load_w([(w_co[l].rearrange("(c p) n -> p c n", p=128)[:, :, mg * 256:(mg + 1) * 256], [128, 4, 256])])
                for mm in range(2):
                    for b in range(NBLK):
                        bi = mma_ring.next()
                        mm_group(bi, banks[bi][:], [(vo[:, c, mm * 128:(mm + 1) * 128], crossT[:, c, b * 512:(b + 1) * 512]) for c in range(4)], list(rd) + [B_cT])
                        rmw_h(bi, banks[bi][:], mg * 2 + mm, j * NBLK + b, 1.0)

        B_fk = [[Buf("fkc") for _ in range(NFH)] for _ in range(L)]
        B_fv = [[Buf("fvc") for _ in range(NFH)] for _ in range(L)]
        B_dk = [[Buf("dkc") for _ in range(NDH)] for _ in range(L)]
        B_dv = [[Buf("dvc") for _ in range(NDH)] for _ in range(L)]
        B_fc = [Buf("fcc") for _ in range(L)]
        kv_dsem = [new_dsem() for _ in range(4)]

        def stage_mix(j, l):
            sc.fence()
            stage_norm(j, 4 * l + 1)
            mixT = R[:, 0:16384].rearrange("p (c t) -> p c t", c=KC)
            cosT = Rf32(16384, SBT); sinT = Rf32(18432, SBT); Ctb = Rf32(20480, SBT)
            QT = R[:, 22528:23552]
            KT = R[:, 23552:27648]
            V1 = R[:, 27648:31776].rearrange("p (t d) -> p t d", d=129)
            negc = Rf32(31776, 32 * NFH).rearrange("p (t h) -> p t h", h=NFH)
            cT = Rf32(32160, SBT); lf = Rf32(34208, SBT); onesr = Rf32(36256, SBT)
            lf_full = lf
            Pt = Ring([(R[:, 38304 + i * 512:38304 + (i + 1) * 512], Buf("P%d" % i)) for i in range(4)])
            On = R[:, 40352:40864].rearrange("p (q d) -> p q d", q=4)
            O1n = Rf32(40864, 1024).rearrange("p (q d) -> p q d", q=8)
            tmpA = Rf32(42912, SBT)
            QTz = R[:, 34208:36256].rearrange("p (c t) -> p c t", c=2)
            B_mixT = Buf("mixT"); B_cos = Buf("cos"); B_sin = Buf("sin"); B_Ctb = Buf("Ctb"); B_QT = Buf("QT")
            B_KTo = Buf("KTown"); B_KTp = Buf("KTprev"); B_Vo = Buf("Vown"); B_Vp = Buf("Vprev")
            B_nco = Buf("negc_own"); B_ncp = Buf("negc_prev"); B_cT = Buf("cT"); B_lf = Buf("lf"); B_ones = Buf("onesr")
            B_On = [Buf("On%d" % q) for q in range(4)]; B_O1 = [Buf("O1n%d" % q) for q in range(8)]; B_tmpA = Buf("tmpA")
            tok0 = j * SBT
            npt = j * NT
            nkt = lambda qb: list(range((tok0 + (qb + 1) * 512) // 128))
            rk = lambda kt: [B_KTp] if kt < npt else [B_KTo]
            rv = lambda kt: [B_Vp] if kt < npt else [B_Vo]

            posi = tmpA.bitcast(I32)
            DMA_SP(posi, pos_d[tok0:tok0 + SBT].partition_broadcast(128), [], [B_tmpA], misc_dsem)
            TWO_PI = float(2.0 * np.pi)
            PI = float(np.pi)
            posf = Ctb
            DVE(lambda e: e.tensor_copy(out=posf, in_=posi), [B_tmpA], [B_Ctb])

            def trig_table(dst, B_dst, shift):
                kf = tmpA
                ki = tmpA.bitcast(I32)
                DVE(lambda e: e.tensor_scalar(out=dst, in0=posf, scalar1=invf, scalar2=float(shift), op0=ALU.mult, op1=ALU.add), [B_Ctb, B_const], [B_dst])
                DVE(lambda e: e.tensor_scalar(out=ki, in0=dst, scalar1=float(1.0 / TWO_PI), scalar2=None, op0=ALU.mult), [B_dst], [B_tmpA])
                DVE(lambda e: e.tensor_copy(out=lf_full, in_=ki), [B_tmpA], [B_lf])
                DVE(lambda e: e.scalar_tensor_tensor(out=dst, in0=lf_full, scalar=-TWO_PI, in1=dst, op0=ALU.mult, op1=ALU.add), [B_lf, B_dst], [B_dst])
                DVE(lambda e: e.tensor_scalar(out=kf, in0=dst, scalar1=PI, scalar2=-TWO_PI, op0=ALU.is_gt, op1=ALU.mult), [B_dst], [B_tmpA])
                DVE(lambda e: e.tensor_tensor(out=dst, in0=dst, in1=kf, op=ALU.add), [B_dst, B_tmpA], [B_dst])
                DVE(lambda e: e.tensor_scalar(out=kf, in0=dst, scalar1=-PI, scalar2=TWO_PI, op0=ALU.is_lt, op1=ALU.mult), [B_dst], [B_tmpA])
                DVE(lambda e: e.tensor_tensor(out=dst, in0=dst, in1=kf, op=ALU.add), [B_dst, B_tmpA], [B_dst])
                DVE(lambda e: e.tensor_scalar(out=dst, in0=dst, scalar1=PI, scalar2=-PI, op0=ALU.min, op1=ALU.max), [B_dst], [B_dst])
                ACT(lambda e: e.activation(out=dst, in_=dst, func=AF.Sin), [B_dst], [B_dst])

            trig_table(sinT, B_sin, 0.0)
            trig_table(cosT, B_cos, PI / 2)
            DVE(lambda e: e.memset(V1[:, :, 128:129], 1.0), [], [B_Vo, B_Vp])
            DVE(lambda e: e.memset(onesr[0:NFH, :], 1.0), [], [B_ones])

            rd, (vf,) = load_w([(wview(w_in, l, OFF_FF, NFH), [128, KC, NFH])])
            for b in range(NBLK):
                bi = proj_fm(lambda k: vf[:, k, :], rd, b, pslice=(0, NFH))
                ACT(lambda e, o=lf[0:NFH, b * 512:(b + 1) * 512], p=banks[bi][0:NFH, :]: e.activation(out=o, in_=p, func=AF.Exp, bias=nfbias[:, l:l + 1], scale=-1.0),
                    [bank_buf[bi], B_const], [B_lf])
            ACT(lambda e: e.activation(out=lf[0:NFH, :], in_=lf[0:NFH, :], func=AF.Ln, bias=one_t[0:NFH, 0:1], scale=1.0), [B_lf, B_const], [B_lf])
            DVE(lambda e: e.tensor_tensor_scan(out=cT[0:NFH, :], data0=onesr[0:NFH, :], data1=lf[0:NFH, :], initial=ccarry[:, l:l + 1], op0=ALU.mult, op1=ALU.subtract),
                [B_ones, B_lf, B_carry], [B_cT])
            DVE(lambda e: e.tensor_copy(out=ccarry[:, l:l + 1], in_=cT[0:NFH, SBT - 1:SBT]), [B_cT], [B_carry])
            bi = mma_ring.next()
            for t in range(NT):
                PE(lambda e, o=banks[bi][:, t * NFH:(t + 1) * NFH], a=cT[0:NFH, t * 128:(t + 1) * 128]: e.transpose(o, a, ident_f[0:NFH, 0:NFH]),
                   [B_cT, B_const], [bank_buf[bi]], signal=(t == NT - 1))
            ACT(lambda e, o=negc[:, npt:npt + NT, :], p=banks[bi][:, 0:NT * NFH].rearrange("p (t h) -> p t h", h=NFH): e.mul(o, p, -1.0), [bank_buf[bi]], [B_nco])
            if j < nsb - 1:
                DMA_SP(fc_d[l, :, npt:npt + NT, :], negc[:, npt:npt + NT, :], [B_nco], [B_fc[l]], kv_dsem[3])
            if j > 0:
                DMA_SP(negc[:, 0:npt, :], fc_d[l, :, 0:npt, :], [B_fc[l]], [B_ncp], kv_dsem[3])

            def kv_proj(vk, vv, rd, kc_d, vc_d, B_kc, B_vc, rope):
                for b in range(NBLK):
                    bi = proj_fm(lambda k: vk[:, k, :], [rd[1]], b)
                    dst = KT[:, tok0 + b * 512:tok0 + (b + 1) * 512]
                    if rope:
                        rope_apply(bi, dst, B_KTo, b)
                    else:
                        ACT(lambda e, o=dst, p=banks[bi]: e.copy(o, p[:]), [bank_buf[bi]], [B_KTo])
                for tq in range(NT // 4):
                    bi = mma_ring.next()
                    for t4 in range(4):
                        t = tq * 4 + t4
                        mm_group(bi, banks[bi][:, t4 * 128:(t4 + 1) * 128], [(xnT[:, k, t * 128:(t + 1) * 128], vv[:, k, :]) for k in range(KC)],
                                 [rd[2], B_xnT[t // 4]])
                    ACT(lambda e, o=V1[:, npt + tq * 4:npt + tq * 4 + 4, 0:128], p=banks[bi][:].rearrange("p (t d) -> p t d", t=4): e.copy(o, p), [bank_buf[bi]], [B_Vo])
                if j < nsb - 1:
                    DMA_SP(kc_d[:, tok0:tok0 + SBT], KT[:, tok0:tok0 + SBT], [B_KTo], [B_kc], kv_dsem[0])
                    DMA_SP(vc_d[:, npt:npt + NT, :], V1[:, npt:npt + NT, :], [B_Vo], [B_vc], kv_dsem[1])
                if j > 0:
                    DMA_SP(KT[:, 0:tok0], kc_d[:, 0:tok0], [B_kc], [B_KTp], kv_dsem[0])
                    DMA_SP(V1[:, 0:npt, :], vc_d[:, 0:npt, :], [B_vc], [B_Vp], kv_dsem[1])

            def rope_apply(bi, dst, B_dst, b):
                i, wt, wb = get_wt()
                ACT(lambda e, o=wt, p=banks[bi]: e.copy(o[:], p[:]), [bank_buf[bi]], [wb])
                b2 = mma_ring.next()
                PE(lambda e, o=banks[b2], w=wt: e.matmul(o[:], perm_f, w[:], start=True, stop=True), [wb, B_const], [bank_buf[b2]], signal=True)
                i2, wt2, wb2 = get_wt()
                DVE(lambda e, o=wt2, p=banks[b2], s=sinT[:, b * 512:(b + 1) * 512]: e.tensor_tensor(out=o[:], in0=p[:], in1=s, op=ALU.mult), [bank_buf[b2], B_sin], [wb2])
                DVE(lambda e, o=wt, c=cosT[:, b * 512:(b + 1) * 512]: e.tensor_tensor(out=o[:], in0=o[:], in1=c, op=ALU.mult), [wb, B_cos], [wb])
                if dst is None:
                    for comp in range(2):
                        p0, p1 = comp * 64, comp * 64 + 64
                        DVE(lambda e, o=QTz[p0:p1, comp, b * 512:(b + 1) * 512], a=wt, c=wt2, p0=p0, p1=p1: e.tensor_tensor(out=o, in0=a[p0:p1, :], in1=c[p0:p1, :], op=ALU.add),
                            [wb, wb2], [B_dst])
                else:
                    DVE(lambda e, o=dst, a=wt, c=wt2: e.tensor_tensor(out=o, in0=a[:], in1=c[:], op=ALU.add), [wb, wb2], [B_dst])

            for hd in range(NFH):
                rd, (vq, vk, vv) = load_w([(wview(w_in, l, OFF_FQ + hd * 128, 128), [128, KC, 128]),
                                           (wview(w_in, l, OFF_FK + hd * 128, 128), [128, KC, 128]),
                                           (wview(w_in, l, OFF_FV + hd * 128, 128), [128, KC, 128])])
                for b in range(NBLK):
                    bi = proj_fm(lambda k: vq[:, k, :], [rd[0]], b)
                    ACT(lambda e, o=QT[:, b * 512:(b + 1) * 512], p=banks[bi]: e.copy(o, p[:]), [bank_buf[bi]], [B_QT])
                kv_proj(vk, vv, rd, fk_d[l, hd], fv_d[l, hd], B_fk[l][hd], B_fv[l][hd], rope=False)
                for b in range(NBLK):
                    bi = mma_ring.next()
                    PE(lambda e, o=banks[bi], a=selt[0:NFH, hd * 128:(hd + 1) * 128], r=cT[0:NFH, b * 512:(b + 1) * 512]: e.matmul(o[:], a, r, start=True, stop=True),
                       [B_cT, B_const], [bank_buf[bi]], signal=True)
                    ACT(lambda e, o=Ctb[:, b * 512:(b + 1) * 512], p=banks[bi]: e.copy(o, p[:]), [bank_buf[bi]], [B_Ctb])

                def out_fn(qb, qs, acc, rec, bb, bs, hd=hd):
                    DVE(lambda e, o=On[:, qs, :], a=acc, r=rec: e.tensor_scalar(out=o, in0=a, scalar1=r, scalar2=None, op0=ALU.mult), [bb, bs], [B_On[qs]])
                    if qs == 3:
                        transpose_to(On, B_On, lambda: mixT[:, hd, qb * 512:(qb + 1) * 512], B_mixT)

                attn(j, QT, KT, lambda kt: V1[:, kt, :], nkt, 128.0 ** -0.5, Pt, None, [B_QT], rk, rv, out_fn,
                     Ctb=Ctb, B_Ctb=B_Ctb, negc=lambda kt, hd=hd: negc[:, kt, hd:hd + 1], B_negc=lambda kt: (B_ncp if kt < npt else B_nco))

            DVE(lambda e: e.memset(QTz[64:128, 0, :], 0.0), [], [B_lf])
            DVE(lambda e: e.memset(QTz[0:64, 1, :], 0.0), [], [B_lf])
            for hd in range(NDH):
                rd, (vq, vk, vv) = load_w([(wview(w_in, l, OFF_DQ + hd * 128, 128), [128, KC, 128]),
                                           (wview(w_in, l, OFF_DK + hd * 128, 128), [128, KC, 128]),
                                           (wview(w_in, l, OFF_DV + hd * 128, 128), [128, KC, 128])])
                for b in range(NBLK):
                    bi = proj_fm(lambda k: vq[:, k, :], [rd[0]], b)
                    rope_apply(bi, None, B_lf, b)
                kv_proj(vk, vv, rd, dk_d[l, hd], dv_d[l, hd], B_dk[l][hd], B_dv[l][hd], rope=True)
                for comp in range(2):
                    p0, p1 = comp * 64, comp * 64 + 64

                    def out_fn(qb, qs, acc, rec, bb, bs, hd=hd, comp=comp):
                        if comp == 0:
                            DVE(lambda e, o=O1n[:, qb * 4 + qs, :], a=acc, r=rec: e.tensor_scalar(out=o, in0=a, scalar1=r, scalar2=None, op0=ALU.mult), [bb, bs], [B_O1[qb * 4 + qs]])
                            return
                        i, wt, wb = get_wt()
                        a_ap = wt[:, 0:128]
                        DVE(lambda e, o=a_ap, a=acc, r=rec: e.tensor_scalar(out=o, in0=a, scalar1=r, scalar2=None, op0=ALU.mult), [bb, bs], [wb])
                        DVE(lambda e, o=a_ap, o1=O1n[:, qb * 4 + qs, :]: e.scalar_tensor_tensor(out=o, in0=o, scalar=nlamt[:, l:l + 1], in1=o1, op0=ALU.mult, op1=ALU.add),
                            [wb, B_O1[qb * 4 + qs], B_const], [wb])
                        c1 = smallcols.next(); c2 = smallcols.next()
                        ACT(lambda e, a=a_ap, j_=wt[:, 128:256], s=small[:, c1:c1 + 1]: e.activation(out=j_, in_=a, func=AF.Square, accum_out=s), [wb], [wb, B_smallc[c1]])
                        ACT(lambda e, s=small[:, c1:c1 + 1], r=small[:, c2:c2 + 1]: e.activation(out=r, in_=s, func=AF.Ln, bias=eps_t[:, 0:1], scale=1.0 / 128), [B_smallc[c1], B_const], [B_smallc[c2]])
                        ACT(lambda e, r=small[:, c2:c2 + 1]: e.activation(out=r, in_=r, func=AF.Exp, scale=-0.5), [B_smallc[c2]], [B_smallc[c2]])
                        DVE(lambda e, o=On[:, qs, :], a=a_ap, r=small[:, c2:c2 + 1]: e.scalar_tensor_tensor(out=o, in0=a, scalar=r, in1=subln[:, l, :], op0=ALU.mult, op1=ALU.mult),
                            [wb, B_smallc[c2], B_const], [B_On[qs]])
                        if qs == 3:
                            transpose_to(On, B_On, lambda: mixT[:, NFH + hd, qb * 512:(qb + 1) * 512], B_mixT)

                    attn(j, QTz[:, comp, :], KT, lambda kt: V1[:, kt, :], nkt, 64.0 ** -0.5, Pt, None, [B_lf], rk, rv, out_fn)

            flush_deferred()
            sc.fence()
            u = Rf32(22528, SBT + 2); gbS = Rf32(24580, SBT); tconv = Rf32(26628, SBT)
            B_u = Buf("u"); B_gbS = Buf("gbS"); B_tc = Buf("tconv")
            for ch in range(4):
                rd, (vgb, vgc, vhc) = load_w([(wview(w_in, l, OFF_GB + ch * 128, 128), [128, KC, 128]),
                                              (wview(w_in, l, OFF_GC + ch * 128, 128), [128, KC, 128]),
                                              (wview(w_in, l, OFF_HC + ch * 128, 128), [128, KC, 128])])
                ACT(lambda e, ch=ch: e.copy(u[:, 0:2], halo[:, l, ch, :]), [B_halo], [B_u])
                for b in range(NBLK):
                    bi = proj_fm(lambda k: vgb[:, k, :], [rd[0]], b)
                    ACT(lambda e, o=gbS[:, b * 512:(b + 1) * 512], p=banks[bi]: e.copy(o, p[:]), [bank_buf[bi]], [B_gbS])
                    bi = proj_fm(lambda k: vgc[:, k, :], [rd[1]], b)
                    i, wt, wb = get_wt()
                    ACT(lambda e, o=wt, p=banks[bi]: e.copy(o[:], p[:]), [bank_buf[bi]], [wb])
                    bi = proj_fm(lambda k: vhc[:, k, :], [rd[2]], b)
                    DVE(lambda e, o=u[:, 2 + b * 512:2 + (b + 1) * 512], w=wt, p=banks[bi]: e.tensor_tensor(out=o, in0=w[:], in1=p[:], op=ALU.mult), [wb, bank_buf[bi]], [B_u])
                cw = convw[:, l, ch, :]
                DVE(lambda e, cw=cw: e.tensor_scalar(out=tconv, in0=u[:, 2:2 + SBT], scalar1=cw[:, 2:3], scalar2=cw[:, 3:4], op0=ALU.mult, op1=ALU.add), [B_u, B_const], [B_tc])
                DVE(lambda e, cw=cw: e.scalar_tensor_tensor(out=tconv, in0=u[:, 1:1 + SBT], scalar=cw[:, 1:2], in1=tconv, op0=ALU.mult, op1=ALU.add), [B_u, B_tc, B_const], [B_tc])
                DVE(lambda e, cw=cw: e.scalar_tensor_tensor(out=tconv, in0=u[:, 0:SBT], scalar=cw[:, 0:1], in1=tconv, op0=ALU.mult, op1=ALU.add), [B_u, B_tc, B_const], [B_tc])
                DVE(lambda e, ch=ch: e.tensor_tensor(out=mixT[:, 2 * NFH + ch, :], in0=tconv, in1=gbS, op=ALU.mult), [B_tc, B_gbS], [B_mixT])
                ACT(lambda e, ch=ch: e.copy(halo[:, l, ch, :], u[:, SBT:SBT + 2]), [B_u], [B_halo])

            for mg in range(D // 256):
                rd, (vo,) = load_w([(wview(w_out, l, mg * 256, 256), [128, KC, 256])])
                for mm in range(2):
                    for b in range(NBLK):
                        bi = mma_ring.next()
                        mm_group(bi, banks[bi][:], [(vo[:, c, mm * 128:(mm + 1) * 128], mixT[:, c, b * 512:(b + 1) * 512]) for c in range(KC)], list(rd) + [B_mixT])
                        rmw_h(bi, banks[bi][:], mg * 2 + mm, j * NBLK + b, 1.0)

        B_carry = Buf("ccarry")
        B_halo = Buf("halo")
        one_t = small[:, 33:34]
        final_evs = []
        eps_t = small[:, 32:33]
        B_const.const = False
        DVE(lambda e: e.memset(small[:, 32:33], EPS), [], [B_const])
        DVE(lambda e: e.memset(small[:, 33:34], 1.0), [], [B_const])
        for l in range(L):
            DVE(lambda e, l=l: e.tensor_scalar(out=subln[:, l, :], in0=subln[:, l, :], scalar1=laminit[:, l:l + 1], scalar2=None, op0=ALU.mult), [B_const], [B_const])
        sc.fence(("pe", "act", "dve", "sp", "pool"))
        B_const.const = True

        if STAGES["cross"]:
            for l in range(L):
                stage_mem(l)
        for j in range(nsb):
            stage_init(j)
            for l in range(L):
                if STAGES["ffn1"]:
                    stage_ffn(j, l, w_g1, w_u1, w_d1, 4 * l + 0)
                if STAGES["mix"]:
                    stage_mix(j, l)
                if STAGES["cross"]:
                    stage_cross(j, l)
                if STAGES["ffn2"]:
                    stage_ffn(j, l, w_g2, w_u2, w_d2, 4 * l + 3)
            stage_final(j)

        for ev in final_evs:
            sc._wait("sp", ev)
        sc.fence(("pe", "act", "dve", "sp", "pool"))

        engmap = {"pe": "tensor", "act": "scalar", "dve": "vector", "pool": "gpsimd", "sp": "sync"}
        with nc.Block() as block:
            for en in Sched.ENGS:
                prog = sc.q[en]

                def body(e, prog=prog):
                    for it in prog:
                        if it[0] == "w":
                            e.wait_ge(it[1], it[2])
                        else:
                            ins = it[1](e)
                            if it[2] is not None:
                                ins.then_inc(it[2], it[3])

                getattr(block, engmap[en])(body)
        stats = {e: len(sc.q[e]) for e in Sched.ENGS}
    return nc, stats


STAGES = {"ffn1": True, "mix": True, "cross": True, "ffn2": True}


def host_consts(depth):
    L = depth
    cf = np.zeros((128, 3 * 128 + 3), np.float32)
    cf[:, 0:128] = np.eye(128, dtype=np.float32)
    cf[:, 128:256] = 1.0
    perm = np.zeros((128, 128), np.float32)
    invf = np.zeros((128,), np.float32)
    inv_freq = ROPE_THETA ** (-np.arange(0, 16, 2, dtype=np.float32) / 16.0)
    for comp in range(2):
        base = comp * 64
        for i in range(8):
            m1, m2 = base + i, base + 8 + i
            perm[m2, m1] = 1.0
            perm[m1, m2] = 1.0
            invf[m1] = -inv_freq[i]
            invf[m2] = inv_freq[i]
    cf[:, 256:384] = perm
    cf[:, 384] = invf
    cf[:, 385] = -np.pi
    cbf = np.eye(128, dtype=np.float32).astype(ml_dtypes.bfloat16)
    sel = np.zeros((NFH, NFH * 128), np.float32)
    for h in range(NFH):
        sel[h, h * 128:(h + 1) * 128] = 1.0
    mask = np.zeros((128, 4, 512), np.float32)
    p = np.arange(128)[:, None]
    jj = np.arange(512)[None, :]
    for v in range(4):
        mask[:, v, :] = np.where(jj - p >= 128 * v, 0.0, -1e30)
    laminit = np.zeros((128, L), np.float32)
    for l in range(L):
        laminit[:, l] = 0.8 - 0.6 * np.exp(-0.3 * l)
    return cf, cbf, sel, mask, laminit


def make_in_maps(inputs, depth, s_len, batches):
    L = depth
    f = lambda a: np.ascontiguousarray(np.asarray(a))
    cf, cbf, sel, mask, laminit = host_consts(L)
    g = np.zeros((4 * L + 1, D), np.float32)
    for l in range(L):
        g[4 * l + 0] = inputs["ffn1_norm"][l]
        g[4 * l + 1] = inputs["mix_norm"][l]
        g[4 * l + 2] = inputs["cross_norm"][l]
        g[4 * l + 3] = inputs["ffn2_norm"][l]
    g[4 * L] = inputs["final_norm"]
    gains = f(g.reshape(4 * L + 1, KC, 128).transpose(2, 0, 1))
    memg = f(np.broadcast_to(np.asarray(inputs["mem_norm"])[None, :L, :], (128, L, D)))
    fb = f(np.asarray(inputs["forget_bias"])[:L].T)
    cw = np.zeros((128, L, 4, 4), np.float32)
    for l in range(L):
        for ch in range(4):
            cw[:, l, ch, 0:3] = np.asarray(inputs["conv_w"])[l, :, ch * 128:(ch + 1) * 128].T
            cw[:, l, ch, 3] = np.asarray(inputs["conv_b"])[l, ch * 128:(ch + 1) * 128]
    lam = np.stack([np.asarray(inputs[k])[:L] for k in ("lambda_q1", "lambda_k1", "lambda_q2", "lambda_k2")], axis=1)
    lam = f(np.broadcast_to(lam[None], (128, L, 4, 64)))
    subln = f(np.broadcast_to(np.asarray(inputs["diff_subln"])[None, :L, :], (128, L, 128)))
    shared = {
        "gains": gains, "mem_gain_b": memg, "forget_bias_t": fb, "conv_wb": cw, "lam_b": lam, "subln_b": subln,
        "const_f32": cf, "const_bf16": cbf, "const_sel": sel, "const_mask": mask, "const_laminit": laminit,
    }
    for k in ("ffn1_w_gate", "ffn1_w_up", "ffn1_w_down", "ffn2_w_gate", "ffn2_w_up", "ffn2_w_down",
              "mix_w_in", "mix_w_out", "cross_w_q", "cross_w_kv", "cross_w_o"):
        shared[k] = f(np.asarray(inputs[k])[:L])
    maps = []
    for b in batches:
        m = dict(shared)
        m["x"] = f(np.asarray(inputs["x"])[b, :s_len])
        m["mem"] = f(np.asarray(inputs["mem"])[b])
        m["positions"] = f(np.asarray(inputs["positions"])[b, :s_len].astype(np.int32))
        maps.append(m)
    return maps


_CACHE = {}


def run(inputs, depth=4, nsb=4, batches=(0, 1, 2, 3), trace=False):
    s_len = nsb * SBT
    key = (depth, nsb, tuple(sorted(STAGES.items())))
    if key not in _CACHE:
        _CACHE[key] = build_program(depth, nsb, s_len)
    nc, stats = _CACHE[key]
    maps = make_in_maps(inputs, depth, s_len, batches)
    res = run_bass_kernel_spmd(nc, maps, core_ids=list(range(len(batches))), trace=trace)
    outs = np.stack([r["out"] for r in res.results], axis=0)
    return outs, res, stats


def kernel(**inputs):
    outs, _, _ = run(inputs, depth=4, nsb=4, batches=(0, 1, 2, 3))
    return outs.astype(np.float32)
```

```python
import numpy as np
import ml_dtypes
from contextlib import ExitStack

import concourse.bass as bass
import concourse.mybir as mybir
from concourse.bass_utils import run_bass_kernel_spmd

F32 = mybir.dt.float32
BF16 = mybir.dt.bfloat16
I32 = mybir.dt.int32
AF = mybir.ActivationFunctionType
ALU = mybir.AluOpType

D = 2048
KC = 16
DFF = 5632
FC = 44
S_FULL = 4096
SBT = 1024
NBLK = SBT // 512
NT = SBT // 128
MEM = 256
NFH = 6
NDH = 6
IN_W = 6150
OFF_FQ, OFF_FK, OFF_FV, OFF_FF = 0, 768, 1536, 2304
OFF_DQ, OFF_DK, OFF_DV = 2310, 3078, 3846
OFF_GB, OFF_GC, OFF_HC = 4614, 5126, 5638
EPS = 1e-6
ROPE_THETA = 500000.0
N_CORES = 4


class Ev:
    __slots__ = ("eng", "sem", "val", "idx")

    def __init__(self, eng, sem=None, val=0, idx=0):
        self.eng, self.sem, self.val, self.idx = eng, sem, val, idx


class Buf:
    __slots__ = ("name", "w", "r", "const")

    def __init__(self, name):
        self.name, self.w, self.r, self.const = name, None, {}, False


class DmaSem:
    __slots__ = ("h", "count", "key")

    def __init__(self, h, key):
        self.h, self.count, self.key = h, 0, key


class Sched:
    ENGS = ("pe", "act", "dve", "pool", "sp")

    def __init__(self, esems):
        self.esem = esems
        self.q = {e: [] for e in self.ENGS}
        self.cnt = {e: 0 for e in self.ENGS}
        self.seen = {e: {} for e in self.ENGS}
        self.pe_idx = 0
        self.pe_sig = []
        self.ndma = 0
        self.dma_active = {}

    def _resolve(self, ev):
        if ev.eng == "pe" and ev.sem is None:
            lo, hi = 0, len(self.pe_sig)
            while lo < hi:
                mid = (lo + hi) // 2
                if self.pe_sig[mid][0] >= ev.idx:
                    hi = mid
                else:
                    lo = mid + 1
            assert lo < len(self.pe_sig), "PE event not yet signalled"
            return "pe", self.esem["pe"], self.pe_sig[lo][1]
        return ev.sem[0], ev.sem[1], ev.val

    def _wait(self, eng, ev):
        key, sem, val = self._resolve(ev)
        if self.seen[eng].get(key, 0) >= val:
            return
        self.seen[eng][key] = val
        self.q[eng].append(("w", sem, val))

    def op(self, eng, fn, reads=(), writes=(), signal=True, dma=None):
        deps = []
        for b in reads:
            if b.w is not None:
                deps.append(b.w)
        for b in writes:
            if b.w is not None:
                deps.append(b.w)
            deps.extend(b.r.values())
        is_dma = dma is not None
        for ev in deps:
            if (not is_dma) and eng == "pe" and ev.eng == "pe":
                continue
            self._wait(eng, ev)
        if is_dma:
            dma.count += 16
            ev = Ev("dma", (dma.key, dma.h), dma.count)
            self.q[eng].append(("i", fn, dma.h, 16))
            self.ndma += 1
            if eng != "pool":
                self.dma_active[dma.key] = ev
            rkey = ("dma", dma.key)
        elif eng == "pe":
            self.pe_idx += 1
            if signal:
                self.cnt["pe"] += 1
                self.pe_sig.append((self.pe_idx, self.cnt["pe"]))
                self.q[eng].append(("i", fn, self.esem["pe"], 1))
            else:
                self.q[eng].append(("i", fn, None, 0))
            ev = Ev("pe", None, 0, self.pe_idx)
            rkey = "pe"
        else:
            self.cnt[eng] += 1
            ev = Ev(eng, (eng, self.esem[eng]), self.cnt[eng])
            self.q[eng].append(("i", fn, self.esem[eng], 1))
            rkey = eng
        for b in reads:
            if not b.const:
                b.r[rkey] = ev
        for b in writes:
            b.w = ev
            b.r = {}
        return ev

    def fence(self, engs=("pe", "act", "dve", "sp")):
        last = {}
        for e in engs:
            if e == "pe":
                if self.pe_sig:
                    last[e] = Ev("pe", ("pe", self.esem["pe"]), self.pe_sig[-1][1])
            elif self.cnt[e] > 0:
                last[e] = Ev(e, (e, self.esem[e]), self.cnt[e])
        for e in engs:
            for o, ev in last.items():
                if o != e:
                    self._wait(e, ev)
            for ev in self.dma_active.values():
                self._wait(e, ev)
        self.dma_active = {}


def build_program(depth, nsb, s_len):
    nc = bass.Bass("TRN2", target_bir_lowering=False)
    L = depth

    def din(name, shape, dt=F32):
        return nc.dram_tensor(name, list(shape), dt, kind="ExternalInput").ap()

    def dscr(name, shape, dt):
        return nc.dram_tensor(name, list(shape), dt, kind="Internal").ap()

    x_d = din("x", [s_len, D])
    mem_d = din("mem", [MEM, D])
    pos_d = din("positions", [s_len], I32)
    w_g1 = din("ffn1_w_gate", [L, D, DFF]); w_u1 = din("ffn1_w_up", [L, D, DFF]); w_d1 = din("ffn1_w_down", [L, DFF, D])
    w_g2 = din("ffn2_w_gate", [L, D, DFF]); w_u2 = din("ffn2_w_up", [L, D, DFF]); w_d2 = din("ffn2_w_down", [L, DFF, D])
    w_in = din("mix_w_in", [L, D, IN_W]); w_out = din("mix_w_out", [L, D, D])
    w_cq = din("cross_w_q", [L, D, 512]); w_ckv = din("cross_w_kv", [L, D, 1024]); w_co = din("cross_w_o", [L, 512, D])
    gains_d = din("gains", [128, 4 * L + 1, KC])
    memg_d = din("mem_gain_b", [128, L, D])
    fbias_d = din("forget_bias_t", [NFH, L])
    convw_d = din("conv_wb", [128, L, 4, 4])
    lam_d = din("lam_b", [128, L, 4, 64])
    subln_d = din("subln_b", [128, L, 128])
    cf32_d = din("const_f32", [128, 3 * 128 + 1 + 2])
    cbf_d = din("const_bf16", [128, 128], BF16)
    sel_d = din("const_sel", [NFH, NFH * 128])
    mask_d = din("const_mask", [128, 4, 512])
    laminit_d = din("const_laminit", [128, L])
    out_d = nc.dram_tensor("out", [s_len, D], F32, kind="ExternalOutput").ap()

    hT_d = dscr("hT", [KC, 128, s_len], F32)
    fk_d = dscr("fk_cache", [L, NFH, 128, s_len], BF16)
    fv_d = dscr("fv_cache", [L, NFH, 128, s_len // 128, 129], BF16)
    fc_d = dscr("fc_cache", [L, 128, s_len // 128, NFH], F32)
    dk_d = dscr("dk_cache", [L, NDH, 128, s_len], BF16)
    dv_d = dscr("dv_cache", [L, NDH, 128, s_len // 128, 129], BF16)
    mk_d = dscr("memk", [L, 128, 4, MEM], BF16)
    mv_d = dscr("memv", [L, 128, 2, 4, 129], BF16)

    es = ExitStack()
    with es:
        def sb_t(name, shape, dt):
            return es.enter_context(nc.sbuf_tensor("s_" + name, list(shape), dt))

        esems = {e: es.enter_context(nc.semaphore("sem_" + e)) for e in Sched.ENGS}
        sc = Sched(esems)
        dma_sem_n = [0]

        def new_dsem():
            dma_sem_n[0] += 1
            h = es.enter_context(nc.semaphore("dsem%d" % dma_sem_n[0]))
            return DmaSem(h, "d%d" % dma_sem_n[0])

        xnT = sb_t("xnT", [128, KC, SBT], BF16)
        R = sb_t("R", [128, 45056], BF16)
        wsl = [sb_t("wsl%d" % i, [128, 11264], BF16) for i in range(2)]
        NWT = 8
        wts = [sb_t("wt%d" % i, [128, 512], F32) for i in range(NWT)]
        gains = sb_t("gains", [128, 4 * L + 1, KC], F32)
        cf32 = sb_t("cf32", [128, 3 * 128 + 3], F32)
        cbf = sb_t("cbf", [128, 128], BF16)
        selt = sb_t("selt", [NFH, NFH * 128], F32)
        maskt = sb_t("maskt", [128, 4, 512], F32)
        fbias = sb_t("fbias", [NFH, L], F32)
        nfbias = sb_t("nfbias", [NFH, L], F32)
        convw = sb_t("convw", [128, L, 4, 4], F32)
        subln = sb_t("subln", [128, L, 128], F32)
        lamt = sb_t("lamt", [128, L], F32)
        nlamt = sb_t("nlamt", [128, L], F32)
        laminit = sb_t("laminit", [128, L], F32)
        ccarry = sb_t("ccarry", [NFH, L], F32)
        halo = sb_t("halo", [128, L, 4, 2], F32)
        small = sb_t("small", [128, 64], F32)

        ident_f = cf32[:, 0:128]
        ones_f = cf32[:, 128:256]
        perm_f = cf32[:, 256:384]
        invf = cf32[:, 384:385]
        negpi = cf32[:, 385:386]
        ident_b = cbf[:, :]

        banks = [es.enter_context(nc.psum_tensor("bank%d" % i, [128, 512], F32)) for i in range(8)]
        bank_buf = [Buf("bank%d" % i) for i in range(8)]

        class Ring:
            def __init__(self, items):
                self.items, self.i = items, 0

            def next(self):
                it = self.items[self.i % len(self.items)]
                self.i += 1
                return it

        mm_ring = Ring([0, 1, 2, 3])
        s_ring = Ring([4, 5])
        rstd_t = [sb_t("rstd%d" % i, [128, 512], F32) for i in range(2)]
        rstd_b = [Buf("rstd%d" % i) for i in range(2)]
        rstd_i = [0]
        x_dsem = [new_dsem() for _ in range(NT)]
        wt_buf = [Buf("wt%d" % i) for i in range(NWT)]
        wt_ring = Ring(list(range(NWT)))
        wt_dsem = [new_dsem() for _ in range(NWT)]
        ws_buf = [[Buf("ws%d_%d" % (i, p)) for p in range(3)] for i in range(2)]
        ws_dsem = [[new_dsem() for p in range(3)] for i in range(2)]
        ws_ring = Ring([0, 1])

        B_xnT = [Buf("xnT%d" % b) for b in range(NBLK)]
        B_const = Buf("const"); B_const.const = True
        B_R = Buf("Rmisc")
        misc_dsem = new_dsem()
        hT_buf = [[Buf("hT%d_%d" % (c, b)) for b in range(s_len // 512)] for c in range(KC)]
        st_dsem = [new_dsem() for _ in range(4)]
        st_ring = Ring([0, 1, 2, 3])

        def PE(fn, reads, writes, signal):
            return sc.op("pe", fn, reads, writes, signal=signal)

        def ACT(fn, reads, writes):
            return sc.op("act", fn, reads, writes)

        def DVE(fn, reads, writes):
            return sc.op("dve", fn, reads, writes)

        def POOLC(fn, reads, writes):
            return sc.op("pool", fn, reads, writes)

        def DMA_SP(out, in_, reads, writes, dsem):
            return sc.op("sp", lambda e, o=out, i=in_: e.dma_start(out=o, in_=i), reads, writes, dma=dsem)

        def DMA_POOL(out, in_, reads, writes, dsem):
            return sc.op("pool", lambda e, o=out, i=in_: e.dma_start(out=o, in_=i), reads, writes, dma=dsem)

        def mm_group(bank_i, out_ap, pairs, reads, sig_all=False):
            n = len(pairs)
            for k, pr in enumerate(pairs):
                lt, rh = pr[0], pr[1]
                rds = list(reads) + (list(pr[2]) if len(pr) > 2 else [])
                PE(lambda e, o=out_ap, a=lt, b=rh, s=(k == 0), t=(k == n - 1): e.matmul(o, a, b, start=s, stop=t),
                   rds, [bank_buf[bank_i]], signal=(sig_all or k == n - 1))

        def load_w(parts):
            si = ws_ring.next()
            views = []
            for p, (src, shape) in enumerate(parts):
                n = 1
                for d_ in shape[1:]:
                    n *= d_
                v = wsl[si][:, p * 4096:p * 4096 + n] if len(parts) > 1 else wsl[si][:, 0:n]
                if len(shape) == 3:
                    v = v.rearrange("p (a b) -> p a b", a=shape[1])
                wr = [ws_buf[si][p]] if len(parts) > 1 else ws_buf[si]
                DMA_POOL(v, src, [], wr, ws_dsem[si][p])
                views.append(v)
            rd = ws_buf[si][:len(parts)] if len(parts) > 1 else ws_buf[si]
            return rd, views

        def wview(w_ap, l, c0, ncols):
            return w_ap[l].rearrange("(kc p) n -> p kc n", p=128)[:, :, c0:c0 + ncols]

        def get_wt():
            i = wt_ring.next()
            return i, wts[i], wt_buf[i]

        def load_const(dst, src):
            DMA_SP(dst, src, [], [B_const], misc_dsem)

        B_const.const = False
        load_const(gains[:], gains_d)
        load_const(cf32[:], cf32_d)
        load_const(cbf[:], cbf_d)
        load_const(selt[:], sel_d)
        load_const(maskt[:], mask_d)
        load_const(fbias[:], fbias_d)
        load_const(convw[:], convw_d)
        load_const(subln[:], subln_d)
        load_const(laminit[:], laminit_d)
        lamv = R[:, 0:L * 4 * 64 * 2].bitcast(F32).rearrange("p (l f d) -> p l f d", l=L, f=4)
        DMA_SP(lamv, lam_d, [], [B_R], misc_dsem)
        DVE(lambda e: e.memset(ccarry[:], 0.0), [], [B_const])
        DVE(lambda e: e.memset(halo[:], 0.0), [], [B_const])
        ACT(lambda e: e.mul(nfbias[:], fbias[:], -1.0), [B_const], [B_const])
        for l in range(L):
            s1 = small[:, 0:1]; s2 = small[:, 1:2]; junk = small[:, 8:8 + 0]
            pr = wts[0][:, 0:64]
            DVE(lambda e, l=l: e.tensor_tensor(out=wts[0][:, 0:64], in0=lamv[:, l, 0, :], in1=lamv[:, l, 1, :], op=ALU.mult), [B_R], [wt_buf[0]])
            DVE(lambda e: e.reduce_sum(out=small[:, 0:1], in_=wts[0][:, 0:64], axis=mybir.AxisListType.X), [wt_buf[0]], [B_const])
            DVE(lambda e, l=l: e.tensor_tensor(out=wts[0][:, 64:128], in0=lamv[:, l, 2, :], in1=lamv[:, l, 3, :], op=ALU.mult), [B_R], [wt_buf[0]])
            DVE(lambda e: e.reduce_sum(out=small[:, 1:2], in_=wts[0][:, 64:128], axis=mybir.AxisListType.X), [wt_buf[0]], [B_const])
            ACT(lambda e: e.activation(out=small[:, 2:4], in_=small[:, 0:2], func=AF.Exp), [B_const], [B_const])
            DVE(lambda e: e.tensor_tensor(out=small[:, 4:5], in0=small[:, 2:3], in1=small[:, 3:4], op=ALU.subtract), [B_const], [B_const])
            DVE(lambda e, l=l: e.tensor_tensor(out=lamt[:, l:l + 1], in0=small[:, 4:5], in1=laminit[:, l:l + 1], op=ALU.add), [B_const], [B_const])
            DVE(lambda e, l=l: e.tensor_scalar(out=nlamt[:, l:l + 1], in0=lamt[:, l:l + 1], scalar1=-1.0, scalar2=None, op0=ALU.mult), [B_const], [B_const])
        DVE(lambda e: e.tensor_scalar(out=laminit[:], in0=laminit[:], scalar1=-1.0, scalar2=1.0, op0=ALU.mult, op1=ALU.add), [B_const], [B_const])
        sc.fence(("pe", "act", "dve", "sp", "pool"))
        B_const.const = True

        def rmw_h(bank_i, ps_ap, chunk, gblk, scale):
            i, wt, wb = get_wt()
            hb = hT_buf[chunk][gblk]
            DMA_SP(wt[:], hT_d[chunk, :, gblk * 512:(gblk + 1) * 512], [hb], [wb], wt_dsem[i])
            DVE(lambda e, o=wt, p=ps_ap, s=scale: e.scalar_tensor_tensor(out=o[:], in0=p, scalar=s, in1=o[:], op0=ALU.mult, op1=ALU.add),
                [bank_buf[bank_i], wb], [wb])
            DMA_SP(hT_d[chunk, :, gblk * 512:(gblk + 1) * 512], wt[:], [wb], [hb], wt_dsem[i])

        def stage_init(j):
            sc.fence()
            xt = R[:, 0:NT * D * 2].bitcast(F32).rearrange("p (t d) -> p t d", t=NT)
            xb = [Buf("xin%d" % t) for t in range(NT)]
            for t in range(NT):
                tok0 = j * SBT + t * 128
                DMA_SP(xt[:, t, :], x_d[tok0:tok0 + 128, :], [], [xb[t]], x_dsem[t])
            for c in range(KC):
                for b in range(NBLK):
                    bi = mm_ring.next()
                    for q in range(4):
                        t = b * 4 + q
                        PE(lambda e, o=banks[bi][:, q * 128:(q + 1) * 128], a=xt[:, t, c * 128:(c + 1) * 128]: e.transpose(o, a, ident_f),
                           [xb[t], B_const], [bank_buf[bi]], signal=(q == 3))
                    i, wt, wb = get_wt()
                    ACT(lambda e, o=wt, p=banks[bi]: e.copy(o[:], p[:]), [bank_buf[bi]], [wb])
                    gb = j * NBLK + b
                    DMA_SP(hT_d[c, :, gb * 512:(gb + 1) * 512], wt[:], [wb], [hT_buf[c][gb]], wt_dsem[i])
            sc.fence()

        def stage_norm(j, gidx, out_bf=True, fin=None):
            for b in range(NBLK):
                gb = j * NBLK + b
                bi = mm_ring.next()
                for c in range(KC):
                    i, wt, wb = get_wt()
                    DMA_SP(wt[:], hT_d[c, :, gb * 512:(gb + 1) * 512], [hT_buf[c][gb]], [wb], wt_dsem[i])
                    ACT(lambda e, o=wt: e.activation(out=o[:], in_=o[:], func=AF.Square), [wb], [wb])
                    PE(lambda e, o=banks[bi], r=wt, s=(c == 0), t=(c == KC - 1): e.matmul(o[:], ones_f, r[:], start=s, stop=t),
                       [wb, B_const], [bank_buf[bi]], signal=True)
                rt, rb = rstd_t[rstd_i[0] % 2], rstd_b[rstd_i[0] % 2]
                rstd_i[0] += 1
                ACT(lambda e, o=rt, p=banks[bi]: e.activation(out=o[:], in_=p[:], func=AF.Ln, bias=eps_t[:, 0:1], scale=1.0 / D),
                    [bank_buf[bi], B_const], [rb])
                ACT(lambda e, o=rt: e.activation(out=o[:], in_=o[:], func=AF.Exp, scale=-0.5), [rb], [rb])
                for c in range(KC):
                    i, wt, wb = get_wt()
                    DMA_SP(wt[:], hT_d[c, :, gb * 512:(gb + 1) * 512], [hT_buf[c][gb]], [wb], wt_dsem[i])
                    if fin is None:
                        DVE(lambda e, o=xnT[:, c, b * 512:(b + 1) * 512], w=wt, r=rt, g=gains[:, gidx, c:c + 1]:
                            e.scalar_tensor_tensor(out=o, in0=w[:], scalar=g, in1=r[:], op0=ALU.mult, op1=ALU.mult),
                            [wb, rb, B_const], [B_xnT[b]])
                    else:
                        DVE(lambda e, w=wt, r=rt, g=gains[:, gidx, c:c + 1]:
                            e.scalar_tensor_tensor(out=w[:], in0=w[:], scalar=g, in1=r[:], op0=ALU.mult, op1=ALU.mult),
                            [wb, rb, B_const], [wb])
                        fin(c, b, wt, wb)

        def stage_ffn(j, l, wg, wu, wd, gidx):
            sc.fence()
            stage_norm(j, gidx)
            gT = R[:, 0:FC * SBT].rearrange("p (c t) -> p c t", c=FC)
            B_gT = [[Buf("gT%d_%d" % (c, b)) for b in range(NBLK)] for c in range(FC)]
            for cp in range(FC // 2):
                rd, (vg, vu) = load_w([(wview(wg, l, cp * 256, 256), [128, KC, 256]),
                                       (wview(wu, l, cp * 256, 256), [128, KC, 256])])
                for b in range(NBLK):
                    for cc in range(2):
                        c = cp * 2 + cc
                        bg = mm_ring.next()
                        mm_group(bg, banks[bg][:], [(vg[:, k, cc * 128:(cc + 1) * 128], xnT[:, k, b * 512:(b + 1) * 512]) for k in range(KC)],
                                 [rd[0], B_xnT[b]])
                        bu = mm_ring.next()
                        mm_group(bu, banks[bu][:], [(vu[:, k, cc * 128:(cc + 1) * 128], xnT[:, k, b * 512:(b + 1) * 512]) for k in range(KC)],
                                 [rd[1], B_xnT[b]])
                        i, wt, wb = get_wt()
                        ACT(lambda e, o=wt, p=banks[bg]: e.activation(out=o[:], in_=p[:], func=AF.Silu), [bank_buf[bg]], [wb])
                        DVE(lambda e, o=gT[:, c, b * 512:(b + 1) * 512], w=wt, p=banks[bu]: e.tensor_tensor(out=o, in0=w[:], in1=p[:], op=ALU.mult),
                            [wb, bank_buf[bu]], [B_gT[c][b]])
            for mg in range(D // 256):
                rd, (vd,) = load_w([(wd[l].rearrange("(c p) n -> p c n", p=128)[:, :, mg * 256:(mg + 1) * 256], [128, FC, 256])])
                for mm in range(2):
                    m = mg * 2 + mm
                    for b in range(NBLK):
                        bi = mm_ring.next()
                        mm_group(bi, banks[bi][:], [(vd[:, c, mm * 128:(mm + 1) * 128], gT[:, c, b * 512:(b + 1) * 512], [B_gT[c][b]]) for c in range(FC)],
                                 list(rd))
                        rmw_h(bi, banks[bi][:], m, j * NBLK + b, 0.5)

        def stage_final(j):
            sc.fence()
            ot = R[:, 0:NT * D * 2].bitcast(F32).rearrange("p (t d) -> p t d", t=NT)
            ob = [Buf("oout%d" % t) for t in range(NT)]

            def fin(c, b, wt, wb):
                bi = mm_ring.next()
                for q in range(4):
                    PE(lambda e, o=banks[bi][:, q * 128:(q + 1) * 128], a=wt[:, q * 128:(q + 1) * 128]: e.transpose(o, a, ident_f),
                       [wb, B_const], [bank_buf[bi]], signal=(q == 3))
                for q in range(4):
                    t = b * 4 + q
                    ACT(lambda e, o=ot[:, t, c * 128:(c + 1) * 128], p=banks[bi][:, q * 128:(q + 1) * 128]: e.copy(o, p),
                        [bank_buf[bi]], [ob[t]])

            stage_norm(j, 4 * L, fin=fin)
            for t in range(NT):
                tok0 = j * SBT + t * 128
                ev = DMA_SP(out_d[tok0:tok0 + 128, :], ot[:, t, :], [ob[t]], [Buf("outd")], x_dsem[t])
                final_evs.append(ev)
            sc.fence()


        mma_ring = Ring([0, 1, 2, 3])
        acc_sets = Ring([(4, 5), (6, 7)])
        s_ring3 = mma_ring
        deferred = []

        def flush_deferred():
            while deferred:
                deferred.pop(0)()

        B_smallc = {c: Buf("sc%d" % c) for c in range(34, 64)}
        smallcols = Ring(list(range(40, 64)))
        B_small = Buf("smallcols")

        def Rf32(off, n):
            return R[:, off:off + 2 * n].bitcast(F32)

        def attn(j, QT, KT, vtile, nkt_list, scale, Pt, B_P, reads_q, reads_k, reads_v, out_fn,
                 Ctb=None, B_Ctb=None, negc=None, B_negc=None, causal=True):
            for qb in range(NBLK):
                q0 = j * SBT + qb * 512
                kts = nkt_list(qb)
                accb = acc_sets.next()
                n_k = len(kts)

                def issue_S(ki):
                    kt = kts[ki]
                    k0 = kt * 128
                    diag = causal and (k0 >= q0)
                    bS = s_ring3.next()
                    PE(lambda e, o=banks[bS], a=KT[:, k0:k0 + 128], b=QT[:, qb * 512:(qb + 1) * 512]: e.matmul(o[:], a, b, start=True, stop=True),
                       list(reads_q) + list(reads_k(kt)), [bank_buf[bS]], signal=True)
                    Pv, Pb = Pt.next()
                    if Ctb is not None or diag:
                        i, wt, wb = get_wt()
                        if Ctb is not None:
                            DVE(lambda e, o=wt, p=banks[bS], c=Ctb[:, qb * 512:(qb + 1) * 512], s=scale:
                                e.scalar_tensor_tensor(out=o[:], in0=p[:], scalar=s, in1=c, op0=ALU.mult, op1=ALU.add),
                                [bank_buf[bS], B_Ctb], [wb])
                            if diag:
                                v = (k0 - q0) // 128
                                DVE(lambda e, o=wt, m=maskt[:, v, :]: e.tensor_tensor(out=o[:], in0=o[:], in1=m, op=ALU.add), [wb, B_const], [wb])
                        else:
                            v = (k0 - q0) // 128
                            DVE(lambda e, o=wt, p=banks[bS], m=maskt[:, v, :], s=scale:
                                e.scalar_tensor_tensor(out=o[:], in0=p[:], scalar=s, in1=m, op0=ALU.mult, op1=ALU.add),
                                [bank_buf[bS], B_const], [wb])
                        if negc is not None:
                            ACT(lambda e, o=Pv, w=wt, bcol=negc(kt): e.activation(out=o, in_=w[:], func=AF.Exp, bias=bcol, scale=1.0),
                                [wb, B_negc(kt)], [Pb])
                        else:
                            ACT(lambda e, o=Pv, w=wt: e.activation(out=o, in_=w[:], func=AF.Exp), [wb], [Pb])
                    else:
                        ACT(lambda e, o=Pv, p=banks[bS], s=scale: e.activation(out=o, in_=p[:], func=AF.Exp, scale=s), [bank_buf[bS]], [Pb])
                    return Pv, Pb

                LA = 2
                pend = {}
                for ki in range(min(LA, n_k)):
                    pend[ki] = issue_S(ki)
                for ki in range(n_k):
                    if ki + LA < n_k:
                        pend[ki + LA] = issue_S(ki + LA)
                    Pv, Pb = pend.pop(ki)
                    kt = kts[ki]
                    if ki == min(2, n_k - 1):
                        flush_deferred()
                    for qs in range(4):
                        bk = accb[qs // 2]
                        off = (qs % 2) * 129
                        first = (ki == 0)
                        PE(lambda e, o=banks[bk][:, off:off + 129], a=Pv[:, qs * 128:(qs + 1) * 128], b=vtile(kt), s=(first and qs % 2 == 0), t=(ki == n_k - 1):
                           e.matmul(o, a, b, start=s, stop=t, skip_group_check=True),
                           [Pb] + list(reads_v(kt)), [bank_buf[bk]], signal=(ki == n_k - 1))
                for qs in range(4):
                    bk = accb[qs // 2]
                    off = (qs % 2) * 129
                    col = smallcols.next()
                    rec = small[:, col:col + 1]
                    DVE(lambda e, o=rec, p=banks[bk][:, off + 128:off + 129]: e.reciprocal(out=o, in_=p), [bank_buf[bk]], [B_smallc[col]])
                    out_fn(qb, qs, banks[bk][:, off:off + 128], rec, bank_buf[bk], B_smallc[col])

        def transpose_to(On, B_On, dst_fn, B_dst, n=4):
            def go():
                bi = mma_ring.next()
                pb = banks[bi].bitcast(BF16)
                for q in range(n):
                    PE(lambda e, o=pb[:, q * 128:(q + 1) * 128], a=On[:, q, :]: e.transpose(o, a, ident_b), [B_On[q], B_const], [bank_buf[bi]], signal=(q == n - 1))
                ACT(lambda e, o=dst_fn(), p=pb[:, 0:n * 128]: e.copy(o, p), [bank_buf[bi]], [B_dst])
            deferred.append(go)

        def proj_fm(vw, rd, b, kcn=KC, rhs=None, B_rhs=None, ncol=512, pslice=None):
            bi = mma_ring.next()
            rh = rhs if rhs is not None else (lambda k: xnT[:, k, b * 512:(b + 1) * 512])
            Br = B_rhs if B_rhs is not None else B_xnT[b]
            out = banks[bi][:, 0:ncol] if pslice is None else banks[bi][pslice[0]:pslice[1], 0:ncol]
            mm_group(bi, out, [(vw(k), rh(k)) for k in range(kcn)], list(rd) + [Br])
            return bi

        def stage_mem(l):
            sc.fence()
            mt = Rf32(0, 2 * D).rearrange("p (t d) -> p t d", t=2)
            memn = R[:, 8192:12288].rearrange("p (t d) -> p t d", t=2)
            memnT = R[:, 12288:16384].rearrange("p (c t) -> p c t", c=KC)
            memg = Rf32(16384, D)
            mk_s = R[:, 20480:21504].rearrange("p (h t) -> p h t", h=4)
            mv_s = R[:, 21504:22536].rearrange("p (t h d) -> p t h d", t=2, h=4)
            junk = Rf32(22544, D)
            B_mt = [Buf("mt0"), Buf("mt1")]; B_memn = [Buf("memn0"), Buf("memn1")]; B_memnT = Buf("memnT"); B_memg = Buf("memg")
            B_mk = Buf("mk_s"); B_mv = Buf("mv_s"); B_junk = Buf("junk")
            DMA_SP(memg, memg_d[:, l, :], [], [B_memg], misc_dsem)
            DVE(lambda e: e.memset(mv_s[:, :, :, 128:129], 1.0), [], [B_mv])
            for t in range(2):
                DMA_SP(mt[:, t, :], mem_d[t * 128:(t + 1) * 128, :], [], [B_mt[t]], x_dsem[t])
                ACT(lambda e, t=t: e.activation(out=junk, in_=mt[:, t, :], func=AF.Square, accum_out=small[:, 34 + t:35 + t]), [B_mt[t]], [B_junk, B_small])
                ACT(lambda e, t=t: e.activation(out=small[:, 36 + t:37 + t], in_=small[:, 34 + t:35 + t], func=AF.Ln, bias=eps_t[:, 0:1], scale=1.0 / D), [B_small, B_const], [B_small])
                ACT(lambda e, t=t: e.activation(out=small[:, 36 + t:37 + t], in_=small[:, 36 + t:37 + t], func=AF.Exp, scale=-0.5), [B_small], [B_small])
                DVE(lambda e, t=t: e.scalar_tensor_tensor(out=memn[:, t, :], in0=mt[:, t, :], scalar=small[:, 36 + t:37 + t], in1=memg, op0=ALU.mult, op1=ALU.mult),
                    [B_mt[t], B_small, B_memg], [B_memn[t]])
            for c in range(KC):
                bi = mma_ring.next()
                pb = banks[bi].bitcast(BF16)
                for t in range(2):
                    PE(lambda e, o=pb[:, t * 128:(t + 1) * 128], a=memn[:, t, c * 128:(c + 1) * 128]: e.transpose(o, a, ident_b), [B_memn[t], B_const], [bank_buf[bi]], signal=(t == 1))
                ACT(lambda e, o=memnT[:, c, :], p=pb[:, 0:256]: e.copy(o, p), [bank_buf[bi]], [B_memnT])
            for hd in range(4):
                rd, (vk,) = load_w([(wview(w_ckv, l, hd * 128, 128), [128, KC, 128])])
                bi = mma_ring.next()
                mm_group(bi, banks[bi][:, 0:MEM], [(vk[:, k, :], memnT[:, k, :]) for k in range(KC)], list(rd) + [B_memnT])
                ACT(lambda e, o=mk_s[:, hd, :], p=banks[bi][:, 0:MEM]: e.copy(o, p), [bank_buf[bi]], [B_mk])
            rd, (vv,) = load_w([(wview(w_ckv, l, 512, 512), [128, KC, 512])])
            for t in range(2):
                bi = mma_ring.next()
                mm_group(bi, banks[bi][:], [(memnT[:, k, t * 128:(t + 1) * 128], vv[:, k, :]) for k in range(KC)], list(rd) + [B_memnT])
                ACT(lambda e, o=mv_s[:, t, :, 0:128], p=banks[bi][:].rearrange("p (h d) -> p h d", h=4): e.copy(o, p), [bank_buf[bi]], [B_mv])
            DMA_SP(mk_d[l], mk_s, [B_mk], [B_mkd[l]], misc_dsem)
            DMA_SP(mv_d[l], mv_s, [B_mv], [B_mvd[l]], misc_dsem)
            sc.fence()

        B_mkd = [Buf("mkd%d" % l) for l in range(L)]
        B_mvd = [Buf("mvd%d" % l) for l in range(L)]

        def stage_cross(j, l):
            sc.fence()
            stage_norm(j, 4 * l + 2)
            crossT = R[:, 0:4096].rearrange("p (c t) -> p c t", c=4)
            memK = R[:, 4096:5120].rearrange("p (h t) -> p h t", h=4)
            memV = R[:, 5120:6152].rearrange("p (t h d) -> p t h d", t=2, h=4)
            QT = R[:, 6160:7184]
            Pt = Ring([(R[:, 7184 + i * 512:7184 + (i + 1) * 512], Buf("P%d" % i)) for i in range(4)])
            On = R[:, 9232:9744].rearrange("p (q d) -> p q d", q=4)
            B_cT = Buf("crossT"); B_mK = Buf("memK"); B_mV = Buf("memV"); B_QT = Buf("QT"); B_On = [Buf("On%d" % q) for q in range(4)]
            DMA_SP(memK, mk_d[l], [B_mkd[l]], [B_mK], misc_dsem)
            DMA_SP(memV, mv_d[l], [B_mvd[l]], [B_mV], misc_dsem)
            for hd in range(4):
                rd, (vq,) = load_w([(wview(w_cq, l, hd * 128, 128), [128, KC, 128])])
                for b in range(NBLK):
                    bi = proj_fm(lambda k: vq[:, k, :], rd, b)
                    ACT(lambda e, o=QT[:, b * 512:(b + 1) * 512], p=banks[bi]: e.copy(o, p[:]), [bank_buf[bi]], [B_QT])

                def out_fn(qb, qs, acc, rec, bb, bs, hd=hd):
                    DVE(lambda e, o=On[:, qs, :], a=acc, r=rec: e.tensor_scalar(out=o, in0=a, scalar1=r, scalar2=None, op0=ALU.mult), [bb, bs], [B_On[qs]])
                    if qs == 3:
                        transpose_to(On, B_On, lambda: crossT[:, hd, qb * 512:(qb + 1) * 512], B_cT)

                attn(j, QT, memK[:, hd, :], lambda kt: memV[:, kt, hd, :], lambda qb: [0, 1], 128.0 ** -0.5, Pt, None,
                     [B_QT], lambda kt: [B_mK], lambda kt: [B_mV], out_fn, causal=False)
            flush_deferred()
            for mg in range(D // 256):
                rd, (vo,) = load_w([(w_co[l].rearrange("(c p) n -> p c n", p=128)[:, :, mg * 256:(mg + 1) * 256], [128, 4, 256])])
                for mm in range(2):
                    for b in range(NBLK):
                        bi = mma_ring.next()
                        mm_group(bi, banks[bi][:], [(vo[:, c, mm * 128:(mm + 1) * 128], crossT[:, c, b * 512:(b + 1) * 512]) for c in range(4)], list(rd) + [B_cT])
                        rmw_h(bi, banks[bi][:], mg * 2 + mm, j * NBLK + b, 1.0)

        B_fk = [[Buf("fkc") for _ in range(NFH)] for _ in range(L)]
        B_fv = [[Buf("fvc") for _ in range(NFH)] for _ in range(L)]
        B_dk = [[Buf("dkc") for _ in range(NDH)] for _ in range(L)]
        B_dv = [[Buf("dvc") for _ in range(NDH)] for _ in range(L)]
        B_fc = [Buf("fcc") for _ in range(L)]
        kv_dsem = [new_dsem() for _ in range(4)]

        def stage_mix(j, l):
            sc.fence()
            stage_norm(j, 4 * l + 1)
            mixT = R[:, 0:16384].rearrange("p (c t) -> p c t", c=KC)
            cosT = Rf32(16384, SBT); sinT = Rf32(18432, SBT); Ctb = Rf32(20480, SBT)
            QT = R[:, 22528:23552]
            KT = R[:, 23552:27648]
            V1 = R[:, 27648:31776].rearrange("p (t d) -> p t d", d=129)
            negc = Rf32(31776, 32 * NFH).rearrange("p (t h) -> p t h", h=NFH)
            cT = Rf32(32160, SBT); lf = Rf32(34208, SBT); onesr = Rf32(36256, SBT)
            lf_full = lf
            Pt = Ring([(R[:, 38304 + i * 512:38304 + (i + 1) * 512], Buf("P%d" % i)) for i in range(4)])
            On = R[:, 40352:40864].rearrange("p (q d) -> p q d", q=4)
            O1n = Rf32(40864, 1024).rearrange("p (q d) -> p q d", q=8)
            tmpA = Rf32(42912, SBT)
            QTz = R[:, 34208:36256].rearrange("p (c t) -> p c t", c=2)
            B_mixT = Buf("mixT"); B_cos = Buf("cos"); B_sin = Buf("sin"); B_Ctb = Buf("Ctb"); B_QT = Buf("QT")
            B_KTo = Buf("KTown"); B_KTp = Buf("KTprev"); B_Vo = Buf("Vown"); B_Vp = Buf("Vprev")
            B_nco = Buf("negc_own"); B_ncp = Buf("negc_prev"); B_cT = Buf("cT"); B_lf = Buf("lf"); B_ones = Buf("onesr")
            B_On = [Buf("On%d" % q) for q in range(4)]; B_O1 = [Buf("O1n%d" % q) for q in range(8)]; B_tmpA = Buf("tmpA")
            tok0 = j * SBT
            npt = j * NT
            nkt = lambda qb: list(range((tok0 + (qb + 1) * 512) // 128))
            rk = lambda kt: [B_KTp] if kt < npt else [B_KTo]
            rv = lambda kt: [B_Vp] if kt < npt else [B_Vo]

            posi = tmpA.bitcast(I32)
            DMA_SP(posi, pos_d[tok0:tok0 + SBT].partition_broadcast(128), [], [B_tmpA], misc_dsem)
            TWO_PI = float(2.0 * np.pi)
            PI = float(np.pi)
            posf = Ctb
            DVE(lambda e: e.tensor_copy(out=posf, in_=posi), [B_tmpA], [B_Ctb])

            def trig_table(dst, B_dst, shift):
                kf = tmpA
                ki = tmpA.bitcast(I32)
                DVE(lambda e: e.tensor_scalar(out=dst, in0=posf, scalar1=invf, scalar2=float(shift), op0=ALU.mult, op1=ALU.add), [B_Ctb, B_const], [B_dst])
                DVE(lambda e: e.tensor_scalar(out=ki, in0=dst, scalar1=float(1.0 / TWO_PI), scalar2=None, op0=ALU.mult), [B_dst], [B_tmpA])
                DVE(lambda e: e.tensor_copy(out=lf_full, in_=ki), [B_tmpA], [B_lf])
                DVE(lambda e: e.scalar_tensor_tensor(out=dst, in0=lf_full, scalar=-TWO_PI, in1=dst, op0=ALU.mult, op1=ALU.add), [B_lf, B_dst], [B_dst])
                DVE(lambda e: e.tensor_scalar(out=kf, in0=dst, scalar1=PI, scalar2=-TWO_PI, op0=ALU.is_gt, op1=ALU.mult), [B_dst], [B_tmpA])
                DVE(lambda e: e.tensor_tensor(out=dst, in0=dst, in1=kf, op=ALU.add), [B_dst, B_tmpA], [B_dst])
                DVE(lambda e: e.tensor_scalar(out=kf, in0=dst, scalar1=-PI, scalar2=TWO_PI, op0=ALU.is_lt, op1=ALU.mult), [B_dst], [B_tmpA])
                DVE(lambda e: e.tensor_tensor(out=dst, in0=dst, in1=kf, op=ALU.add), [B_dst, B_tmpA], [B_dst])
                DVE(lambda e: e.tensor_scalar(out=dst, in0=dst, scalar1=PI, scalar2=-PI, op0=ALU.min, op1=ALU.max), [B_dst], [B_dst])
                ACT(lambda e: e.activation(out=dst, in_=dst, func=AF.Sin), [B_dst], [B_dst])

            trig_table(sinT, B_sin, 0.0)
            trig_table(cosT, B_cos, PI / 2)
            DVE(lambda e: e.memset(V1[:, :, 128:129], 1.0), [], [B_Vo, B_Vp])
            DVE(lambda e: e.memset(onesr[0:NFH, :], 1.0), [], [B_ones])

            rd, (vf,) = load_w([(wview(w_in, l, OFF_FF, NFH), [128, KC, NFH])])
            for b in range(NBLK):
                bi = proj_fm(lambda k: vf[:, k, :], rd, b, pslice=(0, NFH))
                ACT(lambda e, o=lf[0:NFH, b * 512:(b + 1) * 512], p=banks[bi][0:NFH, :]: e.activation(out=o, in_=p, func=AF.Exp, bias=nfbias[:, l:l + 1], scale=-1.0),
                    [bank_buf[bi], B_const], [B_lf])
            ACT(lambda e: e.activation(out=lf[0:NFH, :], in_=lf[0:NFH, :], func=AF.Ln, bias=one_t[0:NFH, 0:1], scale=1.0), [B_lf, B_const], [B_lf])
            DVE(lambda e: e.tensor_tensor_scan(out=cT[0:NFH, :], data0=onesr[0:NFH, :], data1=lf[0:NFH, :], initial=ccarry[:, l:l + 1], op0=ALU.mult, op1=ALU.subtract),
                [B_ones, B_lf, B_carry], [B_cT])
            DVE(lambda e: e.tensor_copy(out=ccarry[:, l:l + 1], in_=cT[0:NFH, SBT - 1:SBT]), [B_cT], [B_carry])
            bi = mma_ring.next()
            for t in range(NT):
                PE(lambda e, o=banks[bi][:, t * NFH:(t + 1) * NFH], a=cT[0:NFH, t * 128:(t + 1) * 128]: e.transpose(o, a, ident_f[0:NFH, 0:NFH]),
                   [B_cT, B_const], [bank_buf[bi]], signal=(t == NT - 1))
            ACT(lambda e, o=negc[:, npt:npt + NT, :], p=banks[bi][:, 0:NT * NFH].rearrange("p (t h) -> p t h", h=NFH): e.mul(o, p, -1.0), [bank_buf[bi]], [B_nco])
            if j < nsb - 1:
                DMA_SP(fc_d[l, :, npt:npt + NT, :], negc[:, npt:npt + NT, :], [B_nco], [B_fc[l]], kv_dsem[3])
            if j > 0:
                DMA_SP(negc[:, 0:npt, :], fc_d[l, :, 0:npt, :], [B_fc[l]], [B_ncp], kv_dsem[3])

            def kv_proj(vk, vv, rd, kc_d, vc_d, B_kc, B_vc, rope):
                for b in range(NBLK):
                    bi = proj_fm(lambda k: vk[:, k, :], [rd[1]], b)
                    dst = KT[:, tok0 + b * 512:tok0 + (b + 1) * 512]
                    if rope:
                        rope_apply(bi, dst, B_KTo, b)
                    else:
                        ACT(lambda e, o=dst, p=banks[bi]: e.copy(o, p[:]), [bank_buf[bi]], [B_KTo])
                for tq in range(NT // 4):
                    bi = mma_ring.next()
                    for t4 in range(4):
                        t = tq * 4 + t4
                        mm_group(bi, banks[bi][:, t4 * 128:(t4 + 1) * 128], [(xnT[:, k, t * 128:(t + 1) * 128], vv[:, k, :]) for k in range(KC)],
                                 [rd[2], B_xnT[t // 4]])
                    ACT(lambda e, o=V1[:, npt + tq * 4:npt + tq * 4 + 4, 0:128], p=banks[bi][:].rearrange("p (t d) -> p t d", t=4): e.copy(o, p), [bank_buf[bi]], [B_Vo])
                if j < nsb - 1:
                    DMA_SP(kc_d[:, tok0:tok0 + SBT], KT[:, tok0:tok0 + SBT], [B_KTo], [B_kc], kv_dsem[0])
                    DMA_SP(vc_d[:, npt:npt + NT, :], V1[:, npt:npt + NT, :], [B_Vo], [B_vc], kv_dsem[1])
                if j > 0:
                    DMA_SP(KT[:, 0:tok0], kc_d[:, 0:tok0], [B_kc], [B_KTp], kv_dsem[0])
                    DMA_SP(V1[:, 0:npt, :], vc_d[:, 0:npt, :], [B_vc], [B_Vp], kv_dsem[1])

            def rope_apply(bi, dst, B_dst, b):
                i, wt, wb = get_wt()
                ACT(lambda e, o=wt, p=banks[bi]: e.copy(o[:], p[:]), [bank_buf[bi]], [wb])
                b2 = mma_ring.next()
                PE(lambda e, o=banks[b2], w=wt: e.matmul(o[:], perm_f, w[:], start=True, stop=True), [wb, B_const], [bank_buf[b2]], signal=True)
                i2, wt2, wb2 = get_wt()
                DVE(lambda e, o=wt2, p=banks[b2], s=sinT[:, b * 512:(b + 1) * 512]: e.tensor_tensor(out=o[:], in0=p[:], in1=s, op=ALU.mult), [bank_buf[b2], B_sin], [wb2])
                DVE(lambda e, o=wt, c=cosT[:, b * 512:(b + 1) * 512]: e.tensor_tensor(out=o[:], in0=o[:], in1=c, op=ALU.mult), [wb, B_cos], [wb])
                if dst is None:
                    for comp in range(2):
                        p0, p1 = comp * 64, comp * 64 + 64
                        DVE(lambda e, o=QTz[p0:p1, comp, b * 512:(b + 1) * 512], a=wt, c=wt2, p0=p0, p1=p1: e.tensor_tensor(out=o, in0=a[p0:p1, :], in1=c[p0:p1, :], op=ALU.add),
                            [wb, wb2], [B_dst])
                else:
                    DVE(lambda e, o=dst, a=wt, c=wt2: e.tensor_tensor(out=o, in0=a[:], in1=c[:], op=ALU.add), [wb, wb2], [B_dst])

            for hd in range(NFH):
                rd, (vq, vk, vv) = load_w([(wview(w_in, l, OFF_FQ + hd * 128, 128), [128, KC, 128]),
                                           (wview(w_in, l, OFF_FK + hd * 128, 128), [128, KC, 128]),
                                           (wview(w_in, l, OFF_FV + hd * 128, 128), [128, KC, 128])])
                for b in range(NBLK):
                    bi = proj_fm(lambda k: vq[:, k, :], [rd[0]], b)
                    ACT(lambda e, o=QT[:, b * 512:(b + 1) * 512], p=banks[bi]: e.copy(o, p[:]), [bank_buf[bi]], [B_QT])
                kv_proj(vk, vv, rd, fk_d[l, hd], fv_d[l, hd], B_fk[l][hd], B_fv[l][hd], rope=False)
                for b in range(NBLK):
                    bi = mma_ring.next()
                    PE(lambda e, o=banks[bi], a=selt[0:NFH, hd * 128:(hd + 1) * 128], r=cT[0:NFH, b * 512:(b + 1) * 512]: e.matmul(o[:], a, r, start=True, stop=True),
                       [B_cT, B_const], [bank_buf[bi]], signal=True)
                    ACT(lambda e, o=Ctb[:, b * 512:(b + 1) * 512], p=banks[bi]: e.copy(o, p[:]), [bank_buf[bi]], [B_Ctb])

                def out_fn(qb, qs, acc, rec, bb, bs, hd=hd):
                    DVE(lambda e, o=On[:, qs, :], a=acc, r=rec: e.tensor_scalar(out=o, in0=a, scalar1=r, scalar2=None, op0=ALU.mult), [bb, bs], [B_On[qs]])
                    if qs == 3:
                        transpose_to(On, B_On, lambda: mixT[:, hd, qb * 512:(qb + 1) * 512], B_mixT)

                attn(j, QT, KT, lambda kt: V1[:, kt, :], nkt, 128.0 ** -0.5, Pt, None, [B_QT], rk, rv, out_fn,
                     Ctb=Ctb, B_Ctb=B_Ctb, negc=lambda kt, hd=hd: negc[:, kt, hd:hd + 1], B_negc=lambda kt: (B_ncp if kt < npt else B_nco))

            DVE(lambda e: e.memset(QTz[64:128, 0, :], 0.0), [], [B_lf])
            DVE(lambda e: e.memset(QTz[0:64, 1, :], 0.0), [], [B_lf])
            for hd in range(NDH):
                rd, (vq, vk, vv) = load_w([(wview(w_in, l, OFF_DQ + hd * 128, 128), [128, KC, 128]),
                                           (wview(w_in, l, OFF_DK + hd * 128, 128), [128, KC, 128]),
                                           (wview(w_in, l, OFF_DV + hd * 128, 128), [128, KC, 128])])
                for b in range(NBLK):
                    bi = proj_fm(lambda k: vq[:, k, :], [rd[0]], b)
                    rope_apply(bi, None, B_lf, b)
                kv_proj(vk, vv, rd, dk_d[l, hd], dv_d[l, hd], B_dk[l][hd], B_dv[l][hd], rope=True)
                for comp in range(2):
                    p0, p1 = comp * 64, comp * 64 + 64

                    def out_fn(qb, qs, acc, rec, bb, bs, hd=hd, comp=comp):
                        if comp == 0:
                            DVE(lambda e, o=O1n[:, qb * 4 + qs, :], a=acc, r=rec: e.tensor_scalar(out=o, in0=a, scalar1=r, scalar2=None, op0=ALU.mult), [bb, bs], [B_O1[qb * 4 + qs]])
                            return
                        i, wt, wb = get_wt()
                        a_ap = wt[:, 0:128]
                        DVE(lambda e, o=a_ap, a=acc, r=rec: e.tensor_scalar(out=o, in0=a, scalar1=r, scalar2=None, op0=ALU.mult), [bb, bs], [wb])
                        DVE(lambda e, o=a_ap, o1=O1n[:, qb * 4 + qs, :]: e.scalar_tensor_tensor(out=o, in0=o, scalar=nlamt[:, l:l + 1], in1=o1, op0=ALU.mult, op1=ALU.add),
                            [wb, B_O1[qb * 4 + qs], B_const], [wb])
                        c1 = smallcols.next(); c2 = smallcols.next()
                        ACT(lambda e, a=a_ap, j_=wt[:, 128:256], s=small[:, c1:c1 + 1]: e.activation(out=j_, in_=a, func=AF.Square, accum_out=s), [wb], [wb, B_smallc[c1]])
                        ACT(lambda e, s=small[:, c1:c1 + 1], r=small[:, c2:c2 + 1]: e.activation(out=r, in_=s, func=AF.Ln, bias=eps_t[:, 0:1], scale=1.0 / 128), [B_smallc[c1], B_const], [B_smallc[c2]])
                        ACT(lambda e, r=small[:, c2:c2 + 1]: e.activation(out=r, in_=r, func=AF.Exp, scale=-0.5), [B_smallc[c2]], [B_smallc[c2]])
                        DVE(lambda e, o=On[:, qs, :], a=a_ap, r=small[:, c2:c2 + 1]: e.scalar_tensor_tensor(out=o, in0=a, scalar=r, in1=subln[:, l, :], op0=ALU.mult, op1=ALU.mult),
                            [wb, B_smallc[c2], B_const], [B_On[qs]])
                        if qs == 3:
                            transpose_to(On, B_On, lambda: mixT[:, NFH + hd, qb * 512:(qb + 1) * 512], B_mixT)

                    attn(j, QTz[:, comp, :], KT, lambda kt: V1[:, kt, :], nkt, 64.0 ** -0.5, Pt, None, [B_lf], rk, rv, out_fn)

            flush_deferred()
            sc.fence()
            u = Rf32(22528, SBT + 2); gbS = Rf32(24580, SBT); tconv = Rf32(26628, SBT)
            B_u = Buf("u"); B_gbS = Buf("gbS"); B_tc = Buf("tconv")
            for ch in range(4):
                rd, (vgb, vgc, vhc) = load_w([(wview(w_in, l, OFF_GB + ch * 128, 128), [128, KC, 128]),
                                              (wview(w_in, l, OFF_GC + ch * 128, 128), [128, KC, 128]),
                                              (wview(w_in, l, OFF_HC + ch * 128, 128), [128, KC, 128])])
                ACT(lambda e, ch=ch: e.copy(u[:, 0:2], halo[:, l, ch, :]), [B_halo], [B_u])
                for b in range(NBLK):
                    bi = proj_fm(lambda k: vgb[:, k, :], [rd[0]], b)
                    ACT(lambda e, o=gbS[:, b * 512:(b + 1) * 512], p=banks[bi]: e.copy(o, p[:]), [bank_buf[bi]], [B_gbS])
                    bi = proj_fm(lambda k: vgc[:, k, :], [rd[1]], b)
                    i, wt, wb = get_wt()
                    ACT(lambda e, o=wt, p=banks[bi]: e.copy(o[:], p[:]), [bank_buf[bi]], [wb])
                    bi = proj_fm(lambda k: vhc[:, k, :], [rd[2]], b)
                    DVE(lambda e, o=u[:, 2 + b * 512:2 + (b + 1) * 512], w=wt, p=banks[bi]: e.tensor_tensor(out=o, in0=w[:], in1=p[:], op=ALU.mult), [wb, bank_buf[bi]], [B_u])
                cw = convw[:, l, ch, :]
                DVE(lambda e, cw=cw: e.tensor_scalar(out=tconv, in0=u[:, 2:2 + SBT], scalar1=cw[:, 2:3], scalar2=cw[:, 3:4], op0=ALU.mult, op1=ALU.add), [B_u, B_const], [B_tc])
                DVE(lambda e, cw=cw: e.scalar_tensor_tensor(out=tconv, in0=u[:, 1:1 + SBT], scalar=cw[:, 1:2], in1=tconv, op0=ALU.mult, op1=ALU.add), [B_u, B_tc, B_const], [B_tc])
                DVE(lambda e, cw=cw: e.scalar_tensor_tensor(out=tconv, in0=u[:, 0:SBT], scalar=cw[:, 0:1], in1=tconv, op0=ALU.mult, op1=ALU.add), [B_u, B_tc, B_const], [B_tc])
                DVE(lambda e, ch=ch: e.tensor_tensor(out=mixT[:, 2 * NFH + ch, :], in0=tconv, in1=gbS, op=ALU.mult), [B_tc, B_gbS], [B_mixT])
                ACT(lambda e, ch=ch: e.copy(halo[:, l, ch, :], u[:, SBT:SBT + 2]), [B_u], [B_halo])

            for mg in range(D // 256):
                rd, (vo,) = load_w([(wview(w_out, l, mg * 256, 256), [128, KC, 256])])
                for mm in range(2):
                    for b in range(NBLK):
                        bi = mma_ring.next()
                        mm_group(bi, banks[bi][:], [(vo[:, c, mm * 128:(mm + 1) * 128], mixT[:, c, b * 512:(b + 1) * 512]) for c in range(KC)], list(rd) + [B_mixT])
                        rmw_h(bi, banks[bi][:], mg * 2 + mm, j * NBLK + b, 1.0)

        B_carry = Buf("ccarry")
        B_halo = Buf("halo")
        one_t = small[:, 33:34]
        final_evs = []
        eps_t = small[:, 32:33]
        B_const.const = False
        DVE(lambda e: e.memset(small[:, 32:33], EPS), [], [B_const])
        DVE(lambda e: e.memset(small[:, 33:34], 1.0), [], [B_const])
        for l in range(L):
            DVE(lambda e, l=l: e.tensor_scalar(out=subln[:, l, :], in0=subln[:, l, :], scalar1=laminit[:, l:l + 1], scalar2=None, op0=ALU.mult), [B_const], [B_const])
        sc.fence(("pe", "act", "dve", "sp", "pool"))
        B_const.const = True

        if STAGES["cross"]:
            for l in range(L):
                stage_mem(l)
        for j in range(nsb):
            stage_init(j)
            for l in range(L):
                if STAGES["ffn1"]:
                    stage_ffn(j, l, w_g1, w_u1, w_d1, 4 * l + 0)
                if STAGES["mix"]:
                    stage_mix(j, l)
                if STAGES["cross"]:
                    stage_cross(j, l)
                if STAGES["ffn2"]:
                    stage_ffn(j, l, w_g2, w_u2, w_d2, 4 * l + 3)
            stage_final(j)

        for ev in final_evs:
            sc._wait("sp", ev)
        sc.fence(("pe", "act", "dve", "sp", "pool"))

        engmap = {"pe": "tensor", "act": "scalar", "dve": "vector", "pool": "gpsimd", "sp": "sync"}
        with nc.Block() as block:
            for en in Sched.ENGS:
                prog = sc.q[en]

                def body(e, prog=prog):
                    for it in prog:
                        if it[0] == "w":
                            e.wait_ge(it[1], it[2])
                        else:
                            ins = it[1](e)
                            if it[2] is not None:
                                ins.then_inc(it[2], it[3])

                getattr(block, engmap[en])(body)
        stats = {e: len(sc.q[e]) for e in Sched.ENGS}
    return nc, stats


STAGES = {"ffn1": True, "mix": True, "cross": True, "ffn2": True}


def host_consts(depth):
    L = depth
    cf = np.zeros((128, 3 * 128 + 3), np.float32)
    cf[:, 0:128] = np.eye(128, dtype=np.float32)
    cf[:, 128:256] = 1.0
    perm = np.zeros((128, 128), np.float32)
    invf = np.zeros((128,), np.float32)
    inv_freq = ROPE_THETA ** (-np.arange(0, 16, 2, dtype=np.float32) / 16.0)
    for comp in range(2):
        base = comp * 64
        for i in range(8):
            m1, m2 = base + i, base + 8 + i
            perm[m2, m1] = 1.0
            perm[m1, m2] = 1.0
            invf[m1] = -inv_freq[i]
            invf[m2] = inv_freq[i]
    cf[:, 256:384] = perm
    cf[:, 384] = invf
    cf[:, 385] = -np.pi
    cbf = np.eye(128, dtype=np.float32).astype(ml_dtypes.bfloat16)
    sel = np.zeros((NFH, NFH * 128), np.float32)
    for h in range(NFH):
        sel[h, h * 128:(h + 1) * 128] = 1.0
    mask = np.zeros((128, 4, 512), np.float32)
    p = np.arange(128)[:, None]
    jj = np.arange(512)[None, :]
    for v in range(4):
        mask[:, v, :] = np.where(jj - p >= 128 * v, 0.0, -1e30)
    laminit = np.zeros((128, L), np.float32)
    for l in range(L):
        laminit[:, l] = 0.8 - 0.6 * np.exp(-0.3 * l)
    return cf, cbf, sel, mask, laminit


def make_in_maps(inputs, depth, s_len, batches):
    L = depth
    f = lambda a: np.ascontiguousarray(np.asarray(a))
    cf, cbf, sel, mask, laminit = host_consts(L)
    g = np.zeros((4 * L + 1, D), np.float32)
    for l in range(L):
        g[4 * l + 0] = inputs["ffn1_norm"][l]
        g[4 * l + 1] = inputs["mix_norm"][l]
        g[4 * l + 2] = inputs["cross_norm"][l]
        g[4 * l + 3] = inputs["ffn2_norm"][l]
    g[4 * L] = inputs["final_norm"]
    gains = f(g.reshape(4 * L + 1, KC, 128).transpose(2, 0, 1))
    memg = f(np.broadcast_to(np.asarray(inputs["mem_norm"])[None, :L, :], (128, L, D)))
    fb = f(np.asarray(inputs["forget_bias"])[:L].T)
    cw = np.zeros((128, L, 4, 4), np.float32)
    for l in range(L):
        for ch in range(4):
            cw[:, l, ch, 0:3] = np.asarray(inputs["conv_w"])[l, :, ch * 128:(ch + 1) * 128].T
            cw[:, l, ch, 3] = np.asarray(inputs["conv_b"])[l, ch * 128:(ch + 1) * 128]
    lam = np.stack([np.asarray(inputs[k])[:L] for k in ("lambda_q1", "lambda_k1", "lambda_q2", "lambda_k2")], axis=1)
    lam = f(np.broadcast_to(lam[None], (128, L, 4, 64)))
    subln = f(np.broadcast_to(np.asarray(inputs["diff_subln"])[None, :L, :], (128, L, 128)))
    shared = {
        "gains": gains, "mem_gain_b": memg, "forget_bias_t": fb, "conv_wb": cw, "lam_b": lam, "subln_b": subln,
        "const_f32": cf, "const_bf16": cbf, "const_sel": sel, "const_mask": mask, "const_laminit": laminit,
    }
    for k in ("ffn1_w_gate", "ffn1_w_up", "ffn1_w_down", "ffn2_w_gate", "ffn2_w_up", "ffn2_w_down",
              "mix_w_in", "mix_w_out", "cross_w_q", "cross_w_kv", "cross_w_o"):
        shared[k] = f(np.asarray(inputs[k])[:L])
    maps = []
    for b in batches:
        m = dict(shared)
        m["x"] = f(np.asarray(inputs["x"])[b, :s_len])
        m["mem"] = f(np.asarray(inputs["mem"])[b])
        m["positions"] = f(np.asarray(inputs["positions"])[b, :s_len].astype(np.int32))
        maps.append(m)
    return maps


_CACHE = {}


def run(inputs, depth=4, nsb=4, batches=(0, 1, 2, 3), trace=False):
    s_len = nsb * SBT
    key = (depth, nsb, tuple(sorted(STAGES.items())))
    if key not in _CACHE:
        _CACHE[key] = build_program(depth, nsb, s_len)
    nc, stats = _CACHE[key]
    maps = make_in_maps(inputs, depth, s_len, batches)
    res = run_bass_kernel_spmd(nc, maps, core_ids=list(range(len(batches))), trace=trace)
    outs = np.stack([r["out"] for r in res.results], axis=0)
    return outs, res, stats


REAL_CORES = (0, 1, 4, 5)


def run8(inputs, depth=4, nsb=4, trace=False):
    s_len = nsb * SBT
    key = (depth, nsb, tuple(sorted(STAGES.items())))
    if key not in _CACHE:
        _CACHE[key] = build_program(depth, nsb, s_len)
    nc, stats = _CACHE[key]
    real = make_in_maps(inputs, depth, s_len, (0, 1, 2, 3))
    zero = {k: np.zeros_like(v) for k, v in real[0].items()}
    maps = []
    ri = 0
    for c in range(8):
        if c in REAL_CORES:
            maps.append(real[ri]); ri += 1
        else:
            maps.append(zero)
    res = run_bass_kernel_spmd(nc, maps, core_ids=list(range(8)), trace=trace)
    outs = np.stack([res.results[c]["out"] for c in REAL_CORES], axis=0)
    return outs, res, stats


def kernel(**inputs):
    outs, _, _ = run8(inputs, depth=4, nsb=4)
    return outs.astype(np.float32)
```

```python
import numpy as np
import ml_dtypes
from contextlib import ExitStack

import concourse.bass as bass
import concourse.mybir as mybir
from concourse.bass_utils import run_bass_kernel_spmd

F32 = mybir.dt.float32
BF16 = mybir.dt.bfloat16
I32 = mybir.dt.int32
AF = mybir.ActivationFunctionType
ALU = mybir.AluOpType

D = 2048
KC = 16
DFF = 5632
FC = 44
S_FULL = 4096
SBT = 1024
NBLK = SBT // 512
NT = SBT // 128
MEM = 256
NFH = 6
NDH = 6
IN_W = 6150
OFF_FQ, OFF_FK, OFF_FV, OFF_FF = 0, 768, 1536, 2304
OFF_DQ, OFF_DK, OFF_DV = 2310, 3078, 3846
OFF_GB, OFF_GC, OFF_HC = 4614, 5126, 5638
EPS = 1e-6
ROPE_THETA = 500000.0
N_CORES = 4


class Ev:
    __slots__ = ("eng", "sem", "val", "idx")

    def __init__(self, eng, sem=None, val=0, idx=0):
        self.eng, self.sem, self.val, self.idx = eng, sem, val, idx


class Buf:
    __slots__ = ("name", "w", "r", "const")

    def __init__(self, name):
        self.name, self.w, self.r, self.const = name, None, {}, False


class DmaSem:
    __slots__ = ("h", "count", "key")

    def __init__(self, h, key):
        self.h, self.count, self.key = h, 0, key


class Sched:
    ENGS = ("pe", "act", "dve", "pool", "sp")

    def __init__(self, esems):
        self.esem = esems
        self.q = {e: [] for e in self.ENGS}
        self.cnt = {e: 0 for e in self.ENGS}
        self.seen = {e: {} for e in self.ENGS}
        self.pe_idx = 0
        self.pe_sig = []
        self.ndma = 0
        self.dma_active = {}

    def _resolve(self, ev):
        if ev.eng == "pe" and ev.sem is None:
            lo, hi = 0, len(self.pe_sig)
            while lo < hi:
                mid = (lo + hi) // 2
                if self.pe_sig[mid][0] >= ev.idx:
                    hi = mid
                else:
                    lo = mid + 1
            assert lo < len(self.pe_sig), "PE event not yet signalled"
            return "pe", self.esem["pe"], self.pe_sig[lo][1]
        return ev.sem[0], ev.sem[1], ev.val

    def _wait(self, eng, ev):
        key, sem, val = self._resolve(ev)
        if self.seen[eng].get(key, 0) >= val:
            return
        self.seen[eng][key] = val
        self.q[eng].append(("w", sem, val))

    def op(self, eng, fn, reads=(), writes=(), signal=True, dma=None):
        deps = []
        for b in reads:
            if b.w is not None:
                deps.append(b.w)
        for b in writes:
            if b.w is not None:
                deps.append(b.w)
            deps.extend(b.r.values())
        is_dma = dma is not None
        for ev in deps:
            if (not is_dma) and eng == "pe" and ev.eng == "pe":
                continue
            self._wait(eng, ev)
        if is_dma:
            dma.count += 16
            ev = Ev("dma", (dma.key, dma.h), dma.count)
            self.q[eng].append(("i", fn, dma.h, 16))
            self.ndma += 1
            if eng != "pool":
                self.dma_active[dma.key] = ev
            rkey = ("dma", dma.key)
        elif eng == "pe":
            self.pe_idx += 1
            if signal:
                self.cnt["pe"] += 1
                self.pe_sig.append((self.pe_idx, self.cnt["pe"]))
                self.q[eng].append(("i", fn, self.esem["pe"], 1))
            else:
                self.q[eng].append(("i", fn, None, 0))
            ev = Ev("pe", None, 0, self.pe_idx)
            rkey = "pe"
        else:
            self.cnt[eng] += 1
            ev = Ev(eng, (eng, self.esem[eng]), self.cnt[eng])
            self.q[eng].append(("i", fn, self.esem[eng], 1))
            rkey = eng
        for b in reads:
            if not b.const:
                b.r[rkey] = ev
        for b in writes:
            b.w = ev
            b.r = {}
        return ev

    def fence(self, engs=("pe", "act", "dve", "sp")):
        last = {}
        for e in engs:
            if e == "pe":
                if self.pe_sig:
                    last[e] = Ev("pe", ("pe", self.esem["pe"]), self.pe_sig[-1][1])
            elif self.cnt[e] > 0:
                last[e] = Ev(e, (e, self.esem[e]), self.cnt[e])
        for e in engs:
            for o, ev in last.items():
                if o != e:
                    self._wait(e, ev)
            for ev in self.dma_active.values():
                self._wait(e, ev)
        self.dma_active = {}


def build_program(depth, nsb, s_len):
    nc = bass.Bass("TRN2", target_bir_lowering=False)
    L = depth

    def din(name, shape, dt=F32):
        return nc.dram_tensor(name, list(shape), dt, kind="ExternalInput").ap()

    def dscr(name, shape, dt):
        return nc.dram_tensor(name, list(shape), dt, kind="Internal").ap()

    x_d = din("x", [s_len, D])
    mem_d = din("mem", [MEM, D])
    pos_d = din("positions", [s_len], I32)
    w_g1 = din("ffn1_w_gate", [L, D, DFF]); w_u1 = din("ffn1_w_up", [L, D, DFF]); w_d1 = din("ffn1_w_down", [L, DFF, D])
    w_g2 = din("ffn2_w_gate", [L, D, DFF]); w_u2 = din("ffn2_w_up", [L, D, DFF]); w_d2 = din("ffn2_w_down", [L, DFF, D])
    w_in = din("mix_w_in", [L, D, IN_W]); w_out = din("mix_w_out", [L, D, D])
    w_cq = din("cross_w_q", [L, D, 512]); w_ckv = din("cross_w_kv", [L, D, 1024]); w_co = din("cross_w_o", [L, 512, D])
    gains_d = din("gains", [128, 4 * L + 1, KC])
    memg_d = din("mem_gain_b", [128, L, D])
    fbias_d = din("forget_bias_t", [NFH, L])
    convw_d = din("conv_wb", [128, L, 4, 4])
    lam_d = din("lam_b", [128, L, 4, 64])
    subln_d = din("subln_b", [128, L, 128])
    cf32_d = din("const_f32", [128, 3 * 128 + 1 + 2])
    cbf_d = din("const_bf16", [128, 128], BF16)
    sel_d = din("const_sel", [NFH, NFH * 128])
    mask_d = din("const_mask", [128, 4, 512])
    laminit_d = din("const_laminit", [128, L])
    out_d = nc.dram_tensor("out", [s_len, D], F32, kind="ExternalOutput").ap()

    hT_d = dscr("hT", [KC, 128, s_len], F32)
    fk_d = dscr("fk_cache", [L, NFH, 128, s_len], BF16)
    fv_d = dscr("fv_cache", [L, NFH, 128, s_len // 128, 129], BF16)
    fc_d = dscr("fc_cache", [L, 128, s_len // 128, NFH], F32)
    dk_d = dscr("dk_cache", [L, NDH, 128, s_len], BF16)
    dv_d = dscr("dv_cache", [L, NDH, 128, s_len // 128, 129], BF16)
    mk_d = dscr("memk", [L, 128, 4, MEM], BF16)
    mv_d = dscr("memv", [L, 128, 2, 4, 129], BF16)

    es = ExitStack()
    with es:
        def sb_t(name, shape, dt):
            return es.enter_context(nc.sbuf_tensor("s_" + name, list(shape), dt))

        esems = {e: es.enter_context(nc.semaphore("sem_" + e)) for e in Sched.ENGS}
        sc = Sched(esems)
        dma_sem_n = [0]

        def new_dsem():
            dma_sem_n[0] += 1
            h = es.enter_context(nc.semaphore("dsem%d" % dma_sem_n[0]))
            return DmaSem(h, "d%d" % dma_sem_n[0])

        xnT = sb_t("xnT", [128, KC, SBT], BF16)
        R = sb_t("R", [128, 45056], BF16)
        wsl = [sb_t("wsl%d" % i, [128, 11264], BF16) for i in range(2)]
        NWT = 8
        wts = [sb_t("wt%d" % i, [128, 512], F32) for i in range(NWT)]
        gains = sb_t("gains", [128, 4 * L + 1, KC], F32)
        cf32 = sb_t("cf32", [128, 3 * 128 + 3], F32)
        cbf = sb_t("cbf", [128, 128], BF16)
        selt = sb_t("selt", [NFH, NFH * 128], F32)
        maskt = sb_t("maskt", [128, 4, 512], F32)
        fbias = sb_t("fbias", [NFH, L], F32)
        nfbias = sb_t("nfbias", [NFH, L], F32)
        convw = sb_t("convw", [128, L, 4, 4], F32)
        subln = sb_t("subln", [128, L, 128], F32)
        lamt = sb_t("lamt", [128, L], F32)
        nlamt = sb_t("nlamt", [128, L], F32)
        laminit = sb_t("laminit", [128, L], F32)
        ccarry = sb_t("ccarry", [NFH, L], F32)
        halo = sb_t("halo", [128, L, 4, 2], F32)
        small = sb_t("small", [128, 64], F32)

        ident_f = cf32[:, 0:128]
        ones_f = cf32[:, 128:256]
        perm_f = cf32[:, 256:384]
        invf = cf32[:, 384:385]
        negpi = cf32[:, 385:386]
        ident_b = cbf[:, :]

        banks = [es.enter_context(nc.psum_tensor("bank%d" % i, [128, 512], F32)) for i in range(8)]
        bank_buf = [Buf("bank%d" % i) for i in range(8)]

        class Ring:
            def __init__(self, items):
                self.items, self.i = items, 0

            def next(self):
                it = self.items[self.i % len(self.items)]
                self.i += 1
                return it

        mm_ring = Ring([0, 1, 2, 3])
        s_ring = Ring([4, 5])
        rstd_t = [sb_t("rstd%d" % i, [128, 512], F32) for i in range(2)]
        rstd_b = [Buf("rstd%d" % i) for i in range(2)]
        rstd_i = [0]
        x_dsem = [new_dsem() for _ in range(NT)]
        wt_buf = [Buf("wt%d" % i) for i in range(NWT)]
        wt_ring = Ring(list(range(NWT)))
        wt_dsem = [new_dsem() for _ in range(NWT)]
        ws_buf = [[Buf("ws%d_%d" % (i, p)) for p in range(3)] for i in range(2)]
        ws_dsem = [[new_dsem() for p in range(3)] for i in range(2)]
        ws_ring = Ring([0, 1])

        phase = {"cur": [], "inherit": {}}

        def new_phase(keep=()):
            merged = {}
            for b in phase["cur"]:
                evs = ([b.w] if b.w is not None else []) + list(b.r.values())
                for ev in evs:
                    if ev.eng == "pe" and ev.sem is None:
                        k = "pe"
                        if k not in merged or merged[k].idx < ev.idx:
                            merged[k] = ev
                    else:
                        k = ev.sem[0] if ev.eng != "dma" else ("dma", ev.sem[0])
                        if k not in merged or merged[k].val < ev.val:
                            merged[k] = ev
            phase["inherit"] = merged
            phase["cur"] = list(keep)

        def RB(name):
            b = Buf(name)
            b.r = dict(phase["inherit"])
            phase["cur"].append(b)
            return b

        B_xnT = [Buf("xnT%d" % b) for b in range(NBLK)]
        B_const = Buf("const"); B_const.const = True
        B_R = RB("Rmisc")
        misc_dsem = new_dsem()
        hT_buf = [[Buf("hT%d_%d" % (c, b)) for b in range(s_len // 512)] for c in range(KC)]
        st_dsem = [new_dsem() for _ in range(4)]
        st_ring = Ring([0, 1, 2, 3])

        def PE(fn, reads, writes, signal):
            return sc.op("pe", fn, reads, writes, signal=signal)

        def ACT(fn, reads, writes):
            return sc.op("act", fn, reads, writes)

        def DVE(fn, reads, writes):
            return sc.op("dve", fn, reads, writes)

        def POOLC(fn, reads, writes):
            return sc.op("pool", fn, reads, writes)

        def DMA_SP(out, in_, reads, writes, dsem):
            return sc.op("sp", lambda e, o=out, i=in_: e.dma_start(out=o, in_=i), reads, writes, dma=dsem)

        def DMA_POOL(out, in_, reads, writes, dsem):
            return sc.op("pool", lambda e, o=out, i=in_: e.dma_start(out=o, in_=i), reads, writes, dma=dsem)

        def mm_group(bank_i, out_ap, pairs, reads, sig_all=False):
            n = len(pairs)
            for k, pr in enumerate(pairs):
                lt, rh = pr[0], pr[1]
                rds = list(reads) + (list(pr[2]) if len(pr) > 2 else [])
                PE(lambda e, o=out_ap, a=lt, b=rh, s=(k == 0), t=(k == n - 1): e.matmul(o, a, b, start=s, stop=t),
                   rds, [bank_buf[bank_i]], signal=(sig_all or k == n - 1))

        def load_w(parts):
            si = ws_ring.next()
            views = []
            for p, (src, shape) in enumerate(parts):
                n = 1
                for d_ in shape[1:]:
                    n *= d_
                v = wsl[si][:, p * 4096:p * 4096 + n] if len(parts) > 1 else wsl[si][:, 0:n]
                if len(shape) == 3:
                    v = v.rearrange("p (a b) -> p a b", a=shape[1])
                wr = [ws_buf[si][p]] if len(parts) > 1 else ws_buf[si]
                DMA_POOL(v, src, [], wr, ws_dsem[si][p])
                views.append(v)
            rd = ws_buf[si][:len(parts)] if len(parts) > 1 else ws_buf[si]
            return rd, views

        def wview(w_ap, l, c0, ncols):
            return w_ap[l].rearrange("(kc p) n -> p kc n", p=128)[:, :, c0:c0 + ncols]

        def get_wt():
            i = wt_ring.next()
            return i, wts[i], wt_buf[i]

        def load_const(dst, src):
            DMA_SP(dst, src, [], [B_const], misc_dsem)

        B_const.const = False
        load_const(gains[:], gains_d)
        load_const(cf32[:], cf32_d)
        load_const(cbf[:], cbf_d)
        load_const(selt[:], sel_d)
        load_const(maskt[:], mask_d)
        load_const(fbias[:], fbias_d)
        load_const(convw[:], convw_d)
        load_const(subln[:], subln_d)
        load_const(laminit[:], laminit_d)
        lamv = R[:, 0:L * 4 * 64 * 2].bitcast(F32).rearrange("p (l f d) -> p l f d", l=L, f=4)
        DMA_SP(lamv, lam_d, [], [B_R], misc_dsem)
        DVE(lambda e: e.memset(ccarry[:], 0.0), [], [B_const])
        DVE(lambda e: e.memset(halo[:], 0.0), [], [B_const])
        ACT(lambda e: e.mul(nfbias[:], fbias[:], -1.0), [B_const], [B_const])
        for l in range(L):
            s1 = small[:, 0:1]; s2 = small[:, 1:2]; junk = small[:, 8:8 + 0]
            pr = wts[0][:, 0:64]
            DVE(lambda e, l=l: e.tensor_tensor(out=wts[0][:, 0:64], in0=lamv[:, l, 0, :], in1=lamv[:, l, 1, :], op=ALU.mult), [B_R], [wt_buf[0]])
            DVE(lambda e: e.reduce_sum(out=small[:, 0:1], in_=wts[0][:, 0:64], axis=mybir.AxisListType.X), [wt_buf[0]], [B_const])
            DVE(lambda e, l=l: e.tensor_tensor(out=wts[0][:, 64:128], in0=lamv[:, l, 2, :], in1=lamv[:, l, 3, :], op=ALU.mult), [B_R], [wt_buf[0]])
            DVE(lambda e: e.reduce_sum(out=small[:, 1:2], in_=wts[0][:, 64:128], axis=mybir.AxisListType.X), [wt_buf[0]], [B_const])
            ACT(lambda e: e.activation(out=small[:, 2:4], in_=small[:, 0:2], func=AF.Exp), [B_const], [B_const])
            DVE(lambda e: e.tensor_tensor(out=small[:, 4:5], in0=small[:, 2:3], in1=small[:, 3:4], op=ALU.subtract), [B_const], [B_const])
            DVE(lambda e, l=l: e.tensor_tensor(out=lamt[:, l:l + 1], in0=small[:, 4:5], in1=laminit[:, l:l + 1], op=ALU.add), [B_const], [B_const])
            DVE(lambda e, l=l: e.tensor_scalar(out=nlamt[:, l:l + 1], in0=lamt[:, l:l + 1], scalar1=-1.0, scalar2=None, op0=ALU.mult), [B_const], [B_const])
        DVE(lambda e: e.tensor_scalar(out=laminit[:], in0=laminit[:], scalar1=-1.0, scalar2=1.0, op0=ALU.mult, op1=ALU.add), [B_const], [B_const])
        sc.fence(("pe", "act", "dve", "sp", "pool"))
        B_const.const = True

        def rmw_h(bank_i, ps_ap, chunk, gblk, scale):
            i, wt, wb = get_wt()
            hb = hT_buf[chunk][gblk]
            DMA_SP(wt[:], hT_d[chunk, :, gblk * 512:(gblk + 1) * 512], [hb], [wb], wt_dsem[i])
            DVE(lambda e, o=wt, p=ps_ap, s=scale: e.scalar_tensor_tensor(out=o[:], in0=p, scalar=s, in1=o[:], op0=ALU.mult, op1=ALU.add),
                [bank_buf[bank_i], wb], [wb])
            DMA_SP(hT_d[chunk, :, gblk * 512:(gblk + 1) * 512], wt[:], [wb], [hb], wt_dsem[i])

        def stage_init(j):
            new_phase()
            xt = R[:, 0:NT * D * 2].bitcast(F32).rearrange("p (t d) -> p t d", t=NT)
            xb = [RB("xin%d" % t) for t in range(NT)]
            for t in range(NT):
                tok0 = j * SBT + t * 128
                DMA_SP(xt[:, t, :], x_d[tok0:tok0 + 128, :], [], [xb[t]], x_dsem[t])
            for c in range(KC):
                for b in range(NBLK):
                    bi = mm_ring.next()
                    for q in range(4):
                        t = b * 4 + q
                        PE(lambda e, o=banks[bi][:, q * 128:(q + 1) * 128], a=xt[:, t, c * 128:(c + 1) * 128]: e.transpose(o, a, ident_f),
                           [xb[t], B_const], [bank_buf[bi]], signal=(q == 3))
                    i, wt, wb = get_wt()
                    ACT(lambda e, o=wt, p=banks[bi]: e.copy(o[:], p[:]), [bank_buf[bi]], [wb])
                    gb = j * NBLK + b
                    DMA_SP(hT_d[c, :, gb * 512:(gb + 1) * 512], wt[:], [wb], [hT_buf[c][gb]], wt_dsem[i])

        def stage_norm(j, gidx, out_bf=True, fin=None):
            for b in range(NBLK):
                gb = j * NBLK + b
                bi = mm_ring.next()
                for c in range(KC):
                    i, wt, wb = get_wt()
                    DMA_SP(wt[:], hT_d[c, :, gb * 512:(gb + 1) * 512], [hT_buf[c][gb]], [wb], wt_dsem[i])
                    ACT(lambda e, o=wt: e.activation(out=o[:], in_=o[:], func=AF.Square), [wb], [wb])
                    PE(lambda e, o=banks[bi], r=wt, s=(c == 0), t=(c == KC - 1): e.matmul(o[:], ones_f, r[:], start=s, stop=t),
                       [wb, B_const], [bank_buf[bi]], signal=True)
                rt, rb = rstd_t[rstd_i[0] % 2], rstd_b[rstd_i[0] % 2]
                rstd_i[0] += 1
                ACT(lambda e, o=rt, p=banks[bi]: e.activation(out=o[:], in_=p[:], func=AF.Ln, bias=eps_t[:, 0:1], scale=1.0 / D),
                    [bank_buf[bi], B_const], [rb])
                ACT(lambda e, o=rt: e.activation(out=o[:], in_=o[:], func=AF.Exp, scale=-0.5), [rb], [rb])
                for c in range(KC):
                    i, wt, wb = get_wt()
                    DMA_SP(wt[:], hT_d[c, :, gb * 512:(gb + 1) * 512], [hT_buf[c][gb]], [wb], wt_dsem[i])
                    if fin is None:
                        DVE(lambda e, o=xnT[:, c, b * 512:(b + 1) * 512], w=wt, r=rt, g=gains[:, gidx, c:c + 1]:
                            e.scalar_tensor_tensor(out=o, in0=w[:], scalar=g, in1=r[:], op0=ALU.mult, op1=ALU.mult),
                            [wb, rb, B_const], [B_xnT[b]])
                    else:
                        DVE(lambda e, w=wt, r=rt, g=gains[:, gidx, c:c + 1]:
                            e.scalar_tensor_tensor(out=w[:], in0=w[:], scalar=g, in1=r[:], op0=ALU.mult, op1=ALU.mult),
                            [wb, rb, B_const], [wb])
                        fin(c, b, wt, wb)

        def stage_ffn(j, l, wg, wu, wd, gidx):
            new_phase()
            stage_norm(j, gidx)
            gT = R[:, 0:FC * SBT].rearrange("p (c t) -> p c t", c=FC)
            B_gT = [[RB("gT%d_%d" % (c, b)) for b in range(NBLK)] for c in range(FC)]
            for cp in range(FC // 2):
                rd, (vg, vu) = load_w([(wview(wg, l, cp * 256, 256), [128, KC, 256]),
                                       (wview(wu, l, cp * 256, 256), [128, KC, 256])])
                for b in range(NBLK):
                    for cc in range(2):
                        c = cp * 2 + cc
                        bg = mm_ring.next()
                        mm_group(bg, banks[bg][:], [(vg[:, k, cc * 128:(cc + 1) * 128], xnT[:, k, b * 512:(b + 1) * 512]) for k in range(KC)],
                                 [rd[0], B_xnT[b]])
                        bu = mm_ring.next()
                        mm_group(bu, banks[bu][:], [(vu[:, k, cc * 128:(cc + 1) * 128], xnT[:, k, b * 512:(b + 1) * 512]) for k in range(KC)],
                                 [rd[1], B_xnT[b]])
                        i, wt, wb = get_wt()
                        ACT(lambda e, o=wt, p=banks[bg]: e.activation(out=o[:], in_=p[:], func=AF.Silu), [bank_buf[bg]], [wb])
                        DVE(lambda e, o=gT[:, c, b * 512:(b + 1) * 512], w=wt, p=banks[bu]: e.tensor_tensor(out=o, in0=w[:], in1=p[:], op=ALU.mult),
                            [wb, bank_buf[bu]], [B_gT[c][b]])
            for mg in range(D // 256):
                rd, (vd,) = load_w([(wd[l].rearrange("(c p) n -> p c n", p=128)[:, :, mg * 256:(mg + 1) * 256], [128, FC, 256])])
                for mm in range(2):
                    m = mg * 2 + mm
                    for b in range(NBLK):
                        bi = mm_ring.next()
                        mm_group(bi, banks[bi][:], [(vd[:, c, mm * 128:(mm + 1) * 128], gT[:, c, b * 512:(b + 1) * 512], [B_gT[c][b]]) for c in range(FC)],
                                 list(rd))
                        rmw_h(bi, banks[bi][:], m, j * NBLK + b, 0.5)

        def stage_final(j):
            new_phase()
            ot = R[:, 0:NT * D * 2].bitcast(F32).rearrange("p (t d) -> p t d", t=NT)
            ob = [RB("oout%d" % t) for t in range(NT)]

            def fin(c, b, wt, wb):
                bi = mm_ring.next()
                for q in range(4):
                    PE(lambda e, o=banks[bi][:, q * 128:(q + 1) * 128], a=wt[:, q * 128:(q + 1) * 128]: e.transpose(o, a, ident_f),
                       [wb, B_const], [bank_buf[bi]], signal=(q == 3))
                for q in range(4):
                    t = b * 4 + q
                    ACT(lambda e, o=ot[:, t, c * 128:(c + 1) * 128], p=banks[bi][:, q * 128:(q + 1) * 128]: e.copy(o, p),
                        [bank_buf[bi]], [ob[t]])

            stage_norm(j, 4 * L, fin=fin)
            for t in range(NT):
                tok0 = j * SBT + t * 128
                ev = DMA_SP(out_d[tok0:tok0 + 128, :], ot[:, t, :], [ob[t]], [Buf("outd")], x_dsem[t])
                final_evs.append(ev)


        mma_ring = Ring([0, 1, 2, 3])
        acc_sets = Ring([(4, 5), (6, 7)])
        s_ring3 = mma_ring
        deferred = []

        def flush_deferred():
            while deferred:
                deferred.pop(0)()

        B_smallc = {c: Buf("sc%d" % c) for c in range(34, 64)}
        smallcols = Ring(list(range(40, 64)))
        B_small = Buf("smallcols")

        def Rf32(off, n):
            return R[:, off:off + 2 * n].bitcast(F32)

        def attn(j, QT, KT, vtile, nkt_list, scale, Pt, B_P, reads_q, reads_k, reads_v, out_fn,
                 Ctb=None, B_Ctb=None, negc=None, B_negc=None, causal=True):
            stream = []
            accs = {}
            for qb in range(NBLK):
                kts = nkt_list(qb)
                accs[qb] = acc_sets.next()
                for ki, kt in enumerate(kts):
                    stream.append((qb, ki, kt, len(kts)))

            def issue_S(pos):
                qb, ki, kt, n_k = stream[pos]
                q0 = j * SBT + qb * 512
                k0 = kt * 128
                diag = causal and (k0 >= q0)
                bS = s_ring3.next()
                PE(lambda e, o=banks[bS], a=KT[:, k0:k0 + 128], b=QT[:, qb * 512:(qb + 1) * 512]: e.matmul(o[:], a, b, start=True, stop=True),
                   list(reads_q) + list(reads_k(kt)), [bank_buf[bS]], signal=True)
                Pv, Pb = Pt.next()
                if Ctb is not None or diag:
                    i, wt, wb = get_wt()
                    if Ctb is not None:
                        DVE(lambda e, o=wt, p=banks[bS], c=Ctb[:, qb * 512:(qb + 1) * 512], s=scale:
                            e.scalar_tensor_tensor(out=o[:], in0=p[:], scalar=s, in1=c, op0=ALU.mult, op1=ALU.add),
                            [bank_buf[bS], B_Ctb], [wb])
                        if diag:
                            v = (k0 - q0) // 128
                            DVE(lambda e, o=wt, m=maskt[:, v, :]: e.tensor_tensor(out=o[:], in0=o[:], in1=m, op=ALU.add), [wb, B_const], [wb])
                    else:
                        v = (k0 - q0) // 128
                        DVE(lambda e, o=wt, p=banks[bS], m=maskt[:, v, :], s=scale:
                            e.scalar_tensor_tensor(out=o[:], in0=p[:], scalar=s, in1=m, op0=ALU.mult, op1=ALU.add),
                            [bank_buf[bS], B_const], [wb])
                    if negc is not None:
                        ACT(lambda e, o=Pv, w=wt, bcol=negc(kt): e.activation(out=o, in_=w[:], func=AF.Exp, bias=bcol, scale=1.0),
                            [wb, B_negc(kt)], [Pb])
                    else:
                        ACT(lambda e, o=Pv, w=wt: e.activation(out=o, in_=w[:], func=AF.Exp), [wb], [Pb])
                else:
                    ACT(lambda e, o=Pv, p=banks[bS], s=scale: e.activation(out=o, in_=p[:], func=AF.Exp, scale=s), [bank_buf[bS]], [Pb])
                return Pv, Pb

            LA = 2
            n_s = len(stream)
            pend = {}
            for pos in range(min(LA, n_s)):
                pend[pos] = issue_S(pos)
            for pos in range(n_s):
                if pos + LA < n_s:
                    pend[pos + LA] = issue_S(pos + LA)
                Pv, Pb = pend.pop(pos)
                qb, ki, kt, n_k = stream[pos]
                accb = accs[qb]
                if ki == min(2, n_k - 1):
                    flush_deferred()
                for qs in range(4):
                    bk = accb[qs // 2]
                    off = (qs % 2) * 129
                    PE(lambda e, o=banks[bk][:, off:off + 129], a=Pv[:, qs * 128:(qs + 1) * 128], b=vtile(kt), s=(ki == 0 and qs % 2 == 0), t=(ki == n_k - 1):
                       e.matmul(o, a, b, start=s, stop=t, skip_group_check=True),
                       [Pb] + list(reads_v(kt)), [bank_buf[bk]], signal=(ki == n_k - 1))
                if ki == n_k - 1:
                    for qs in range(4):
                        bk = accb[qs // 2]
                        off = (qs % 2) * 129
                        col = smallcols.next()
                        rec = small[:, col:col + 1]
                        DVE(lambda e, o=rec, p=banks[bk][:, off + 128:off + 129]: e.reciprocal(out=o, in_=p), [bank_buf[bk]], [B_smallc[col]])
                        out_fn(qb, qs, banks[bk][:, off:off + 128], rec, bank_buf[bk], B_smallc[col])

        def transpose_to(On, B_On, dst_fn, B_dst, n=4):
            def go():
                bi = mma_ring.next()
                pb = banks[bi].bitcast(BF16)
                for q in range(n):
                    PE(lambda e, o=pb[:, q * 128:(q + 1) * 128], a=On[:, q, :]: e.transpose(o, a, ident_b), [B_On[q], B_const], [bank_buf[bi]], signal=(q == n - 1))
                ACT(lambda e, o=dst_fn(), p=pb[:, 0:n * 128]: e.copy(o, p), [bank_buf[bi]], [B_dst])
            deferred.append(go)

        def proj_fm(vw, rd, b, kcn=KC, rhs=None, B_rhs=None, ncol=512, pslice=None):
            bi = mma_ring.next()
            rh = rhs if rhs is not None else (lambda k: xnT[:, k, b * 512:(b + 1) * 512])
            Br = B_rhs if B_rhs is not None else B_xnT[b]
            out = banks[bi][:, 0:ncol] if pslice is None else banks[bi][pslice[0]:pslice[1], 0:ncol]
            mm_group(bi, out, [(vw(k), rh(k)) for k in range(kcn)], list(rd) + [Br])
            return bi

        def stage_mem(l):
            new_phase()
            mt = Rf32(0, 2 * D).rearrange("p (t d) -> p t d", t=2)
            memn = R[:, 8192:12288].rearrange("p (t d) -> p t d", t=2)
            memnT = R[:, 12288:16384].rearrange("p (c t) -> p c t", c=KC)
            memg = Rf32(16384, D)
            mk_s = R[:, 20480:21504].rearrange("p (h t) -> p h t", h=4)
            mv_s = R[:, 21504:22536].rearrange("p (t h d) -> p t h d", t=2, h=4)
            junk = Rf32(22544, D)
            B_mt = [RB("mt0"), RB("mt1")]; B_memn = [RB("memn0"), RB("memn1")]; B_memnT = RB("memnT"); B_memg = RB("memg")
            B_mk = RB("mk_s"); B_mv = RB("mv_s"); B_junk = RB("junk")
            DMA_SP(memg, memg_d[:, l, :], [], [B_memg], misc_dsem)
            DVE(lambda e: e.memset(mv_s[:, :, :, 128:129], 1.0), [], [B_mv])
            for t in range(2):
                DMA_SP(mt[:, t, :], mem_d[t * 128:(t + 1) * 128, :], [], [B_mt[t]], x_dsem[t])
                ACT(lambda e, t=t: e.activation(out=junk, in_=mt[:, t, :], func=AF.Square, accum_out=small[:, 34 + t:35 + t]), [B_mt[t]], [B_junk, B_small])
                ACT(lambda e, t=t: e.activation(out=small[:, 36 + t:37 + t], in_=small[:, 34 + t:35 + t], func=AF.Ln, bias=eps_t[:, 0:1], scale=1.0 / D), [B_small, B_const], [B_small])
                ACT(lambda e, t=t: e.activation(out=small[:, 36 + t:37 + t], in_=small[:, 36 + t:37 + t], func=AF.Exp, scale=-0.5), [B_small], [B_small])
                DVE(lambda e, t=t: e.scalar_tensor_tensor(out=memn[:, t, :], in0=mt[:, t, :], scalar=small[:, 36 + t:37 + t], in1=memg, op0=ALU.mult, op1=ALU.mult),
                    [B_mt[t], B_small, B_memg], [B_memn[t]])
            for c in range(KC):
                bi = mma_ring.next()
                pb = banks[bi].bitcast(BF16)
                for t in range(2):
                    PE(lambda e, o=pb[:, t * 128:(t + 1) * 128], a=memn[:, t, c * 128:(c + 1) * 128]: e.transpose(o, a, ident_b), [B_memn[t], B_const], [bank_buf[bi]], signal=(t == 1))
                ACT(lambda e, o=memnT[:, c, :], p=pb[:, 0:256]: e.copy(o, p), [bank_buf[bi]], [B_memnT])
            for hd in range(4):
                rd, (vk,) = load_w([(wview(w_ckv, l, hd * 128, 128), [128, KC, 128])])
                bi = mma_ring.next()
                mm_group(bi, banks[bi][:, 0:MEM], [(vk[:, k, :], memnT[:, k, :]) for k in range(KC)], list(rd) + [B_memnT])
                ACT(lambda e, o=mk_s[:, hd, :], p=banks[bi][:, 0:MEM]: e.copy(o, p), [bank_buf[bi]], [B_mk])
            rd, (vv,) = load_w([(wview(w_ckv, l, 512, 512), [128, KC, 512])])
            for t in range(2):
                bi = mma_ring.next()
                mm_group(bi, banks[bi][:], [(memnT[:, k, t * 128:(t + 1) * 128], vv[:, k, :]) for k in range(KC)], list(rd) + [B_memnT])
                ACT(lambda e, o=mv_s[:, t, :, 0:128], p=banks[bi][:].rearrange("p (h d) -> p h d", h=4): e.copy(o, p), [bank_buf[bi]], [B_mv])
            DMA_SP(mk_d[l], mk_s, [B_mk], [B_mkd[l]], misc_dsem)
            DMA_SP(mv_d[l], mv_s, [B_mv], [B_mvd[l]], misc_dsem)

        B_mkd = [Buf("mkd%d" % l) for l in range(L)]
        B_mvd = [Buf("mvd%d" % l) for l in range(L)]

        def stage_cross(j, l):
            new_phase()
            stage_norm(j, 4 * l + 2)
            crossT = R[:, 0:4096].rearrange("p (c t) -> p c t", c=4)
            memK = R[:, 4096:5120].rearrange("p (h t) -> p h t", h=4)
            memV = R[:, 5120:6152].rearrange("p (t h d) -> p t h d", t=2, h=4)
            QT = R[:, 6160:7184]
            Pt = Ring([(R[:, 7184 + i * 512:7184 + (i + 1) * 512], RB("P%d" % i)) for i in range(4)])
            On = R[:, 9232:9744].rearrange("p (q d) -> p q d", q=4)
            B_cT = RB("crossT"); B_mK = RB("memK"); B_mV = RB("memV"); B_QT = RB("QT"); B_On = [RB("On%d" % q) for q in range(4)]
            DMA_SP(memK, mk_d[l], [B_mkd[l]], [B_mK], misc_dsem)
            DMA_SP(memV, mv_d[l], [B_mvd[l]], [B_mV], misc_dsem)
            for hd in range(4):
                rd, (vq,) = load_w([(wview(w_cq, l, hd * 128, 128), [128, KC, 128])])
                for b in range(NBLK):
                    bi = proj_fm(lambda k: vq[:, k, :], rd, b)
                    ACT(lambda e, o=QT[:, b * 512:(b + 1) * 512], p=banks[bi]: e.copy(o, p[:]), [bank_buf[bi]], [B_QT])

                def out_fn(qb, qs, acc, rec, bb, bs, hd=hd):
                    DVE(lambda e, o=On[:, qs, :], a=acc, r=rec: e.tensor_scalar(out=o, in0=a, scalar1=r, scalar2=None, op0=ALU.mult), [bb, bs], [B_On[qs]])
                    if qs == 3:
                        transpose_to(On, B_On, lambda: crossT[:, hd, qb * 512:(qb + 1) * 512], B_cT)

                attn(j, QT, memK[:, hd, :], lambda kt: memV[:, kt, hd, :], lambda qb: [0, 1], 128.0 ** -0.5, Pt, None,
                     [B_QT], lambda kt: [B_mK], lambda kt: [B_mV], out_fn, causal=False)
            flush_deferred()
            for mg in range(D // 256):
                rd, (vo,) = load_w([(w_co[l].rearrange("(c p) n -> p c n", p=128)[:, :, mg * 256:(mg + 1) * 256], [128, 4, 256])])
                for mm in range(2):
                    for b in range(NBLK):
                        bi = mma_ring.next()
                        mm_group(bi, banks[bi][:], [(vo[:, c, mm * 128:(mm + 1) * 128], crossT[:, c, b * 512:(b + 1) * 512]) for c in range(4)], list(rd) + [B_cT])
                        rmw_h(bi, banks[bi][:], mg * 2 + mm, j * NBLK + b, 1.0)

        B_fk = [[Buf("fkc") for _ in range(NFH)] for _ in range(L)]
        B_fv = [[Buf("fvc") for _ in range(NFH)] for _ in range(L)]
        B_dk = [[Buf("dkc") for _ in range(NDH)] for _ in range(L)]
        B_dv = [[Buf("dvc") for _ in range(NDH)] for _ in range(L)]
        B_fc = [Buf("fcc") for _ in range(L)]
        kv_dsem = [new_dsem() for _ in range(4)]

        def stage_mix(j, l):
            new_phase()
            stage_norm(j, 4 * l + 1)
            mixT = R[:, 0:16384].rearrange("p (c t) -> p c t", c=KC)
            cosT = Rf32(16384, SBT); sinT = Rf32(18432, SBT); Ctb = Rf32(20480, SBT)
            QT = R[:, 22528:23552]
            KT = R[:, 23552:27648]
            V1 = R[:, 27648:31776].rearrange("p (t d) -> p t d", d=129)
            negc = Rf32(31776, 32 * NFH).rearrange("p (t h) -> p t h", h=NFH)
            cT = Rf32(32160, SBT); lf = Rf32(34208, SBT); onesr = Rf32(36256, SBT)
            lf_full = lf
            Pt = Ring([(R[:, 38304 + i * 512:38304 + (i + 1) * 512], RB("P%d" % i)) for i in range(4)])
            On = R[:, 40352:40864].rearrange("p (q d) -> p q d", q=4)
            O1n = Rf32(40864, 1024).rearrange("p (q d) -> p q d", q=8)
            tmpA = Rf32(42912, SBT)
            QTz = R[:, 34208:36256].rearrange("p (c t) -> p c t", c=2)
            B_mixT = RB("mixT"); B_cos = RB("cos"); B_sin = RB("sin"); B_Ctb = RB("Ctb"); B_QT = RB("QT")
            B_KTo = RB("KTown"); B_KTp = RB("KTprev"); B_Vo = RB("Vown"); B_Vp = RB("Vprev")
            B_nco = RB("negc_own"); B_ncp = RB("negc_prev"); B_cT = RB("cT"); B_lf = RB("lf"); B_ones = RB("onesr")
            B_On = [RB("On%d" % q) for q in range(4)]; B_O1 = [RB("O1n%d" % q) for q in range(8)]; B_tmpA = RB("tmpA")
            tok0 = j * SBT
            npt = j * NT
            nkt = lambda qb: list(range((tok0 + (qb + 1) * 512) // 128))
            rk = lambda kt: [B_KTp] if kt < npt else [B_KTo]
            rv = lambda kt: [B_Vp] if kt < npt else [B_Vo]

            posi = tmpA.bitcast(I32)
            DMA_SP(posi, pos_d[tok0:tok0 + SBT].partition_broadcast(128), [], [B_tmpA], misc_dsem)
            TWO_PI = float(2.0 * np.pi)
            PI = float(np.pi)
            posf = Ctb
            DVE(lambda e: e.tensor_copy(out=posf, in_=posi), [B_tmpA], [B_Ctb])

            def trig_table(dst, B_dst, shift):
                kf = tmpA
                ki = tmpA.bitcast(I32)
                DVE(lambda e: e.tensor_scalar(out=dst, in0=posf, scalar1=invf, scalar2=float(shift), op0=ALU.mult, op1=ALU.add), [B_Ctb, B_const], [B_dst])
                DVE(lambda e: e.tensor_scalar(out=ki, in0=dst, scalar1=float(1.0 / TWO_PI), scalar2=None, op0=ALU.mult), [B_dst], [B_tmpA])
                DVE(lambda e: e.tensor_copy(out=lf_full, in_=ki), [B_tmpA], [B_lf])
                DVE(lambda e: e.scalar_tensor_tensor(out=dst, in0=lf_full, scalar=-TWO_PI, in1=dst, op0=ALU.mult, op1=ALU.add), [B_lf, B_dst], [B_dst])
                DVE(lambda e: e.tensor_scalar(out=kf, in0=dst, scalar1=PI, scalar2=-TWO_PI, op0=ALU.is_gt, op1=ALU.mult), [B_dst], [B_tmpA])
                DVE(lambda e: e.tensor_tensor(out=dst, in0=dst, in1=kf, op=ALU.add), [B_dst, B_tmpA], [B_dst])
                DVE(lambda e: e.tensor_scalar(out=kf, in0=dst, scalar1=-PI, scalar2=TWO_PI, op0=ALU.is_lt, op1=ALU.mult), [B_dst], [B_tmpA])
                DVE(lambda e: e.tensor_tensor(out=dst, in0=dst, in1=kf, op=ALU.add), [B_dst, B_tmpA], [B_dst])
                DVE(lambda e: e.tensor_scalar(out=dst, in0=dst, scalar1=PI, scalar2=-PI, op0=ALU.min, op1=ALU.max), [B_dst], [B_dst])
                ACT(lambda e: e.activation(out=dst, in_=dst, func=AF.Sin), [B_dst], [B_dst])

            trig_table(sinT, B_sin, 0.0)
            trig_table(cosT, B_cos, PI / 2)
            DVE(lambda e: e.memset(V1[:, :, 128:129], 1.0), [], [B_Vo, B_Vp])
            DVE(lambda e: e.memset(onesr[0:NFH, :], 1.0), [], [B_ones])

            rd, (vf,) = load_w([(wview(w_in, l, OFF_FF, NFH), [128, KC, NFH])])
            for b in range(NBLK):
                bi = proj_fm(lambda k: vf[:, k, :], rd, b, pslice=(0, NFH))
                ACT(lambda e, o=lf[0:NFH, b * 512:(b + 1) * 512], p=banks[bi][0:NFH, :]: e.activation(out=o, in_=p, func=AF.Exp, bias=nfbias[:, l:l + 1], scale=-1.0),
                    [bank_buf[bi], B_const], [B_lf])
            ACT(lambda e: e.activation(out=lf[0:NFH, :], in_=lf[0:NFH, :], func=AF.Ln, bias=one_t[0:NFH, 0:1], scale=1.0), [B_lf, B_const], [B_lf])
            DVE(lambda e: e.tensor_tensor_scan(out=cT[0:NFH, :], data0=onesr[0:NFH, :], data1=lf[0:NFH, :], initial=ccarry[:, l:l + 1], op0=ALU.mult, op1=ALU.subtract),
                [B_ones, B_lf, B_carry], [B_cT])
            DVE(lambda e: e.tensor_copy(out=ccarry[:, l:l + 1], in_=cT[0:NFH, SBT - 1:SBT]), [B_cT], [B_carry])
            bi = mma_ring.next()
            for t in range(NT):
                PE(lambda e, o=banks[bi][:, t * NFH:(t + 1) * NFH], a=cT[0:NFH, t * 128:(t + 1) * 128]: e.transpose(o, a, ident_f[0:NFH, 0:NFH]),
                   [B_cT, B_const], [bank_buf[bi]], signal=(t == NT - 1))
            ACT(lambda e, o=negc[:, npt:npt + NT, :], p=banks[bi][:, 0:NT * NFH].rearrange("p (t h) -> p t h", h=NFH): e.mul(o, p, -1.0), [bank_buf[bi]], [B_nco])
            if j < nsb - 1:
                DMA_SP(fc_d[l, :, npt:npt + NT, :], negc[:, npt:npt + NT, :], [B_nco], [B_fc[l]], kv_dsem[3])
            if j > 0:
                DMA_SP(negc[:, 0:npt, :], fc_d[l, :, 0:npt, :], [B_fc[l]], [B_ncp], kv_dsem[3])

            def kv_proj(vk, vv, rd, kc_d, vc_d, B_kc, B_vc, rope):
                cps = []
                for b in range(NBLK):
                    bi = proj_fm(lambda k: vk[:, k, :], [rd[1]], b)
                    dst = KT[:, tok0 + b * 512:tok0 + (b + 1) * 512]
                    if rope:
                        cps.append(rope_copy(bi) + (dst, B_KTo, b))
                    else:
                        ACT(lambda e, o=dst, p=banks[bi]: e.copy(o, p[:]), [bank_buf[bi]], [B_KTo])
                for c_ in cps:
                    rope_rot(*c_)
                for tq in range(NT // 4):
                    bi = mma_ring.next()
                    for t4 in range(4):
                        t = tq * 4 + t4
                        mm_group(bi, banks[bi][:, t4 * 128:(t4 + 1) * 128], [(xnT[:, k, t * 128:(t + 1) * 128], vv[:, k, :]) for k in range(KC)],
                                 [rd[2], B_xnT[t // 4]])
                    ACT(lambda e, o=V1[:, npt + tq * 4:npt + tq * 4 + 4, 0:128], p=banks[bi][:].rearrange("p (t d) -> p t d", t=4): e.copy(o, p), [bank_buf[bi]], [B_Vo])
                if j < nsb - 1:
                    DMA_SP(kc_d[:, tok0:tok0 + SBT], KT[:, tok0:tok0 + SBT], [B_KTo], [B_kc], kv_dsem[0])
                    DMA_SP(vc_d[:, npt:npt + NT, :], V1[:, npt:npt + NT, :], [B_Vo], [B_vc], kv_dsem[1])
                if j > 0:
                    DMA_SP(KT[:, 0:tok0], kc_d[:, 0:tok0], [B_kc], [B_KTp], kv_dsem[0])
                    DMA_SP(V1[:, 0:npt, :], vc_d[:, 0:npt, :], [B_vc], [B_Vp], kv_dsem[1])

            def rope_copy(bi):
                i, wt, wb = get_wt()
                ACT(lambda e, o=wt, p=banks[bi]: e.copy(o[:], p[:]), [bank_buf[bi]], [wb])
                return wt, wb

            def rope_rot(wt, wb, dst, B_dst, b):
                b2 = mma_ring.next()
                PE(lambda e, o=banks[b2], w=wt: e.matmul(o[:], perm_f, w[:], start=True, stop=True), [wb, B_const], [bank_buf[b2]], signal=True)
                i2, wt2, wb2 = get_wt()
                DVE(lambda e, o=wt2, p=banks[b2], s=sinT[:, b * 512:(b + 1) * 512]: e.tensor_tensor(out=o[:], in0=p[:], in1=s, op=ALU.mult), [bank_buf[b2], B_sin], [wb2])
                DVE(lambda e, o=wt, c=cosT[:, b * 512:(b + 1) * 512]: e.tensor_tensor(out=o[:], in0=o[:], in1=c, op=ALU.mult), [wb, B_cos], [wb])
                if dst is None:
                    for comp in range(2):
                        p0, p1 = comp * 64, comp * 64 + 64
                        DVE(lambda e, o=QTz[p0:p1, comp, b * 512:(b + 1) * 512], a=wt, c=wt2, p0=p0, p1=p1: e.tensor_tensor(out=o, in0=a[p0:p1, :], in1=c[p0:p1, :], op=ALU.add),
                            [wb, wb2], [B_dst])
                else:
                    DVE(lambda e, o=dst, a=wt, c=wt2: e.tensor_tensor(out=o, in0=a[:], in1=c[:], op=ALU.add), [wb, wb2], [B_dst])

            for hd in range(NFH):
                rd, (vq, vk, vv) = load_w([(wview(w_in, l, OFF_FQ + hd * 128, 128), [128, KC, 128]),
                                           (wview(w_in, l, OFF_FK + hd * 128, 128), [128, KC, 128]),
                                           (wview(w_in, l, OFF_FV + hd * 128, 128), [128, KC, 128])])
                for b in range(NBLK):
                    bi = proj_fm(lambda k: vq[:, k, :], [rd[0]], b)
                    ACT(lambda e, o=QT[:, b * 512:(b + 1) * 512], p=banks[bi]: e.copy(o, p[:]), [bank_buf[bi]], [B_QT])
                kv_proj(vk, vv, rd, fk_d[l, hd], fv_d[l, hd], B_fk[l][hd], B_fv[l][hd], rope=False)
                for b in range(NBLK):
                    bi = mma_ring.next()
                    PE(lambda e, o=banks[bi], a=selt[0:NFH, hd * 128:(hd + 1) * 128], r=cT[0:NFH, b * 512:(b + 1) * 512]: e.matmul(o[:], a, r, start=True, stop=True),
                       [B_cT, B_const], [bank_buf[bi]], signal=True)
                    ACT(lambda e, o=Ctb[:, b * 512:(b + 1) * 512], p=banks[bi]: e.copy(o, p[:]), [bank_buf[bi]], [B_Ctb])

                def out_fn(qb, qs, acc, rec, bb, bs, hd=hd):
                    DVE(lambda e, o=On[:, qs, :], a=acc, r=rec: e.tensor_scalar(out=o, in0=a, scalar1=r, scalar2=None, op0=ALU.mult), [bb, bs], [B_On[qs]])
                    if qs == 3:
                        transpose_to(On, B_On, lambda: mixT[:, hd, qb * 512:(qb + 1) * 512], B_mixT)

                attn(j, QT, KT, lambda kt: V1[:, kt, :], nkt, 128.0 ** -0.5, Pt, None, [B_QT], rk, rv, out_fn,
                     Ctb=Ctb, B_Ctb=B_Ctb, negc=lambda kt, hd=hd: negc[:, kt, hd:hd + 1], B_negc=lambda kt: (B_ncp if kt < npt else B_nco))

            DVE(lambda e: e.memset(QTz[64:128, 0, :], 0.0), [], [B_lf])
            DVE(lambda e: e.memset(QTz[0:64, 1, :], 0.0), [], [B_lf])
            for hd in range(NDH):
                rd, (vq, vk, vv) = load_w([(wview(w_in, l, OFF_DQ + hd * 128, 128), [128, KC, 128]),
                                           (wview(w_in, l, OFF_DK + hd * 128, 128), [128, KC, 128]),
                                           (wview(w_in, l, OFF_DV + hd * 128, 128), [128, KC, 128])])
                cps = []
                for b in range(NBLK):
                    bi = proj_fm(lambda k: vq[:, k, :], [rd[0]], b)
                    cps.append(rope_copy(bi) + (None, B_lf, b))
                for c_ in cps:
                    rope_rot(*c_)
                kv_proj(vk, vv, rd, dk_d[l, hd], dv_d[l, hd], B_dk[l][hd], B_dv[l][hd], rope=True)
                for comp in range(2):
                    p0, p1 = comp * 64, comp * 64 + 64

                    def out_fn(qb, qs, acc, rec, bb, bs, hd=hd, comp=comp):
                        if comp == 0:
                            DVE(lambda e, o=O1n[:, qb * 4 + qs, :], a=acc, r=rec: e.tensor_scalar(out=o, in0=a, scalar1=r, scalar2=None, op0=ALU.mult), [bb, bs], [B_O1[qb * 4 + qs]])
                            return
                        i, wt, wb = get_wt()
                        a_ap = wt[:, 0:128]
                        DVE(lambda e, o=a_ap, a=acc, r=rec: e.tensor_scalar(out=o, in0=a, scalar1=r, scalar2=None, op0=ALU.mult), [bb, bs], [wb])
                        DVE(lambda e, o=a_ap, o1=O1n[:, qb * 4 + qs, :]: e.scalar_tensor_tensor(out=o, in0=o, scalar=nlamt[:, l:l + 1], in1=o1, op0=ALU.mult, op1=ALU.add),
                            [wb, B_O1[qb * 4 + qs], B_const], [wb])
                        c1 = smallcols.next(); c2 = smallcols.next()
                        ACT(lambda e, a=a_ap, j_=wt[:, 128:256], s=small[:, c1:c1 + 1]: e.activation(out=j_, in_=a, func=AF.Square, accum_out=s), [wb], [wb, B_smallc[c1]])
                        ACT(lambda e, s=small[:, c1:c1 + 1], r=small[:, c2:c2 + 1]: e.activation(out=r, in_=s, func=AF.Ln, bias=eps_t[:, 0:1], scale=1.0 / 128), [B_smallc[c1], B_const], [B_smallc[c2]])
                        ACT(lambda e, r=small[:, c2:c2 + 1]: e.activation(out=r, in_=r, func=AF.Exp, scale=-0.5), [B_smallc[c2]], [B_smallc[c2]])
                        DVE(lambda e, o=On[:, qs, :], a=a_ap, r=small[:, c2:c2 + 1]: e.scalar_tensor_tensor(out=o, in0=a, scalar=r, in1=subln[:, l, :], op0=ALU.mult, op1=ALU.mult),
                            [wb, B_smallc[c2], B_const], [B_On[qs]])
                        if qs == 3:
                            transpose_to(On, B_On, lambda: mixT[:, NFH + hd, qb * 512:(qb + 1) * 512], B_mixT)

                    attn(j, QTz[:, comp, :], KT, lambda kt: V1[:, kt, :], nkt, 64.0 ** -0.5, Pt, None, [B_lf], rk, rv, out_fn)

            flush_deferred()
            new_phase(keep=[B_mixT])
            u = Rf32(22528, SBT + 2); gbS = Rf32(24580, SBT); tconv = Rf32(26628, SBT)
            B_u = RB("u"); B_gbS = RB("gbS"); B_tc = RB("tconv")
            for ch in range(4):
                rd, (vgb, vgc, vhc) = load_w([(wview(w_in, l, OFF_GB + ch * 128, 128), [128, KC, 128]),
                                              (wview(w_in, l, OFF_GC + ch * 128, 128), [128, KC, 128]),
                                              (wview(w_in, l, OFF_HC + ch * 128, 128), [128, KC, 128])])
                ACT(lambda e, ch=ch: e.copy(u[:, 0:2], halo[:, l, ch, :]), [B_halo], [B_u])
                for b in range(NBLK):
                    bi = proj_fm(lambda k: vgb[:, k, :], [rd[0]], b)
                    ACT(lambda e, o=gbS[:, b * 512:(b + 1) * 512], p=banks[bi]: e.copy(o, p[:]), [bank_buf[bi]], [B_gbS])
                    bi = proj_fm(lambda k: vgc[:, k, :], [rd[1]], b)
                    i, wt, wb = get_wt()
                    ACT(lambda e, o=wt, p=banks[bi]: e.copy(o[:], p[:]), [bank_buf[bi]], [wb])
                    bi = proj_fm(lambda k: vhc[:, k, :], [rd[2]], b)
                    DVE(lambda e, o=u[:, 2 + b * 512:2 + (b + 1) * 512], w=wt, p=banks[bi]: e.tensor_tensor(out=o, in0=w[:], in1=p[:], op=ALU.mult), [wb, bank_buf[bi]], [B_u])
                cw = convw[:, l, ch, :]
                DVE(lambda e, cw=cw: e.tensor_scalar(out=tconv, in0=u[:, 2:2 + SBT], scalar1=cw[:, 2:3], scalar2=cw[:, 3:4], op0=ALU.mult, op1=ALU.add), [B_u, B_const], [B_tc])
                DVE(lambda e, cw=cw: e.scalar_tensor_tensor(out=tconv, in0=u[:, 1:1 + SBT], scalar=cw[:, 1:2], in1=tconv, op0=ALU.mult, op1=ALU.add), [B_u, B_tc, B_const], [B_tc])
                DVE(lambda e, cw=cw: e.scalar_tensor_tensor(out=tconv, in0=u[:, 0:SBT], scalar=cw[:, 0:1], in1=tconv, op0=ALU.mult, op1=ALU.add), [B_u, B_tc, B_const], [B_tc])
                DVE(lambda e, ch=ch: e.tensor_tensor(out=mixT[:, 2 * NFH + ch, :], in0=tconv, in1=gbS, op=ALU.mult), [B_tc, B_gbS], [B_mixT])
                ACT(lambda e, ch=ch: e.copy(halo[:, l, ch, :], u[:, SBT:SBT + 2]), [B_u], [B_halo])

            for mg in range(D // 256):
                rd, (vo,) = load_w([(wview(w_out, l, mg * 256, 256), [128, KC, 256])])
                for mm in range(2):
                    for b in range(NBLK):
                        bi = mma_ring.next()
                        mm_group(bi, banks[bi][:], [(vo[:, c, mm * 128:(mm + 1) * 128], mixT[:, c, b * 512:(b + 1) * 512]) for c in range(KC)], list(rd) + [B_mixT])
                        rmw_h(bi, banks[bi][:], mg * 2 + mm, j * NBLK + b, 1.0)

        B_carry = Buf("ccarry")
        B_halo = Buf("halo")
        one_t = small[:, 33:34]
        final_evs = []
        eps_t = small[:, 32:33]
        B_const.const = False
        DVE(lambda e: e.memset(small[:, 32:33], EPS), [], [B_const])
        DVE(lambda e: e.memset(small[:, 33:34], 1.0), [], [B_const])
        for l in range(L):
            DVE(lambda e, l=l: e.tensor_scalar(out=subln[:, l, :], in0=subln[:, l, :], scalar1=laminit[:, l:l + 1], scalar2=None, op0=ALU.mult), [B_const], [B_const])
        sc.fence(("pe", "act", "dve", "sp", "pool"))
        B_const.const = True

        if STAGES["cross"]:
            for l in range(L):
                stage_mem(l)
        for j in range(nsb):
            stage_init(j)
            for l in range(L):
                if STAGES["ffn1"]:
                    stage_ffn(j, l, w_g1, w_u1, w_d1, 4 * l + 0)
                if STAGES["mix"]:
                    stage_mix(j, l)
                if STAGES["cross"]:
                    stage_cross(j, l)
                if STAGES["ffn2"]:
                    stage_ffn(j, l, w_g2, w_u2, w_d2, 4 * l + 3)
            stage_final(j)

        for ev in final_evs:
            sc._wait("sp", ev)
        sc.fence(("pe", "act", "dve", "sp", "pool"))

        engmap = {"pe": "tensor", "act": "scalar", "dve": "vector", "pool": "gpsimd", "sp": "sync"}
        with nc.Block() as block:
            for en in Sched.ENGS:
                prog = sc.q[en]

                def body(e, prog=prog):
                    for it in prog:
                        if it[0] == "w":
                            e.wait_ge(it[1], it[2])
                        else:
                            ins = it[1](e)
                            if it[2] is not None:
                                ins.then_inc(it[2], it[3])

                getattr(block, engmap[en])(body)
        stats = {e: len(sc.q[e]) for e in Sched.ENGS}
    return nc, stats


STAGES = {"ffn1": True, "mix": True, "cross": True, "ffn2": True}


def host_consts(depth):
    L = depth
    cf = np.zeros((128, 3 * 128 + 3), np.float32)
    cf[:, 0:128] = np.eye(128, dtype=np.float32)
    cf[:, 128:256] = 1.0
    perm = np.zeros((128, 128), np.float32)
    invf = np.zeros((128,), np.float32)
    inv_freq = ROPE_THETA ** (-np.arange(0, 16, 2, dtype=np.float32) / 16.0)
    for comp in range(2):
        base = comp * 64
        for i in range(8):
            m1, m2 = base + i, base + 8 + i
            perm[m2, m1] = 1.0
            perm[m1, m2] = 1.0
            invf[m1] = -inv_freq[i]
            invf[m2] = inv_freq[i]
    cf[:, 256:384] = perm
    cf[:, 384] = invf
    cf[:, 385] = -np.pi
    cbf = np.eye(128, dtype=np.float32).astype(ml_dtypes.bfloat16)
    sel = np.zeros((NFH, NFH * 128), np.float32)
    for h in range(NFH):
        sel[h, h * 128:(h + 1) * 128] = 1.0
    mask = np.zeros((128, 4, 512), np.float32)
    p = np.arange(128)[:, None]
    jj = np.arange(512)[None, :]
    for v in range(4):
        mask[:, v, :] = np.where(jj - p >= 128 * v, 0.0, -1e30)
    laminit = np.zeros((128, L), np.float32)
    for l in range(L):
        laminit[:, l] = 0.8 - 0.6 * np.exp(-0.3 * l)
    return cf, cbf, sel, mask, laminit


def make_in_maps(inputs, depth, s_len, batches):
    L = depth
    f = lambda a: np.ascontiguousarray(np.asarray(a))
    cf, cbf, sel, mask, laminit = host_consts(L)
    g = np.zeros((4 * L + 1, D), np.float32)
    for l in range(L):
        g[4 * l + 0] = inputs["ffn1_norm"][l]
        g[4 * l + 1] = inputs["mix_norm"][l]
        g[4 * l + 2] = inputs["cross_norm"][l]
        g[4 * l + 3] = inputs["ffn2_norm"][l]
    g[4 * L] = inputs["final_norm"]
    gains = f(g.reshape(4 * L + 1, KC, 128).transpose(2, 0, 1))
    memg = f(np.broadcast_to(np.asarray(inputs["mem_norm"])[None, :L, :], (128, L, D)))
    fb = f(np.asarray(inputs["forget_bias"])[:L].T)
    cw = np.zeros((128, L, 4, 4), np.float32)
    for l in range(L):
        for ch in range(4):
            cw[:, l, ch, 0:3] = np.asarray(inputs["conv_w"])[l, :, ch * 128:(ch + 1) * 128].T
            cw[:, l, ch, 3] = np.asarray(inputs["conv_b"])[l, ch * 128:(ch + 1) * 128]
    lam = np.stack([np.asarray(inputs[k])[:L] for k in ("lambda_q1", "lambda_k1", "lambda_q2", "lambda_k2")], axis=1)
    lam = f(np.broadcast_to(lam[None], (128, L, 4, 64)))
    subln = f(np.broadcast_to(np.asarray(inputs["diff_subln"])[None, :L, :], (128, L, 128)))
    shared = {
        "gains": gains, "mem_gain_b": memg, "forget_bias_t": fb, "conv_wb": cw, "lam_b": lam, "subln_b": subln,
        "const_f32": cf, "const_bf16": cbf, "const_sel": sel, "const_mask": mask, "const_laminit": laminit,
    }
    for k in ("ffn1_w_gate", "ffn1_w_up", "ffn1_w_down", "ffn2_w_gate", "ffn2_w_up", "ffn2_w_down",
              "mix_w_in", "mix_w_out", "cross_w_q", "cross_w_kv", "cross_w_o"):
        shared[k] = f(np.asarray(inputs[k])[:L])
    maps = []
    for b in batches:
        m = dict(shared)
        m["x"] = f(np.asarray(inputs["x"])[b, :s_len])
        m["mem"] = f(np.asarray(inputs["mem"])[b])
        m["positions"] = f(np.asarray(inputs["positions"])[b, :s_len].astype(np.int32))
        maps.append(m)
    return maps


_CACHE = {}


def run(inputs, depth=4, nsb=4, batches=(0, 1, 2, 3), trace=False):
    s_len = nsb * SBT
    key = (depth, nsb, tuple(sorted(STAGES.items())))
    if key not in _CACHE:
        _CACHE[key] = build_program(depth, nsb, s_len)
    nc, stats = _CACHE[key]
    maps = make_in_maps(inputs, depth, s_len, batches)
    res = run_bass_kernel_spmd(nc, maps, core_ids=list(range(len(batches))), trace=trace)
    outs = np.stack([r["out"] for r in res.results], axis=0)
    return outs, res, stats


REAL_CORES = (0, 1, 4, 5)


def run8(inputs, depth=4, nsb=4, trace=False):
    s_len = nsb * SBT
    key = (depth, nsb, tuple(sorted(STAGES.items())))
    if key not in _CACHE:
        _CACHE[key] = build_program(depth, nsb, s_len)
    nc, stats = _CACHE[key]
    real = make_in_maps(inputs, depth, s_len, (0, 1, 2, 3))
    zero = {k: np.zeros_like(v) for k, v in real[0].items()}
    maps = []
    ri = 0
    for c in range(8):
        if c in REAL_CORES:
            maps.append(real[ri]); ri += 1
        else:
            maps.append(zero)
    res = run_bass_kernel_spmd(nc, maps, core_ids=list(range(8)), trace=trace)
    outs = np.stack([res.results[c]["out"] for c in REAL_CORES], axis=0)
    return outs, res, stats


def kernel(**inputs):
    outs, _, _ = run8(inputs, depth=4, nsb=4)
    return outs.astype(np.float32)
```
